# Optimizing a Trainium2 kernel written in Bass

```python
import jax
import jax.numpy as jnp
from jax import lax
import numpy as np

D_MODEL = 1024
BATCH = 32
SEQ = 256
DEPTH = 2
DEC_BATCH = 2
DEC_SEQ = 1024
PAST_LEN = 512

GRID_W = 64
CHUNK = 64
Q_BLOCK = 128
EPS = 1e-6

A_HEADS = 4
A_DK = 64
A_DV = 128
A_GATE_RANK = 16
A_GATE_TEMP = 16.0
B_HEADS = 4
B_DK = 64
B_DV = 128
C_HEADS = 8
C_Q_RANK = 384
C_KV_RANK = 256
C_NOPE = 128
C_ROPE = 64
C_DV = 128
ROPE_THETA = 10000.0
D_FF = 2816
CONV_W = 3
N_MOD = 6

L0_SIZES = (A_HEADS * A_DK, A_HEADS * A_DK, A_HEADS * A_DV, A_HEADS * A_DV, 2 * A_GATE_RANK,
            B_HEADS * B_DK, B_HEADS * B_DK, B_HEADS * B_DV, B_HEADS * B_DV, 4 * B_HEADS)
L0_IN = sum(L0_SIZES)
L0_MIX = A_HEADS * A_DV + B_HEADS * B_DV
L1_SIZES = (C_Q_RANK, C_KV_RANK, C_ROPE)
L1_IN = sum(L1_SIZES)

kernel_name = 'hybrid_gla_mlstm_mla_flow_step'


def split_cols(t, sizes):
    idx = np.cumsum(sizes)[:-1].tolist()
    return jnp.split(t, idx, axis=-1)


def rmsnorm(x, g):
    xf = x.astype(jnp.float32)
    y = xf * lax.rsqrt(jnp.mean(jnp.square(xf), axis=-1, keepdims=True) + EPS)
    return (y * g.astype(jnp.float32)).astype(x.dtype)


def modulation(cond, w_mod, b_mod):
    m = (jax.nn.silu(cond) @ w_mod + b_mod)[:, None, :]
    return jnp.split(m, N_MOD, axis=-1)


def to_chunks(t):
    b, n = t.shape[0], t.shape[1]
    t = t.reshape((b, n // CHUNK, CHUNK) + t.shape[2:])
    perm = (1, 0, 3, 2) + tuple(range(4, t.ndim))
    return t.transpose(perm)


def from_chunks(t):
    nc, b, h, l, d = t.shape
    return t.transpose(1, 0, 3, 2, 4).reshape(b, nc * l, h, d)


def gla_chunked(q, k, v, log_a, s0):
    qc, kc, vc, ac = (to_chunks(t) for t in (q, k, v, log_a))
    causal = jnp.tril(jnp.ones((CHUNK, CHUNK), dtype=bool))

    def step(s, inp):
        qi, ki, vi, ai = inp
        b = jnp.cumsum(ai, axis=2)
        inter = jnp.einsum('bhtk,bhkv->bhtv', qi * jnp.exp(b), s)
        rel = b[:, :, :, None, :] - b[:, :, None, :, :]
        decay = jnp.exp(jnp.where(causal[:, :, None], rel, -jnp.inf))
        scores = jnp.einsum('bhtk,bhsk,bhtsk->bhts', qi, ki, decay)
        out = inter + jnp.einsum('bhts,bhsv->bhtv', scores, vi)
        b_end = b[:, :, -1, :]
        s_new = (jnp.exp(b_end)[..., None] * s
                 + jnp.einsum('bhsk,bhsv->bhkv', ki * jnp.exp(b_end[:, :, None, :] - b), vi))
        return s_new, out

    s_fin, out = lax.scan(step, s0, (qc, kc, vc, ac))
    return from_chunks(out), s_fin


def mlstm_chunked(q, k, v, log_i, log_f, state):
    qc, kc, vc, ic, fc = (to_chunks(t) for t in (q, k, v, log_i, log_f))
    causal = jnp.tril(jnp.ones((CHUNK, CHUNK), dtype=bool))

    def step(carry, inp):
        c_mat, n_vec, m = carry
        qi, ki, vi, ii, fi = inp
        b = jnp.cumsum(fi, axis=-1)
        log_inter = b + m[..., None]
        log_intra = jnp.where(causal, b[..., :, None] - b[..., None, :] + ii[..., None, :], -jnp.inf)
        m_out = jnp.maximum(log_inter, jnp.max(log_intra, axis=-1))
        w_inter = jnp.exp(log_inter - m_out)
        w_intra = jnp.exp(log_intra - m_out[..., None])
        qk = jnp.einsum('bhtk,bhsk->bhts', qi, ki) * w_intra
        num = (w_inter[..., None] * jnp.einsum('bhtk,bhkv->bhtv', qi, c_mat)
               + jnp.einsum('bhts,bhsv->bhtv', qk, vi))
        den = w_inter * jnp.einsum('bhtk,bhk->bht', qi, n_vec) + jnp.sum(qk, axis=-1)
        h = num / jnp.maximum(jnp.abs(den), jnp.exp(-m_out))[..., None]
        b_end = b[..., -1]
        log_src = b_end[..., None] - b + ii
        m_new = jnp.maximum(b_end + m, jnp.max(log_src, axis=-1))
        carry_w = jnp.exp(b_end + m - m_new)
        src_w = jnp.exp(log_src - m_new[..., None])
        c_new = carry_w[..., None, None] * c_mat + jnp.einsum('bhs,bhsk,bhsv->bhkv', src_w, ki, vi)
        n_new = carry_w[..., None] * n_vec + jnp.einsum('bhs,bhsk->bhk', src_w, ki)
        return (c_new, n_new, m_new), h

    (c_f, n_f, m_f), h = lax.scan(step, state, (qc, kc, vc, ic, fc))
    return from_chunks(h), (c_f, n_f, m_f)


def recurrent_mixer(h, w_in, gla_w_gate_f, gla_b_gate_f, gla_w_gate_b, gla_b_gate_b, gla_g_norm,
                    mlstm_b_gates, mlstm_g_norm, w_out, init):
    bsz, n, _ = h.shape
    f32 = jnp.float32
    qa, ka, va, ga, lra, qb, kb, vb, ob, gates = split_cols(h @ w_in, L0_SIZES)

    def heads(t, d):
        return t.reshape(bsz, n, -1, d).astype(f32)

    def flip(t):
        return jnp.flip(t, axis=1)

    if init is None:
        za = jnp.zeros((bsz, A_HEADS, A_DK, A_DV), f32)
        zc = jnp.zeros((bsz, B_HEADS, B_DK, B_DV), f32)
        zn = jnp.zeros((bsz, B_HEADS, B_DK), f32)
        zm = jnp.zeros((bsz, B_HEADS), f32)
        init = (za, za, zc, zn, zm, zc, zn, zm)
    sa_f0, sa_b0, cf0, nf0, mf0, cb0, nb0, mb0 = (t.astype(f32) for t in init)

    qa = heads(qa, A_DK) * A_DK ** -0.5
    ka = heads(ka, A_DK)
    va = heads(va, A_DV)
    lr_f, lr_b = jnp.split(lra, 2, axis=-1)
    loga_f = jax.nn.log_sigmoid((lr_f @ gla_w_gate_f + gla_b_gate_f).astype(f32)).reshape(
        bsz, n, A_HEADS, A_DK) / A_GATE_TEMP
    loga_b = jax.nn.log_sigmoid((lr_b @ gla_w_gate_b + gla_b_gate_b).astype(f32)).reshape(
        bsz, n, A_HEADS, A_DK) / A_GATE_TEMP
    oa_f, sa_f = gla_chunked(qa, ka, va, loga_f, sa_f0)
    oa_b, sa_b = gla_chunked(flip(qa), flip(ka), flip(va), flip(loga_b), sa_b0)
    oa = rmsnorm(oa_f + flip(oa_b), gla_g_norm) * jax.nn.silu(heads(ga, A_DV))

    qb = heads(qb, B_DK)
    kb = heads(kb, B_DK) * B_DK ** -0.5
    vb = heads(vb, B_DV)
    g = (gates + mlstm_b_gates).astype(f32)
    i_f, i_b, f_f, f_b = jnp.split(g, 4, axis=-1)
    hb_f, st_f = mlstm_chunked(qb, kb, vb, i_f, jax.nn.log_sigmoid(f_f), (cf0, nf0, mf0))
    hb_b, st_b = mlstm_chunked(flip(qb), flip(kb), flip(vb), flip(i_b),
                               flip(jax.nn.log_sigmoid(f_b)), (cb0, nb0, mb0))
    om = jax.nn.sigmoid(heads(ob, B_DV)) * rmsnorm(hb_f + flip(hb_b), mlstm_g_norm)

    mixed = jnp.concatenate([oa.reshape(bsz, n, -1), om.reshape(bsz, n, -1)], axis=-1).astype(h.dtype)
    return mixed @ w_out, (sa_f, sa_b, st_f[0], st_f[1], st_f[2], st_b[0], st_b[1], st_b[2])


def axial_rope_tables(n):
    rows = n // GRID_W
    pos = jnp.arange(rows * GRID_W)
    row = (pos // GRID_W).astype(jnp.float32)
    col = (pos % GRID_W).astype(jnp.float32)
    half = C_ROPE // 2
    inv = 1.0 / (ROPE_THETA ** (jnp.arange(0, half, 2, dtype=jnp.float32) / half))
    ang_r = row[:, None] * inv[None, :]
    ang_c = col[:, None] * inv[None, :]
    return jnp.cos(ang_r), jnp.sin(ang_r), jnp.cos(ang_c), jnp.sin(ang_c)


def rotate_half(x, cos, sin):
    x1, x2 = jnp.split(x, 2, axis=-1)
    cos = cos[:, None, :]
    sin = sin[:, None, :]
    return jnp.concatenate([x1 * cos - x2 * sin, x1 * sin + x2 * cos], axis=-1)


def apply_axial_rope(x, tables):
    cr, sr, cc, sc = tables
    xr, xc = jnp.split(x.astype(jnp.float32), 2, axis=-1)
    return jnp.concatenate([rotate_half(xr, cr, sr), rotate_half(xc, cc, sc)], axis=-1).astype(x.dtype)


def blocked_attention(q, k, v, scale):
    bsz, n, h, dq = q.shape
    nb = n // Q_BLOCK
    qb = q.reshape(bsz, nb, Q_BLOCK, h, dq).transpose(1, 0, 2, 3, 4)

    def one_block(qi):
        s = jnp.einsum('bqhd,bkhd->bhqk', qi, k).astype(jnp.float32) * scale
        p = jax.nn.softmax(s, axis=-1).astype(v.dtype)
        return jnp.einsum('bhqk,bkhd->bqhd', p, v)

    out = lax.map(one_block, qb)
    return out.transpose(1, 0, 2, 3, 4).reshape(bsz, n, h, v.shape[-1])


def mla_project(h, w_in, g_q_norm, w_qb, g_kv_norm):
    bsz, n, _ = h.shape
    q_a, kv_a, k_r = split_cols(h @ w_in, L1_SIZES)
    q = (rmsnorm(q_a, g_q_norm) @ w_qb).reshape(bsz, n, C_HEADS, C_NOPE + C_ROPE)
    q_nope, q_rope = jnp.split(q, [C_NOPE], axis=-1)
    return q_nope, q_rope, rmsnorm(kv_a, g_kv_norm), k_r


def mla_attend(q_nope, q_rope, ckv, k_rope, w_kvb, w_out):
    bsz, n = q_nope.shape[0], q_nope.shape[1]
    nk = ckv.shape[1]
    kv = (ckv @ w_kvb).reshape(bsz, nk, C_HEADS, C_NOPE + C_DV)
    k_nope, v = jnp.split(kv, [C_NOPE], axis=-1)
    k = jnp.concatenate([k_nope, jnp.broadcast_to(k_rope[:, :, None, :], (bsz, nk, C_HEADS, C_ROPE))], axis=-1)
    q = jnp.concatenate([q_nope, q_rope], axis=-1)
    o = blocked_attention(q, k, v, (C_NOPE + C_ROPE) ** -0.5)
    return o.reshape(bsz, n, C_HEADS * C_DV) @ w_out


def mla_context(h, w_in, g_q_norm, w_qb, g_kv_norm, w_kvb, w_out):
    q_nope, q_rope, ckv, k_r = mla_project(h, w_in, g_q_norm, w_qb, g_kv_norm)
    return mla_attend(q_nope, q_rope, ckv, k_r, w_kvb, w_out), (ckv, k_r)


def mla_latent(h, w_in, g_q_norm, w_qb, g_kv_norm, w_kvb, w_out, ctx_ckv, ctx_krope):
    q_nope, q_rope, ckv, k_r = mla_project(h, w_in, g_q_norm, w_qb, g_kv_norm)
    tables = axial_rope_tables(h.shape[1])
    q_rope = apply_axial_rope(q_rope, tables)
    k_r = apply_axial_rope(k_r[:, :, None, :], tables)[:, :, 0, :]
    ckv_all = jnp.concatenate([ctx_ckv.astype(ckv.dtype), ckv], axis=1)
    kr_all = jnp.concatenate([ctx_krope.astype(k_r.dtype), k_r], axis=1)
    return mla_attend(q_nope, q_rope, ckv_all, kr_all, w_kvb, w_out)


def conv_ffn(h, w_up, conv_w, conv_b, w_down):
    u = h @ w_up
    u = lax.conv_general_dilated(
        u, conv_w[:, None, :].astype(u.dtype), window_strides=(1,),
        padding=[(CONV_W // 2, CONV_W // 2)], dimension_numbers=('NWC', 'WIO', 'NWC'),
        feature_group_count=u.shape[-1]) + conv_b
    a, g = jnp.split(u, 2, axis=-1)
    return (jax.nn.silu(g) * a) @ w_down


def setup_inputs(seed: int = 0) -> dict:
    key = jax.random.key(seed)
    ks = iter(jax.random.split(key, 96))
    D = D_MODEL

    def nrm(shape, scale=1.0):
        return scale * jax.random.normal(next(ks), shape, jnp.float32)

    def gain(n):
        return 1.0 + nrm((n,), 0.02)

    inp = {}
    inp['x_prompt'] = nrm((BATCH, SEQ, D))
    inp['x_sample'] = nrm((DEC_BATCH, DEC_SEQ, D))
    inp['state_l0_gla_fwd'] = nrm((DEC_BATCH, A_HEADS, A_DK, A_DV), 0.5)
    inp['state_l0_gla_bwd'] = nrm((DEC_BATCH, A_HEADS, A_DK, A_DV), 0.5)
    inp['state_l0_mlstm_c_fwd'] = nrm((DEC_BATCH, B_HEADS, B_DK, B_DV), 0.5)
    inp['state_l0_mlstm_n_fwd'] = nrm((DEC_BATCH, B_HEADS, B_DK), 0.5)
    inp['state_l0_mlstm_m_fwd'] = nrm((DEC_BATCH, B_HEADS), 0.5)
    inp['state_l0_mlstm_c_bwd'] = nrm((DEC_BATCH, B_HEADS, B_DK, B_DV), 0.5)
    inp['state_l0_mlstm_n_bwd'] = nrm((DEC_BATCH, B_HEADS, B_DK), 0.5)
    inp['state_l0_mlstm_m_bwd'] = nrm((DEC_BATCH, B_HEADS), 0.5)
    inp['cache_l1_ckv'] = nrm((DEC_BATCH, PAST_LEN, C_KV_RANK))
    inp['cache_l1_krope'] = nrm((DEC_BATCH, PAST_LEN, C_ROPE))
    inp['c'] = nrm((DEC_BATCH, D))
    inp['c_ctx'] = nrm((D,))
    inp['l0_w_mod'] = nrm((D, N_MOD * D), 0.5 * D ** -0.5)
    inp['l0_b_mod'] = nrm((N_MOD * D,), 0.02)
    inp['l0_g_pre_mix'] = gain(D)
    inp['l0_g_post_mix'] = gain(D)
    inp['l0_g_pre_ffn'] = gain(D)
    inp['l0_g_post_ffn'] = gain(D)
    inp['l0_w_in'] = nrm((D, L0_IN), D ** -0.5)
    inp['l0_gla_w_gate_f'] = nrm((A_GATE_RANK, A_HEADS * A_DK), A_GATE_RANK ** -0.5)
    inp['l0_gla_b_gate_f'] = nrm((A_HEADS * A_DK,), 0.02)
    inp['l0_gla_w_gate_b'] = nrm((A_GATE_RANK, A_HEADS * A_DK), A_GATE_RANK ** -0.5)
    inp['l0_gla_b_gate_b'] = nrm((A_HEADS * A_DK,), 0.02)
    inp['l0_gla_g_norm'] = gain(A_DV)
    inp['l0_mlstm_b_gates'] = jnp.concatenate([nrm((2 * B_HEADS,), 0.1), 3.0 + nrm((2 * B_HEADS,), 0.1)])
    inp['l0_mlstm_g_norm'] = gain(B_DV)
    inp['l0_w_out'] = nrm((L0_MIX, D), L0_MIX ** -0.5)
    inp['l0_ffn_w_up'] = nrm((D, 2 * D_FF), D ** -0.5)
    inp['l0_ffn_conv_w'] = nrm((CONV_W, 2 * D_FF), CONV_W ** -0.5)
    inp['l0_ffn_conv_b'] = nrm((2 * D_FF,), 0.02)
    inp['l0_ffn_w_down'] = nrm((D_FF, D), D_FF ** -0.5)
    inp['l1_w_mod'] = nrm((D, N_MOD * D), 0.5 * D ** -0.5)
    inp['l1_b_mod'] = nrm((N_MOD * D,), 0.02)
    inp['l1_g_pre_mix'] = gain(D)
    inp['l1_g_post_mix'] = gain(D)
    inp['l1_g_pre_ffn'] = gain(D)
    inp['l1_g_post_ffn'] = gain(D)
    inp['l1_w_in'] = nrm((D, L1_IN), D ** -0.5)
    inp['l1_g_q_norm'] = gain(C_Q_RANK)
    inp['l1_w_qb'] = nrm((C_Q_RANK, C_HEADS * (C_NOPE + C_ROPE)), C_Q_RANK ** -0.5)
    inp['l1_g_kv_norm'] = gain(C_KV_RANK)
    inp['l1_w_kvb'] = nrm((C_KV_RANK, C_HEADS * (C_NOPE + C_DV)), C_KV_RANK ** -0.5)
    inp['l1_w_out'] = nrm((C_HEADS * C_DV, D), (C_HEADS * C_DV) ** -0.5)
    inp['l1_ffn_w_up'] = nrm((D, 2 * D_FF), D ** -0.5)
    inp['l1_ffn_conv_w'] = nrm((CONV_W, 2 * D_FF), CONV_W ** -0.5)
    inp['l1_ffn_conv_b'] = nrm((2 * D_FF,), 0.02)
    inp['l1_ffn_w_down'] = nrm((D_FF, D), D_FF ** -0.5)
    return inp


def reference(x_prompt, x_sample, state_l0_gla_fwd, state_l0_gla_bwd, state_l0_mlstm_c_fwd,
              state_l0_mlstm_n_fwd, state_l0_mlstm_m_fwd, state_l0_mlstm_c_bwd, state_l0_mlstm_n_bwd,
              state_l0_mlstm_m_bwd, cache_l1_ckv, cache_l1_krope, c, c_ctx,
              l0_w_mod, l0_b_mod, l0_g_pre_mix, l0_g_post_mix, l0_g_pre_ffn, l0_g_post_ffn, l0_w_in,
              l0_gla_w_gate_f, l0_gla_b_gate_f, l0_gla_w_gate_b, l0_gla_b_gate_b, l0_gla_g_norm,
              l0_mlstm_b_gates, l0_mlstm_g_norm, l0_w_out, l0_ffn_w_up, l0_ffn_conv_w, l0_ffn_conv_b,
              l0_ffn_w_down,
              l1_w_mod, l1_b_mod, l1_g_pre_mix, l1_g_post_mix, l1_g_pre_ffn, l1_g_post_ffn, l1_w_in,
              l1_g_q_norm, l1_w_qb, l1_g_kv_norm, l1_w_kvb, l1_w_out, l1_ffn_w_up, l1_ffn_conv_w,
              l1_ffn_conv_b, l1_ffn_w_down):
    ctx_cond = c_ctx[None, :]
    shared = (
        (l0_w_mod, l0_b_mod, l0_g_pre_mix, l0_g_post_mix, l0_g_pre_ffn, l0_g_post_ffn,
         l0_ffn_w_up, l0_ffn_conv_w, l0_ffn_conv_b, l0_ffn_w_down),
        (l1_w_mod, l1_b_mod, l1_g_pre_mix, l1_g_post_mix, l1_g_pre_ffn, l1_g_post_ffn,
         l1_ffn_w_up, l1_ffn_conv_w, l1_ffn_conv_b, l1_ffn_w_down),
    )
    mixers = (
        (l0_w_in, l0_gla_w_gate_f, l0_gla_b_gate_f, l0_gla_w_gate_b, l0_gla_b_gate_b, l0_gla_g_norm,
         l0_mlstm_b_gates, l0_mlstm_g_norm, l0_w_out),
        (l1_w_in, l1_g_q_norm, l1_w_qb, l1_g_kv_norm, l1_w_kvb, l1_w_out),
    )
    caches = (
        (state_l0_gla_fwd, state_l0_gla_bwd, state_l0_mlstm_c_fwd, state_l0_mlstm_n_fwd,
         state_l0_mlstm_m_fwd, state_l0_mlstm_c_bwd, state_l0_mlstm_n_bwd, state_l0_mlstm_m_bwd),
        (cache_l1_ckv, cache_l1_krope),
    )
    xp, xs = x_prompt, x_sample
    ctx_states = []
    for li in range(DEPTH):
        (w_mod, b_mod, g_pre_mix, g_post_mix, g_pre_ffn, g_post_ffn,
         w_up, conv_w, conv_b, w_down) = shared[li]
        sh1p, sc1p, gt1p, sh2p, sc2p, gt2p = modulation(ctx_cond, w_mod, b_mod)
        sh1s, sc1s, gt1s, sh2s, sc2s, gt2s = modulation(c, w_mod, b_mod)
        hp = rmsnorm(xp, g_pre_mix) * (1.0 + sc1p) + sh1p
        hs = rmsnorm(xs, g_pre_mix) * (1.0 + sc1s) + sh1s
        if li % 2 == 0:
            mp, st = recurrent_mixer(hp, *mixers[li], None)
            ms, _ = recurrent_mixer(hs, *mixers[li], caches[li])
        else:
            mp, st = mla_context(hp, *mixers[li])
            ms = mla_latent(hs, *mixers[li], *caches[li])
        ctx_states.append(st)
        xp = xp + gt1p * rmsnorm(mp, g_post_mix)
        xs = xs + gt1s * rmsnorm(ms, g_post_mix)
        hp = rmsnorm(xp, g_pre_ffn) * (1.0 + sc2p) + sh2p
        hs = rmsnorm(xs, g_pre_ffn) * (1.0 + sc2s) + sh2s
        xp = xp + gt2p * rmsnorm(conv_ffn(hp, w_up, conv_w, conv_b, w_down), g_post_ffn)
        xs = xs + gt2s * rmsnorm(conv_ffn(hs, w_up, conv_w, conv_b, w_down), g_post_ffn)
    (gla_f, gla_b, mc_f, mn_f, mm_f, mc_b, mn_b, mm_b) = ctx_states[0]
    (ckv, krope) = ctx_states[1]
    return (xp, xs, gla_f, gla_b, mc_f, mn_f, mm_f, mc_b, mn_b, mm_b, ckv, krope)
```

```python
import contextlib
import numpy as np
import concourse.bass as bass
import concourse.mybir as mybir
from concourse.bass_utils import run_bass_kernel_spmd
from concourse.alu_op_type import AluOpType as ALU

F32 = mybir.dt.float32
F32R = mybir.dt.float32r
AF = mybir.ActivationFunctionType
AX = mybir.AxisListType

SAME_ENGINE_SYNC = True
class Tile:
    def __init__(self, sc, name, shape, dtype=F32, space="sb"):
        self.name, self.shape, self.dtype, self.space = name, list(shape), dtype, space
        alloc = sc.nc.sbuf_tensor if space == "sb" else sc.nc.psum_tensor
        self.h = sc.es.enter_context(alloc(name, list(shape), dtype))
        self.wr = {}
        self.rd = {}

    def __getitem__(self, idx):
        return Ref(self, idx)

    def v(self, dtype):
        return _View(self, dtype)


class _View:
    def __init__(self, tile, dtype):
        self.tile, self.dtype = tile, dtype

    def __getitem__(self, idx):
        return Ref(self.tile, idx, self.dtype)


class Ref:
    def __init__(self, tile, idx, dtype=None):
        if not isinstance(idx, tuple):
            idx = (idx,)
        idx = idx + (slice(None),) * (len(tile.shape) - len(idx))
        self.tile = tile
        ap = tile.h[idx]
        if dtype is not None and dtype != tile.dtype:
            ap = ap.bitcast(dtype)
        self.ap = ap
        box = []
        for i, n in zip(idx, tile.shape):
            if isinstance(i, int):
                box.append((i, i + 1))
            else:
                a, b, st = i.indices(n)
                box.append((a, b))
        self.box = tuple(box)


def _overlap(a, b):
    return all(x[0] < y[1] and y[0] < x[1] for x, y in zip(a, b))


def _contains(a, b):
    return all(x[0] <= y[0] and x[1] >= y[1] for x, y in zip(a, b))


class Op:
    __slots__ = ("eng", "fn", "waits", "is_dma", "sem", "target", "signal", "rank", "seq", "clock", "selfwait")


class Sched:
    def __init__(self, nc, es, n_dma_sems=20):
        self.nc = nc
        self.stacks = [es]
        self.E = {"pe": nc.tensor, "act": nc.scalar, "dve": nc.vector, "pool": nc.gpsimd, "sp": nc.sync}
        self.ops = []
        self.seq = {e: 0 for e in self.E}
        self.clock = {e: {} for e in self.E}
        self.esem = {e: es.enter_context(nc.semaphore("es_" + e)) for e in ("pe", "act", "dve", "pool")}
        self.dsem = {}
        for q in ("sp", "pool"):
            self.dsem[q] = [[es.enter_context(nc.semaphore("ds_%s_%d" % (q, i))), 0] for i in range(n_dma_sems)]
        self.dnext = {q: 0 for q in self.dsem}
        self.ntile = 0
        self.emitted = 0
        self.dma_barrier = 0
        self.cnt = {e: 0 for e in self.E}
        self.waited = {}
        self.stats = dict(nops=0, nwait=0)

    @property
    def es(self):
        return self.stacks[-1]

    def tile(self, name, shape, dtype=F32, space="sb"):
        self.ntile += 1
        return Tile(self, "%s_%d" % (name, self.ntile), shape, dtype, space)

    @contextlib.contextmanager
    def scope(self):
        sub = contextlib.ExitStack()
        self.stacks.append(sub)
        try:
            yield sub
        finally:
            self.flush()
            self.stacks.pop()
            sub.close()

    def carve(self, parent, name, off, shape, dtype=F32):
        t = Tile.__new__(Tile)
        t.name, t.shape, t.dtype, t.space = name, list(shape), dtype, parent.space
        n = 1
        for d_ in shape[1:]:
            n *= d_
        names = "abcdefg"[:len(parent.shape) - 1]
        flat = parent.h[:].rearrange("p %s -> p (%s)" % (" ".join(names), " ".join(names)))
        ap = flat[0:shape[0], off:off + n]
        if dtype != parent.dtype:
            ap = ap.bitcast(dtype)
        if len(shape) > 2:
            nm = "abcdefg"[:len(shape) - 1]
            kw = {nm[i]: shape[1 + i] for i in range(len(shape) - 2)}
            ap = ap.rearrange("p (%s) -> p %s" % (" ".join(nm), " ".join(nm)), **kw)
        t.h = ap
        t.wr, t.rd = {}, {}
        return t

    def _record(self, eng, fn, reads, writes, is_dma=False):
        op = Op()
        op.eng, op.fn, op.is_dma = eng, fn, is_dma
        op.signal, op.rank, op.sem, op.target, op.selfwait = False, 0, None, 0, None
        oid = len(self.ops)
        deps = set()
        ps_seen = {}
        for r, isw in [(r, False) for r in reads] + [(w, True) for w in writes]:
            if isinstance(r, Ref) and r.tile.space == "ps":
                ps_seen[id(r.tile)] = (r.tile, ps_seen.get(id(r.tile), (None, False))[1] or isw)
        for t, isw in ps_seen.values():
            acc = t.__dict__.get("acc", {})
            for f, (o, w) in acc.items():
                if f != eng or w or isw:
                    deps.add(o)
            t.acc = {eng: (oid, isw)}
        reads = [r for r in reads if isinstance(r, Ref) and r.tile.space != "ps"]
        writes = [w for w in writes if isinstance(w, Ref) and w.tile.space != "ps"]
        for r in reads:
            if not isinstance(r, Ref):
                continue
            for box, w in r.tile.wr.items():
                if _overlap(box, r.box):
                    deps.add(w)
        for w in writes:
            if not isinstance(w, Ref):
                continue
            for box, o in w.tile.wr.items():
                if _overlap(box, w.box):
                    deps.add(o)
            for (box, _e), o in w.tile.rd.items():
                if _overlap(box, w.box):
                    deps.add(o)
        for w in writes:
            if not isinstance(w, Ref):
                continue
            t = w.tile
            t.wr = {b: o for b, o in t.wr.items() if not _contains(w.box, b)}
            t.rd = {k: o for k, o in t.rd.items() if not _contains(w.box, k[0])}
            t.wr[w.box] = oid
        for r in reads:
            if not isinstance(r, Ref):
                continue
            r.tile.rd[(r.box, eng if not is_dma else ("dma", oid))] = oid
        clk = self.clock[eng]
        waits = []
        for d in sorted(deps):
            p = self.ops[d]
            if p.is_dma:
                if d < self.dma_barrier:
                    continue
                key = ("dma", d)
                if clk.get(key, -1) >= 0:
                    continue
                waits.append(d)
                clk[key] = 0
            else:
                if p.eng == eng and (eng == "pe" or not SAME_ENGINE_SYNC):
                    continue
                if clk.get(p.eng, -1) >= p.seq:
                    continue
                waits.append(d)
                if clk.get(p.eng, -1) < p.seq:
                    clk[p.eng] = p.seq
            for k, v in p.clock.items():
                if clk.get(k, -1) < v:
                    clk[k] = v
        op.waits = waits
        op.seq = self.seq[eng]
        self.seq[eng] += 1
        op.clock = dict(clk)
        if not is_dma:
            op.clock[eng] = op.seq
        self.ops.append(op)
        return op

    def op(self, eng, fn, reads=(), writes=()):
        return self._record(eng, fn, list(reads), list(writes))

    def dma(self, q, out, in_, **kw):
        oap = out.ap if isinstance(out, Ref) else out
        iap = in_.ap if isinstance(in_, Ref) else in_
        op = self._record(q, lambda e: e.dma_start(out=oap, in_=iap, **kw),
                          [in_] if isinstance(in_, Ref) else [], [out] if isinstance(out, Ref) else [], is_dma=True)
        i = self.dnext[q]
        self.dnext[q] = (i + 1) % len(self.dsem[q])
        ent = self.dsem[q][i]
        if ent[1] > 0:
            op.selfwait = (ent[0], ent[1])
        ent[1] += 16
        op.sem, op.target = ent[0], ent[1]
        return op

    def _wait(self, engname, key, val):
        wk = (engname, id(key))
        if self.waited.get(wk, -1) >= val:
            return
        self.waited[wk] = val
        self.E[engname].wait_ge(key, val)
        self.stats["nwait"] += 1

    def flush(self, final=False):
        pend = self.ops[self.emitted:]
        last = {}
        for op in pend:
            for d in op.waits:
                p = self.ops[d]
                if not p.is_dma:
                    assert d >= self.emitted, "dependency on pre-barrier op"
                    p.signal = True
            if not op.is_dma:
                last[op.eng] = op
        for op in last.values():
            op.signal = True
        for op in pend:
            if op.signal:
                self.cnt[op.eng] += 1
                op.rank = self.cnt[op.eng]
        for op in pend:
            eng = self.E[op.eng]
            need = {}
            for d in op.waits:
                p = self.ops[d]
                key, val = (p.sem, p.target) if p.is_dma else (self.esem[p.eng], p.rank)
                if need.get(id(key), (None, -1))[1] < val:
                    need[id(key)] = (key, val)
            if op.selfwait is not None:
                key, val = op.selfwait
                if need.get(id(key), (None, -1))[1] < val:
                    need[id(key)] = (key, val)
            for key, val in need.values():
                self._wait(op.eng, key, val)
            ins = op.fn(eng)
            if op.is_dma:
                ins.then_inc(op.sem, 16)
            elif op.signal:
                ins.then_inc(self.esem[op.eng], 1)
            op.fn = None
        self.stats["nops"] += len(pend)
        self.emitted = len(self.ops)
        engs = ["sp"] if final else ["pe", "act", "dve", "sp", "pool"]
        for e in engs:
            for f in ("pe", "act", "dve", "pool"):
                if self.cnt[f] > 0:
                    self._wait(e, self.esem[f], self.cnt[f])
            for q in self.dsem:
                for sem, tgt in self.dsem[q]:
                    if tgt > 0:
                        self._wait(e, sem, tgt)
        for e in self.E:
            for f in ("pe", "act", "dve", "pool"):
                self.clock[e][f] = self.seq[f] - 1
        self.dma_barrier = len(self.ops)

    def mm(self, out, lhsT, rhs, start=True, stop=True):
        return self.op("pe", lambda e: e.matmul(out.ap, lhsT.ap, rhs.ap, start=start, stop=stop), [lhsT, rhs], [out])

    def tr(self, out, in_, ident):
        return self.op("pe", lambda e: e.transpose(out.ap, in_.ap, ident.ap), [in_, ident], [out])

    def act(self, out, in_, func, bias=None, scale=None, accum=None, eng="act"):
        kw = {}
        rd = [in_]
        wr = [out]
        if bias is not None:
            kw["bias"] = bias.ap if isinstance(bias, Ref) else bias
            if isinstance(bias, Ref):
                rd.append(bias)
        if scale is not None:
            kw["scale"] = scale.ap if isinstance(scale, Ref) else scale
            if isinstance(scale, Ref):
                rd.append(scale)
        if accum is not None:
            kw["accum_out"] = accum.ap
            wr.append(accum)
        return self.op(eng, lambda e: e.activation(out=out.ap, in_=in_.ap, func=func, **kw), rd, wr)

    def tt(self, out, a, b, op, eng="dve"):
        return self.op(eng, lambda e: e.tensor_tensor(out=out.ap, in0=a.ap, in1=b.ap, op=op), [a, b], [out])

    def ts(self, out, a, s1, op0, s2=None, op1=None, eng="dve", accum=None):
        rd = [a]
        wr = [out]
        v1 = s1.ap if isinstance(s1, Ref) else s1
        v2 = s2.ap if isinstance(s2, Ref) else s2
        if isinstance(s1, Ref):
            rd.append(s1)
        if isinstance(s2, Ref):
            rd.append(s2)
        kw = {}
        if op1 is not None:
            kw["op1"] = op1
        if accum is not None:
            kw["accum_out"] = accum.ap
            wr.append(accum)
        return self.op(eng, lambda e: e.tensor_scalar(out=out.ap, in0=a.ap, scalar1=v1, scalar2=v2, op0=op0, **kw), rd, wr)

    def stt(self, out, a, scalar, b, op0, op1):
        rd = [a, b]
        v = scalar.ap if isinstance(scalar, Ref) else scalar
        if isinstance(scalar, Ref):
            rd.append(scalar)
        return self.op("dve", lambda e: e.scalar_tensor_tensor(out=out.ap, in0=a.ap, scalar=v, in1=b.ap, op0=op0, op1=op1), rd, [out])

    def copy(self, out, in_, eng="act"):
        if eng == "act":
            return self.op("act", lambda e: e.copy(out.ap, in_.ap), [in_], [out])
        return self.op(eng, lambda e: e.tensor_copy(out.ap, in_.ap), [in_], [out])

    def rmax(self, out, in_):
        return self.op("dve", lambda e: e.reduce_max(out.ap, in_.ap, axis=AX.X), [in_], [out])

    def rsum(self, out, in_):
        return self.op("dve", lambda e: e.reduce_sum(out.ap, in_.ap, axis=AX.X), [in_], [out])

    def memset(self, out, val, eng="dve"):
        return self.op(eng, lambda e: e.memset(out.ap, val), [], [out])


def _reref(ref, pattern, **kw):
    r = Ref.__new__(Ref)
    r.tile, r.box = ref.tile, ref.box
    r.ap = ref.ap.rearrange(pattern, **kw)
    return r


Ref.re = _reref

D = 1024
T = 1280
NU = 5
EPS = 1e-6
TT = [(0, 512), (512, 512), (1024, 256)]
NKEY = 1792
SLOTW = 4096
ATT_SCALE = 192.0 ** -0.5
NEG = -30000.0
WARM_ATT = 0
WARM_PROJ = 0
BOTH_SWAP = 0
FFN_POOL_ACC = 0
OUT_DELAY = 6
MOD_IDLE = 30
WARM_MIX = 0

C_ID, C_ONE, C_TGF, C_TGB, C_TMF, C_TMB, C_MPF, C_MPB, C_MAF, C_MAB, C_SLF, C_SLB = [i * 128 for i in range(12)]
NCST = 12 * 128


def make_consts():
    c = np.zeros((128, NCST), np.float32)
    s = np.arange(128)[:, None]
    t = np.arange(128)[None, :]
    le = (s <= t).astype(np.float32)
    ge = (s >= t).astype(np.float32)
    c[:, C_ID:C_ID + 128] = np.eye(128, dtype=np.float32)
    c[:, C_ONE:C_ONE + 128] = 1.0
    c[:, C_TGF:C_TGF + 128] = -le / 16.0
    c[:, C_TGB:C_TGB + 128] = -ge / 16.0
    c[:, C_TMF:C_TMF + 128] = -le
    c[:, C_TMB:C_TMB + 128] = -ge
    c[:, C_MPF:C_MPF + 128] = le
    c[:, C_MPB:C_MPB + 128] = ge
    c[:, C_MAF:C_MAF + 128] = np.where(s >= t, 0.0, -1e30)
    c[:, C_MAB:C_MAB + 128] = np.where(s <= t, 0.0, -1e30)
    c[127, C_SLF:C_SLF + 128] = 1.0
    c[0, C_SLB:C_SLB + 128] = 1.0
    return c


class Ring:
    def __init__(self, sc, nslot):
        self.sc = sc
        self.slots = [sc.tile("ring", [128, SLOTW], F32R) for _ in range(nslot)]
        self.i = 0
        self.pre = {}

    def load(self, dram3d, nk, W, q="pool", idx=None, key=None):
        if key is not None and key in self.pre:
            return self.pre.pop(key)
        reserved = {id(s_.t) for s_ in self.pre.values()}
        if idx is None:
            for _ in range(len(self.slots)):
                idx = self.i
                self.i = (self.i + 1) % len(self.slots)
                if id(self.slots[idx]) not in reserved:
                    break
        assert id(self.slots[idx]) not in reserved, "ring slot holds preloaded weights that were not consumed yet"
        t = self.slots[idx]
        dst = t[:, 0:nk * W].re("p (k n) -> p k n", k=nk)
        self.sc.dma(q, dst, dram3d)
        s_ = _Slot(t, nk, W)
        s_.idx = idx
        return s_

    def preload(self, key, dram3d, nk, W, idx=None):
        self.pre[key] = self.load(dram3d, nk, W, idx=idx)


class _Slot:
    def __init__(self, t, nk, W):
        self.t, self.nk, self.W = t, nk, W

    def w(self, kc, a, b, p0=0, p1=128):
        return self.t[p0:p1, kc * self.W + a: kc * self.W + b]


class PsPool:
    def __init__(self, banks, width):
        self.banks, self.width = banks, width
        self.slots = [(b, o) for b in banks for o in range(0, 512, width)]
        self.i = 0

    def get(self):
        b, o = self.slots[self.i]
        self.i = (self.i + 1) % len(self.slots)
        return _PsView(b, o)


class _PsView:
    def __init__(self, bank, off):
        self.bank, self.off = bank, off

    def __getitem__(self, idx):
        p, c = idx
        a, b, _ = c.indices(512 - self.off)
        return self.bank[p, self.off + a:self.off + b]


def build_program(stop=None, dumps=(), nslot=2):
    nc = bass.Bass("TRN2", target_bir_lowering=False)

    def din(name, shape):
        return nc.dram_tensor(name, list(shape), F32, kind="ExternalInput").ap()

    def dout(name, shape):
        return nc.dram_tensor(name, list(shape), F32, kind="ExternalOutput").ap()

    xin = din("xin", [T, D])
    cond2T = din("cond2T", [128, 8, 2])
    cstd = din("cst", [128, NCST])
    chaind = din("chain", [128, 2, NU])
    ginit = din("ginit", [2, NU, 4, 64, 128])
    minit = din("minit", [2, NU, 4, 64, 129])
    mminit = din("mminit", [128, 2, NU, 4])
    ropeT = din("ropeT", [128, T])
    ktab = din("ktab", [6, NKEY])
    qtab = din("qtab", [5, T])
    ckvc = din("ckvc", [512, 256])
    krc = din("krc", [512, 64])
    wmod = din("wmod", [2, D, 6144])
    bmodT = din("bmodT", [128, 2, 96])
    gvecT = din("gvecT", [128, 2, 4, 16])
    w0cat = din("w0cat", [D, 4096])
    wgd = din("wg", [33, 2, 512])
    gnd = din("gn", [128, 2, 128])
    bgd = din("bgates", [128, 16])
    woutd = din("wout", [2, D, D])
    wupd = din("wup", [2, D, 5632])
    convd = din("convT", [128, 2, 44, 4])
    wdownd = din("wdown", [2, 2816, D])
    w1in = din("w1in", [D, 768])
    wqbd = din("wqb", [384, 2048])
    wkvbd = din("wkvb", [256, 2048])
    gqkvd = din("gqkv", [128, 5])
    yout = dout("y", [T, D])
    gla_o = dout("gla_o", [2, NU, 4, 64, 128])
    mC_o = dout("mC_o", [2, NU, 4, 64, 129])
    mm_o = dout("mm_o", [1, 40])
    ckv_o = dout("ckv_o", [T, 256])
    kr_o = dout("kr_o", [T, 64])
    dbg_out = {}

    class Stop(Exception):
        pass

    es = contextlib.ExitStack()
    with es:
        sc = Sched(nc, es)
        tile = sc.tile
        x = tile("x", [128, 8, T])
        A = tile("A", [128, 8, T], F32R)
        B = tile("B", [128, 8, T], F32R)
        Af, Bf = A.v(F32), B.v(F32)
        ring = Ring(sc, nslot)
        cst = tile("cst", [128, NCST])
        onesR = tile("onesR", [128, 128], F32R)
        chain = tile("chain", [128, 2, NU])
        modv = tile("modv", [128, 2, 96])
        bmod = tile("bmod", [128, 2, 96])
        gv = tile("gv", [128, 2, 4, 16])
        A1 = tile("A1", [128, 2, 2, 16])
        GG = tile("GG", [128, 2, 2, 16])
        banks = [tile("ps", [128, 512], F32, "ps") for _ in range(8)]

        ident = cst[:, C_ID:C_ID + 128]
        ones = cst[:, C_ONE:C_ONE + 128]

        pG = PsPool(banks[0:4], 512)
        pS = banks[4]
        pQ = PsPool(banks[5:8] + banks[0:4], 512)
        pO, pD = banks[5], banks[6]
        pX = PsPool([banks[7]] + banks[0:3], 512)
        pF = PsPool(banks[0:4] + banks[5:8], 512)

        def dump(name, ref, shape):
            if name in dumps:
                o = dout("dbg_" + name, shape)
                sc.dma("sp", o, ref)
                dbg_out[name] = shape

        def phase_end(name):
            if stop == name:
                raise Stop()

        def norm_tiles():
            return dict(sq=[tile("sq", [128, 512], F32R) for _ in range(2)], R=[tile("R", [128, 512]) for _ in range(2)],
                        lnt=tile("lnt", [128, 512]), tmp=[tile("tmp512", [128, 512]) for _ in range(2)], c=[0, 0, 0])

        def compute_R(nt, src_fn, nch, dim, n, bank=None):
            bank = pS if bank is None else bank
            for ch in range(nch):
                nt["c"][0] += 1
                sq = nt["sq"][nt["c"][0] % len(nt["sq"])]
                sc.act(sq[:, 0:n], src_fn(ch), AF.Square)
                sc.mm(bank[:, 0:n], onesR[:, :], sq[:, 0:n], start=(ch == 0), stop=(ch == nch - 1))
            sc.act(nt["lnt"][:, 0:n], bank[:, 0:n], AF.Ln, bias=EPS, scale=1.0 / dim)
            nt["c"][1] += 1
            R = nt["R"][nt["c"][1] % len(nt["R"])]
            sc.act(R[:, 0:n], nt["lnt"][:, 0:n], AF.Exp, scale=-0.5)
            return R

        def shiftv(l, which, ch, cond):
            c = (which * 3) * 16 + ch * 2 + cond
            return modv[:, l, c:c + 1]

        def norm_mod(l, which, dst):
            with sc.scope():
                nt = norm_tiles()
                for ti, (c0, n) in enumerate(TT):
                    cond = 0 if ti < 2 else 1
                    R = compute_R(nt, lambda ch: x[:, ch, c0:c0 + n], 8, D, n)
                    for ch in range(8):
                        nt["c"][2] += 1
                        tm = nt["tmp"][nt["c"][2] % 2]
                        sc.tt(tm[:, 0:n], x[:, ch, c0:c0 + n], R[:, 0:n], ALU.mult)
                        k = ch * 2 + cond
                        sc.act(dst[:, ch, c0:c0 + n], tm[:, 0:n], AF.Identity,
                               bias=shiftv(l, which, ch, cond), scale=A1[:, l, which, k:k + 1])

        def post_pre(lp, wp, src, ln_, wn, dst, oproj=None):
            with sc.scope():
                sqs = [tile("sq", [128, 512], F32R) for _ in range(3)]
                tms = [tile("tmp512", [128, 512]) for _ in range(3)]
                Ra = [tile("Ra", [128, 512]) for _ in range(3)]
                Rb = [tile("Rb", [128, 512]) for _ in range(3)]
                lnts = [tile("lnt", [128, 512]) for _ in range(3)]

                if oproj is not None:
                    ol, osrc = oproj
                    s0_ = ring.load(wout_ap(ol, 0), 8, 512, key=("wout", ol, 0))
                    slots_ = [s0_, ring.load(wout_ap(ol, 1), 8, 512, idx=1 - s0_.idx)]
                    pend = []

                    def stats_mm(item):
                        nb_, ti_, n_ = item
                        sc.mm(banks[5 + ti_][:, 0:n_], onesR[:, :], sqs[ti_][:, 0:n_], start=(nb_ == 0), stop=(nb_ == 7))

                    for half in range(2):
                        slot = slots_[half]
                        for nb4 in range(4):
                            nb = half * 4 + nb4
                            for ti, (c0, n) in enumerate(TT):
                                ps = pG.get()
                                for kc in range(8):
                                    sc.mm(ps[:, 0:n], slot.w(kc, nb4 * 128, nb4 * 128 + 128), osrc[:, kc, c0:c0 + n],
                                          start=(kc == 0), stop=(kc == 7))
                                while len(pend) > 2:
                                    stats_mm(pend.pop(0))
                                sc.copy(A[:, nb, c0:c0 + n], ps[:, 0:n], "act")
                                sc.act(sqs[ti][:, 0:n], ps[:, 0:n], AF.Square)
                                pend.append((nb, ti, n))
                        if half == 0:
                            ring.preload(("wup", ol, 0), wup_ap(ol, 0), 8, 512)
                    while pend:
                        stats_mm(pend.pop(0))

                def tile_gen(ti):
                    c0, n = TT[ti]
                    cond = 0 if ti < 2 else 1
                    bA, bB = (banks[5 + ti], banks[ti]) if oproj is not None else (banks[ti], banks[3 + ti])
                    sq, tm, R, R2, lnt_ = sqs[ti], tms[ti], Ra[ti], Rb[ti], lnts[ti]
                    if oproj is None:
                        for ch in range(8):
                            sc.act(sq[:, 0:n], src[:, ch, c0:c0 + n], AF.Square)
                            sc.mm(bA[:, 0:n], onesR[:, :], sq[:, 0:n], start=(ch == 0), stop=(ch == 7))
                            yield
                    sc.act(lnt_[:, 0:n], bA[:, 0:n], AF.Ln, bias=EPS, scale=1.0 / D)
                    sc.act(R[:, 0:n], lnt_[:, 0:n], AF.Exp, scale=-0.5)
                    yield
                    for ch in range(8):
                        sc.tt(tm[:, 0:n], src[:, ch, c0:c0 + n], R[:, 0:n], ALU.mult)
                        k = ch * 2 + cond
                        sc.stt(x[:, ch, c0:c0 + n], tm[:, 0:n], GG[:, lp, wp, k:k + 1], x[:, ch, c0:c0 + n],
                               ALU.mult, ALU.add)
                        sc.act(sq[:, 0:n], x[:, ch, c0:c0 + n], AF.Square)
                        sc.mm(bB[:, 0:n], onesR[:, :], sq[:, 0:n], start=(ch == 0), stop=(ch == 7))
                        yield
                    sc.act(lnt_[:, 0:n], bB[:, 0:n], AF.Ln, bias=EPS, scale=1.0 / D)
                    sc.act(R2[:, 0:n], lnt_[:, 0:n], AF.Exp, scale=-0.5)
                    yield
                    for ch in range(8):
                        sc.tt(tm[:, 0:n], x[:, ch, c0:c0 + n], R2[:, 0:n], ALU.mult)
                        k = ch * 2 + cond
                        sc.act(dst[:, ch, c0:c0 + n], tm[:, 0:n], AF.Identity,
                               bias=shiftv(ln_, wn, ch, cond), scale=A1[:, ln_, wn, k:k + 1])
                        yield

                gens = [tile_gen(ti) for ti in range(3)]
                while gens:
                    for g in list(gens):
                        try:
                            next(g)
                        except StopIteration:
                            gens.remove(g)

        def post_norm(l, which, src, final=False):
            with sc.scope():
                sqs = [tile("sq", [128, 512], F32R) for _ in range(3)]
                tms = [tile("tmp512", [128, 512]) for _ in range(3)]
                Ra = [tile("Ra", [128, 512]) for _ in range(3)]
                lnts = [tile("lnt", [128, 512]) for _ in range(3)]
                ys = [tile("ys", [128, D]) for _ in range(3)] if final else None
                pT_ = PsPool(banks[3:8], 512)

                def tile_gen(ti):
                    c0, n = TT[ti]
                    cond = 0 if ti < 2 else 1
                    bA = banks[ti]
                    sq, tm, R, lnt_ = sqs[ti], tms[ti], Ra[ti], lnts[ti]
                    for ch in range(8):
                        sc.act(sq[:, 0:n], src[:, ch, c0:c0 + n], AF.Square)
                        sc.mm(bA[:, 0:n], onesR[:, :], sq[:, 0:n], start=(ch == 0), stop=(ch == 7))
                        yield
                    sc.act(lnt_[:, 0:n], bA[:, 0:n], AF.Ln, bias=EPS, scale=1.0 / D)
                    sc.act(R[:, 0:n], lnt_[:, 0:n], AF.Exp, scale=-0.5)
                    yield
                    for ch in range(8):
                        sc.tt(tm[:, 0:n], src[:, ch, c0:c0 + n], R[:, 0:n], ALU.mult)
                        k = ch * 2 + cond
                        sc.stt(x[:, ch, c0:c0 + n], tm[:, 0:n], GG[:, l, which, k:k + 1], x[:, ch, c0:c0 + n],
                               ALU.mult, ALU.add)
                        yield
                    if final:
                        for tt in range(c0 // 128, (c0 + n) // 128):
                            yt_ = ys[ti]
                            for half in range(2):
                                ps = pT_.get()
                                for j in range(4):
                                    ch = half * 4 + j
                                    sc.tr(ps[:, j * 128:(j + 1) * 128], x[:, ch, tt * 128:(tt + 1) * 128], ident)
                                sc.copy(yt_[:, half * 512:(half + 1) * 512], ps[:, 0:512], "act" if half == 0 else "dve")
                                yield
                            sc.dma("sp", yout[tt * 128:(tt + 1) * 128, :], yt_[:])

                gens = [tile_gen(ti) for ti in range(3)]
                while gens:
                    for g in list(gens):
                        try:
                            next(g)
                        except StopIteration:
                            gens.remove(g)

        def wout_ap(l, half):
            return woutd[l, :, half * 512:(half + 1) * 512].rearrange("(k p) n -> p k n", p=128)

        def wup_ap(l, s):
            return wupd[l, :, s * 512:(s + 1) * 512].rearrange("(k p) n -> p k n", p=128)

        def out_proj(l, src, dst):
            s0_ = ring.load(wout_ap(l, 0), 8, 512, key=("wout", l, 0))
            slots_ = [s0_, ring.load(wout_ap(l, 1), 8, 512, idx=1 - s0_.idx)]
            for half in range(2):
                slot = slots_[half]
                for nb4 in range(4):
                    nb = half * 4 + nb4
                    for (c0, n) in TT:
                        ps = pG.get()
                        for kc in range(8):
                            sc.mm(ps[:, 0:n], slot.w(kc, nb4 * 128, nb4 * 128 + 128), src[:, kc, c0:c0 + n],
                                  start=(kc == 0), stop=(kc == 7))
                        sc.copy(dst[:, nb, c0:c0 + n], ps[:, 0:n], "act")
                if half == 0:
                    ring.preload(("wup", l, 0), wup_ap(l, 0), 8, 512)

        c2 = tile("c2", [128, 8, 2])
        scond = tile("scond", [128, 8, 2], F32R)
        mrow_ref = [None]

        def mod_gen(slabs, idle=0):
            for (l, sl) in slabs:
                slot = ring.load(wmod[l, :, sl * 512:(sl + 1) * 512].rearrange("(k p) n -> p k n", p=128), 8, 512, idx=mod_slot[0])
                mod_inflight[0] = True
                for _ in range(idle):
                    yield
                for kc in range(8):
                    sc.mm(pS[0:2, 0:512], scond[:, kc, :], slot.w(kc, 0, 512), start=(kc == 0), stop=(kc == 7))
                yield
                sc.copy(mrow_ref[0][:], pS[0:2, 0:512], "act")
                yield
                for j4 in range(4):
                    sc.tr(pS[:, j4 * 2:j4 * 2 + 2], mrow_ref[0][0:2, j4 * 128:(j4 + 1) * 128], cst[0:2, C_ID:C_ID + 2])
                c0_ = sl * 8
                sc.tt(modv[:, l, c0_:c0_ + 8], pS[:, 0:8], bmod[:, l, c0_:c0_ + 8], ALU.add)
                j = sl // 2
                if sl % 2 == 1 and j in (1, 4):
                    which = 0 if j == 1 else 1
                    sc.stt(A1[:, l, which, :], modv[:, l, j * 16:(j + 1) * 16], 1.0, gv[:, l, which * 2, :], ALU.add, ALU.mult)
                if sl % 2 == 1 and j in (2, 5):
                    which = 0 if j == 2 else 1
                    sc.tt(GG[:, l, which, :], modv[:, l, j * 16:(j + 1) * 16], gv[:, l, which * 2 + 1, :], ALU.mult)
                mod_inflight[0] = False
                yield

        def take(g, n):
            for _ in range(n):
                try:
                    next(g)
                except StopIteration:
                    return
                yield

        modbg = [None]
        mod_slot = [None]
        mod_inflight = [False]

        def mod_settle():
            while mod_inflight[0] and modbg[0] is not None:
                try:
                    next(modbg[0])
                except StopIteration:
                    modbg[0] = None

        def ffn(l):
            with sc.scope():
                U = [tile("U", [128, NU, 258]) for _ in range(2)]
                ft = [tile("ft", [128, NU, 256]) for _ in range(2)]
                actg = tile("actg", [128, 4, T], F32R)
                cv = tile("cv", [128, 44, 4])
                tmpw = [tile("tmpw", [128, 512]) for _ in range(2)] if FFN_POOL_ACC else None
                wi = [0]
                sc.dma("sp", cv[:], convd[:, l, :, :])
                for u_ in U:
                    sc.memset(u_[:], 0.0)
                for grp in range(6):
                    npair = 4 if grp < 5 else 2
                    for half in range(npair // 2):
                        s = grp * 2 + half
                        slot = ring.load(wup_ap(l, s), 8, 512, key=("wup", l, s))
                        for pp in range(2):
                            j = s * 2 + pp
                            jj = j - grp * 4
                            for bi, coff in ((0, pp * 128), (1, 256 + pp * 128)):
                                Ub = U[bi]
                                for ti, (c0, n) in enumerate(TT):
                                    ps = pF.get()
                                    for kc in range(8):
                                        sc.mm(ps[:, 0:n], slot.w(kc, coff, coff + 128), B[:, kc, c0:c0 + n],
                                              start=(kc == 0), stop=(kc == 7))
                                    u0, nu = c0 // 256, n // 256
                                    sc.copy(Ub[:, u0:u0 + nu, 1:257], ps[:, 0:n].re("p (a b) -> p a b", a=nu), "act")
                                sc.tt(Ub[:, 1:5, 0:1], Ub[:, 0:4, 256:257], chain[:, 0, 1:5].re("p (a b) -> p a b", b=1), ALU.mult)
                                sc.tt(Ub[:, 0:4, 257:258], Ub[:, 1:5, 1:2], chain[:, 1, 0:4].re("p (a b) -> p a b", b=1), ALU.mult)
                                blk = j if bi == 0 else 22 + j
                                t1 = ft[bi]
                                sc.act(t1[:], Ub[:, :, 1:257], AF.Identity, bias=cv[:, blk, 3:4], scale=cv[:, blk, 1:2])
                                sc.stt(t1[:], Ub[:, :, 0:256], cv[:, blk, 0:1], t1[:], ALU.mult, ALU.add)
                                sc.stt(t1[:], Ub[:, :, 2:258], cv[:, blk, 2:3], t1[:], ALU.mult, ALU.add)
                            sc.act(ft[1][:], ft[1][:], AF.Silu)
                            sc.tt(actg[:, jj, :].re("p (a b) -> p a b", a=NU), ft[1][:], ft[0][:], ALU.mult)
                    slotd = ring.load(wdownd[l, grp * 512:grp * 512 + npair * 128, :].rearrange("(k p) n -> p k n", p=128),
                                      npair, 1024)
                    if l == 0 and grp == 5:
                        ring.preload(("w1in", 0), w1in[:, 0:512].rearrange("(k p) n -> p k n", p=128), 8, 512)
                    for nb in range(8):
                        for (c0, n) in TT:
                            ps = pF.get()
                            for kk in range(npair):
                                sc.mm(ps[:, 0:n], slotd.w(kk, nb * 128, nb * 128 + 128), actg[:, kk, c0:c0 + n],
                                      start=(kk == 0), stop=(kk == npair - 1))
                            if grp == 0:
                                sc.copy(A[:, nb, c0:c0 + n], ps[:, 0:n], "act")
                            elif FFN_POOL_ACC:
                                wi[0] += 1
                                tw = tmpw[wi[0] % 2]
                                sc.copy(tw[:, 0:n], ps[:, 0:n], "act")
                                sc.tt(A[:, nb, c0:c0 + n], tw[:, 0:n], Af[:, nb, c0:c0 + n], ALU.add, eng="pool")
                            else:
                                sc.tt(A[:, nb, c0:c0 + n], ps[:, 0:n], Af[:, nb, c0:c0 + n], ALU.add)
                if l == 0:
                    ring.preload(("w1in", 1), w1in[:, 512:768].rearrange("(k p) n -> p k n", p=128), 8, 256)
            if l == 0:
                post_pre(0, 1, Af, 1, 0, A)
            else:
                post_norm(l, 1, Af, final=True)

        def mixer0():
            qk = tile("qk", [128, T], F32R)
            qkf = qk.v(F32)
            ktok = tile("ktok", [128, 10, 64])
            vtok = tile("vtok", [128, 10, 130], F32R)
            gate = tile("gate", [128, 10, 128])
            mixer_lra = [None]
            mixer_cr = [None]
            Sinit = [[tile("Sinit", [64, 129]) for _ in range(2)] for _ in range(2)]
            Sst = [[tile("Sst", [64, 130], F32R) for _ in range(2)] for _ in range(2)]
            mrep = [[tile("mrep", [128, 1]) for _ in range(2)] for _ in range(2)]
            gn = tile("gn", [128, 2, 128])
            bgt = tile("bgt", [128, 16])
            mmi = tile("mmi", [128, 2, NU, 4])
            gsb = tile("gsb", [128, 10, 16])
            lnf = tile("lnf", [128, 10, 8])
            bsb = tile("bsb", [128, 10, 8])
            csb = tile("csb", [128, 10, 8])
            mcol = tile("mcol", [128, 40])
            sm = {}
            si = [0]

            def ofw(chn, c, wr=False):
                return (B if wr else Bf)[:, chn, c * 128:(c + 1) * 128]

            sc.dma("sp", gn[:], gnd)
            sc.dma("sp", bgt[:], bgd)
            sc.dma("sp", mmi[:], mminit)
            def fill(out, src, val):
                sc.act(out, src, AF.Identity, bias=float(val), scale=0.0)

            fill(vtok[:, :, 128:129], x[:, 0, 0:10].re("p (a b) -> p a b", b=1), 1.0)
            fill(vtok[:, :, 129:130], x[:, 0, 0:10].re("p (a b) -> p a b", b=1), 0.0)
            for a_ in Sst:
                for b_ in a_:
                    fill(b_[:], x[0:64, 0, 0:130], 0.0)
            for a_ in mrep:
                for b_ in a_:
                    sc.memset(b_[:], 0.0)

            if stop == "mixsetup":
                raise Stop()

            def w0_used(slot_idx):
                return 480 if slot_idx == 0 else (464 if slot_idx == 4 else 448)

            def w0_ap(slot_idx):
                return w0cat[:, slot_idx * 512:slot_idx * 512 + w0_used(slot_idx)].rearrange("(k p) n -> p k n", p=128)

            def project(slot_idx, is_gla, head):
                mod_settle()
                used = w0_used(slot_idx)
                slot = ring.load(w0_ap(slot_idx), 8, used, idx=slot_idx % 2, key=("w0", slot_idx))
                mod_slot[0] = slot_idx % 2
                qs, ks = (0.125, 1.0) if is_gla else (1.0, 0.125)
                for (c0, n) in TT:
                    ps = pG.get()
                    for kc in range(8):
                        sc.mm(ps[:, 0:n], slot.w(kc, 0, 128), A[:, kc, c0:c0 + n], start=(kc == 0), stop=(kc == 7))
                    sc.act(qk[0:64, c0:c0 + n], ps[0:64, 0:n], AF.Copy, scale=qs)
                    sc.act(qk[64:128, c0:c0 + n], ps[64:128, 0:n], AF.Copy, scale=ks)
                    if slot_idx == 0:
                        ps2 = pG.get()
                        for kc in range(8):
                            sc.mm(ps2[0:32, 0:n], slot.w(kc, 448, 480), A[:, kc, c0:c0 + n], start=(kc == 0), stop=(kc == 7))
                        sc.copy(mixer_lra[0][0:32, c0:c0 + n], ps2[0:32, 0:n], "dve")
                if stop == "projF":
                    raise Stop()
                NT = 336 if (not is_gla and head == 0) else 320
                for tt in range(10):
                    ps = pG.get()
                    for kc in range(8):
                        sc.mm(ps[:, 0:NT], A[:, kc, tt * 128:(tt + 1) * 128], slot.w(kc, 128, 128 + NT),
                              start=(kc == 0), stop=(kc == 7))
                    sc.act(ktok[:, tt, :], ps[:, 0:64], AF.Copy, scale=ks)
                    sc.act(gate[:, tt, :], ps[:, 192:320], AF.Silu if is_gla else AF.Sigmoid)
                    sc.copy(vtok[:, tt, 0:128], ps[:, 64:192], "dve")
                    if NT == 336:
                        sc.tt(gsb[:, tt, :], ps[:, 320:336], bgt[:, :], ALU.add)

            tasks = []

            def interleave(gens):
                tasks[:] = list(gens)
                while tasks:
                    for g in list(tasks):
                        try:
                            next(g)
                        except StopIteration:
                            tasks.remove(g)

            def both(ga, gb, res):
                act = [gb, ga] if BOTH_SWAP else [ga, gb]
                while act:
                    for g in list(act):
                        if g is None:
                            act.remove(g)
                            continue
                        try:
                            next(g)
                        except StopIteration as e_:
                            if g is ga:
                                res["v"] = e_.value
                            act.remove(g)
                    yield

            def warm_gen():
                while True:
                    for _w in range(WARM_MIX):
                        sc.mm(banks[3][:, 0:512], onesR[:, :], A[:, 0, 0:512])
                    yield

            def run_head(chains):
                tasks[:] = list(chains)
                while tasks:
                    for g in list(tasks):
                        try:
                            next(g)
                        except StopIteration:
                            tasks.remove(g)
                    if modbg[0] is not None:
                        try:
                            next(modbg[0])
                        except StopIteration:
                            modbg[0] = None

            def delayed(n, g):
                for _ in range(n):
                    yield
                yield from g

            def out_stage(smt, osum, c, is_gla, head):
                y1 = smt("y1", [128, 128])
                ss = smt("ss", [128, 1])
                sc.act(y1[:], osum[:, 0:128], AF.Square, accum=ss[:])
                lr_ = smt("lr_", [128, 1])
                sc.act(lr_[:], ss[:], AF.Ln, bias=EPS, scale=1.0 / 128)
                rs = smt("rs", [128, 1])
                sc.act(rs[:], lr_[:], AF.Exp, scale=-0.5)
                yield
                sc.stt(y1[:], osum[:, 0:128], rs[:], gn[:, 0 if is_gla else 1, :], ALU.mult, ALU.mult)
                sc.tt(y1[:], y1[:], gate[:, c, :], ALU.mult)
                yield
                pt = pQ.get()
                sc.tr(pt[:, 0:128], y1[:], ident)
                chn = head if is_gla else 4 + head
                sc.copy(B[:, chn, c * 128:(c + 1) * 128], pt[:, 0:128], "act")
                yield

            def chunk_order(d):
                return list(range(10)) if d == 0 else list(range(9, -1, -1))

            def cslice(d, a, b):
                o = a if d == 0 else b
                return cst[:, o:o + 128]

            def mk_smt(prefix):
                def f(name, shape, nbuf=1, par=0, dtype=F32):
                    key = prefix + name
                    if key not in sm:
                        sm[key] = [tile(key, shape, dtype) for _ in range(nbuf)]
                    return sm[key][par % nbuf]
                return f

            def gla_chain(i, d, wg):
                smt = mk_smt("g%d_" % d)
                triG = mixer_cr[0][:, d * 128:(d + 1) * 128]
                maskP = cslice(d, C_MPF, C_MPB)
                order = chunk_order(d)
                st = {"cur": 0}

                def PRE(k):
                    c = order[k]
                    cs = slice(c * 128, (c + 1) * 128)
                    if ((c % 2 == 0) if d == 0 else (c % 2 == 1)):
                        sc.dma("sp", Sinit[d][(c // 2) % 2][:, 0:128], ginit[d, c // 2, i, :, :])
                    pz = pQ.get()
                    sc.mm(pz[:, 0:128], mixer_lra[0][0:33, cs], wg[0:33, d, :])
                    lnv = smt("lnv", [128, 128], dtype=F32R)
                    e1 = lnv.v(F32)
                    sc.act(lnv[:], pz[:, 0:128], AF.Exp, scale=-1.0)
                    yield
                    sc.act(lnv[:], e1[:], AF.Ln, bias=1.0)
                    yield
                    pbT = pQ.get()
                    sc.mm(pbT[:, 0:128], lnv[:, :], triG)
                    eqk = smt("eqk", [128, 128])
                    sc.act(eqk[0:64, :], pbT[0:64, 0:128], AF.Exp)
                    sc.act(eqk[64:128, :], pbT[64:128, 0:128], AF.Exp, scale=-1.0)
                    yield
                    pbt = pQ.get()
                    sc.mm(pbt[:, 0:64], triG, lnv[:, 0:64])
                    ekt = smt("ekt", [128, 64])
                    sc.act(ekt[:], pbt[:, 0:64], AF.Exp, scale=-1.0)
                    yield
                    qt = smt("qt", [64, 128], 2, k, dtype=F32R)
                    sc.tt(qt[:], qkf[0:64, cs], eqk[0:64, :], ALU.mult)
                    kt = smt("kt", [64, 128], dtype=F32R)
                    sc.tt(kt[:], qkf[64:128, cs], eqk[64:128, :], ALU.mult)
                    ktk = smt("ktk", [128, 64], 2, k, dtype=F32R)
                    sc.tt(ktk[:], ktok[:, c, :], ekt[:], ALU.mult)
                    yield
                    psT = pQ.get()
                    sc.mm(psT[:, 0:128], kt[:, :], qt[:, :])
                    P = smt("P", [128, 128], 2, k, dtype=F32R)
                    sc.tt(P[:], psT[:, 0:128], maskP, ALU.mult)
                    yield
                    ecl = smt("ecl", [64, 1], 2, k)
                    sc.copy(ecl[:], eqk[0:64, 127:128] if d == 0 else eqk[0:64, 0:1], "dve")
                    return dict(qt=qt, P=P, ktk=ktk, ecl=ecl)

                def POST(k, pre):
                    c = order[k]
                    u = c // 2
                    first = (c % 2 == 0) if d == 0 else (c % 2 == 1)
                    last = not first
                    cur = st["cur"]
                    if first:
                        Si = Sinit[d][u % 2]
                        sc.stt(Sst[d][1 - cur][:, 0:128], Sst[d][cur].v(F32)[:, 0:128], chain[0:64, d, u:u + 1],
                               Si[:, 0:128], ALU.mult, ALU.add)
                        cur = 1 - cur
                    S = Sst[d][cur]
                    qt, P, ktk, ecl = pre["qt"], pre["P"], pre["ktk"], pre["ecl"]
                    po = pQ.get()
                    sc.mm(po[:, 0:128], qt[:, :], S[:, 0:128], start=True, stop=False)
                    sc.mm(po[:, 0:128], P[:, :], vtok[:, c, 0:128], start=False, stop=True)
                    pdS = pQ.get()
                    sc.mm(pdS[0:64, 0:128], ktk[:, :], vtok[:, c, 0:128])
                    yield
                    Sn = Sst[d][1 - cur]
                    sc.tt(Sn[:, 0:128], pdS[0:64, 0:128], S.v(F32)[:, 0:128], ALU.add)
                    sc.ts(Sn[:, 0:128], Sn.v(F32)[:, 0:128], ecl[:], ALU.mult)
                    st["cur"] = 1 - cur
                    if last:
                        sc.dma("sp", gla_o[d, u, i, :, :], Sn.v(F32)[:, 0:128])
                    yield
                    if k < 5:
                        sc.copy(ofw(i, c, True), po[:, 0:128], "act")
                        yield
                    else:
                        osum = smt("osum", [128, 128], 2, k)
                        sc.tt(osum[:], po[:, 0:128], ofw(i, c), ALU.add)
                        yield
                        tasks.append(delayed(OUT_DELAY, out_stage(smt, osum, c, True, i)))

                cur_pre = yield from PRE(0)
                for k in range(10):
                    res = {}
                    yield from both(PRE(k + 1) if k + 1 < 10 else None, POST(k, cur_pre), res)
                    cur_pre = res.get("v")

            def gla_head(i, wg):
                project(i, True, i)
                ring.preload(("w0", i + 1), w0_ap(i + 1), 8, w0_used(i + 1), idx=(i + 1) % 2)
                sc.dma("pool", wg[:], wgd[:, :, i * 128:(i + 1) * 128])
                run_head([gla_chain(i, 0, wg), gla_chain(i, 1, wg)])

            def mlstm_gates():
                e8 = tile("e8", [128, 10, 8])
                sc.act(e8[:], gsb[:, :, 8:16], AF.Exp, scale=-1.0)
                sc.act(lnf[:], e8[:], AF.Ln, bias=1.0)
                for d in range(2):
                    triM = cslice(d, C_TMF, C_TMB)
                    for tt in range(10):
                        pb = pQ.get()
                        sc.mm(pb[:, 0:4], triM, lnf[:, tt, d * 4:(d + 1) * 4])
                        sc.copy(bsb[:, tt, d * 4:(d + 1) * 4], pb[:, 0:4], "act")
                sc.tt(csb[:], gsb[:, :, 0:8], bsb[:], ALU.subtract)

            def mlstm_chain(i, d):
                smt = mk_smt("m%d_" % d)
                maskadd = cslice(d, C_MAF, C_MAB)
                sel = cslice(d, C_SLF, C_SLB)
                kk = d * 4 + i
                order = chunk_order(d)
                st = {"cur": 0}

                def PRE(k):
                    c = order[k]
                    cs = slice(c * 128, (c + 1) * 128)
                    if ((c % 2 == 0) if d == 0 else (c % 2 == 1)):
                        sc.dma("sp", Sinit[d][(c // 2) % 2][:], minit[d, c // 2, i, :, :])
                    ccol = csb[:, c, kk:kk + 1]
                    bcol = bsb[:, c, kk:kk + 1]
                    kTc = smt("kTc", [64, 128], 2, k, dtype=F32R)
                    sc.copy(kTc[:], qkf[64:128, cs], "act")
                    diagc = smt("diagc", [128, 128])
                    sc.ts(diagc[:], ident, ccol, ALU.mult)
                    yield
                    pcb = pQ.get()
                    sc.mm(pcb[:, 0:128], ones, diagc[:, :])
                    Dm = smt("Dm", [128, 128], 2, k)
                    sc.stt(Dm[:], pcb[:, 0:128], bcol, maskadd, ALU.add, ALU.add)
                    yield
                    rmx = smt("rmx", [128, 1], 2, k)
                    sc.rmax(rmx[:], Dm[:])
                    yield
                    return dict(kTc=kTc, Dm=Dm, rmx=rmx)

                def POST(k, pre):
                    c = order[k]
                    cs = slice(c * 128, (c + 1) * 128)
                    u = c // 2
                    first = (c % 2 == 0) if d == 0 else (c % 2 == 1)
                    last = not first
                    cur = st["cur"]
                    ccol = csb[:, c, kk:kk + 1]
                    bcol = bsb[:, c, kk:kk + 1]
                    kTc, Dm, rmx = pre["kTc"], pre["Dm"], pre["rmx"]
                    if first:
                        Si = Sinit[d][u % 2]
                        sc.stt(Sst[d][1 - cur][:, 0:129], Sst[d][cur].v(F32)[:, 0:129], chain[0:64, d, u:u + 1], Si[:], ALU.mult, ALU.add)
                        sc.stt(mrep[d][1 - cur][:], mrep[d][cur][:], chain[:, d, u:u + 1], mmi[:, d, u, i:i + 1],
                               ALU.mult, ALU.add)
                        cur = 1 - cur
                    Cg = Sst[d][cur]
                    mr = mrep[d][cur]
                    bm = smt("bm", [128, 1])
                    sc.tt(bm[:], bcol, mr[:], ALU.add)
                    mb = smt("mb", [128, 2])
                    mbf = mb
                    sc.tt(mb[:, 0:1], bm[:], rmx[:], ALU.max)
                    sc.copy(mb[:, 1:2], bcol, "dve")
                    yield
                    psl = pQ.get()
                    sc.mm(psl[:, 0:2], sel, mb[:, :])
                    mbn = smt("mbn", [128, 2])
                    sc.copy(mbn[:], psl[:, 0:2], "act")
                    yield
                    dlt = smt("dlt", [128, 1])
                    sc.tt(dlt[:], mbn[:, 1:2], mbn[:, 0:1], ALU.subtract)
                    t2 = smt("t2", [128, 1])
                    sc.tt(t2[:], dlt[:], mr[:], ALU.add)
                    sc.copy(mrep[d][1 - cur][:], mbn[:, 0:1], "dve")
                    nm = smt("nm", [128, 1])
                    sc.ts(nm[:], mbf[:, 0:1], -1.0, ALU.mult)
                    yield
                    srcw = smt("srcw", [128, 1])
                    sc.act(srcw[:], ccol, AF.Exp, bias=dlt[:])
                    cw = smt("cw", [128, 1])
                    sc.act(cw[:], t2[:], AF.Exp)
                    Wi = smt("Wi", [128, 128])
                    sc.act(Wi[:], Dm[:], AF.Exp, bias=nm[:])
                    winter = smt("winter", [128, 1])
                    sc.act(winter[:], bm[:], AF.Exp, bias=nm[:])
                    enm = smt("enm", [128, 1])
                    sc.act(enm[:], nm[:], AF.Exp)
                    yield
                    ksw = smt("ksw", [128, 64], dtype=F32R)
                    sc.ts(ksw[:], ktok[:, c, :], srcw[:], ALU.mult)
                    pit = pG.get()
                    sc.mm(pit[:, 0:130], qk[0:64, cs], Cg[:, 0:130])
                    its = smt("its", [128, 129])
                    sc.act(its[:], pit[:, 0:129], AF.Copy, scale=winter[:])
                    yield
                    pdC = pG.get()
                    sc.mm(pdC[0:64, 0:130], ksw[:, :], vtok[:, c, 0:130])
                    Cn = Sst[d][1 - cur]
                    sc.stt(Cn[:], Cg.v(F32)[:], cw[0:64, :], pdC[0:64, 0:130], ALU.mult, ALU.add)
                    st["cur"] = 1 - cur
                    if last:
                        sc.dma("sp", mC_o[d, u, i, :, :], Cn.v(F32)[:, 0:129])
                        col = (d * NU + u) * 4 + i
                        sc.copy(mcol[:, col:col + 1], mbn[:, 0:1], "dve")
                    yield
                    pqk = pQ.get()
                    sc.mm(pqk[:, 0:128], qk[0:64, cs], kTc[:, :])
                    qkw = smt("qkw", [128, 128])
                    sc.tt(qkw[:], pqk[:, 0:128], Wi[:], ALU.mult)
                    yield
                    pT = pQ.get()
                    sc.tr(pT[:, 0:128], qkw[:], ident)
                    qkT = smt("qkT", [128, 128], dtype=F32R)
                    sc.copy(qkT[:], pT[:, 0:128], "act")
                    yield
                    pin = pG.get()
                    sc.mm(pin[:, 0:130], qkT[:, :], vtok[:, c, 0:130])
                    nd = smt("nd", [128, 129])
                    sc.tt(nd[:], pin[:, 0:129], its[:], ALU.add)
                    yield
                    nden = smt("nden", [128, 1])
                    sc.ts(nden[:], nd[:, 128:129], -1.0, ALU.mult)
                    aden = smt("aden", [128, 1])
                    sc.tt(aden[:], nden[:], nd[:, 128:129], ALU.max)
                    dd = smt("dd", [128, 1])
                    sc.tt(dd[:], aden[:], enm[:], ALU.max)
                    rden = smt("rden", [128, 1])
                    sc.op("dve", lambda e, o=rden[:], i_=dd[:]: e.reciprocal(o.ap, i_.ap), [dd[:]], [rden[:]])
                    yield
                    if k < 5:
                        sc.ts(ofw(4 + i, c, True), nd[:, 0:128], rden[:], ALU.mult)
                        yield
                    else:
                        osum = smt("osum", [128, 128], 2, k)
                        sc.stt(osum[:], nd[:, 0:128], rden[:], ofw(4 + i, c), ALU.mult, ALU.add)
                        yield
                        tasks.append(delayed(OUT_DELAY, out_stage(smt, osum, c, False, i)))

                cur_pre = yield from PRE(0)
                for k in range(10):
                    res = {}
                    yield from both(PRE(k + 1) if k + 1 < 10 else None, POST(k, cur_pre), res)
                    cur_pre = res.get("v")

            def mlstm_head(i):
                project(4 + i, False, i)
                if i < 3:
                    ring.preload(("w0", 5 + i), w0_ap(5 + i), 8, w0_used(5 + i), idx=(5 + i) % 2)
                else:
                    ring.preload(("wout", 0, 0), wout_ap(0, 0), 8, 512, idx=0)
                if i == 0:
                    mlstm_gates()
                run_head([mlstm_chain(i, 0), mlstm_chain(i, 1)])

            with sc.scope():
                lraT = tile("lraT", [33, T], F32R)
                wgt = [tile("wgt", [33, 2, 128], F32R) for _ in range(1)]
                cstR = tile("cstR", [128, 256], F32R)
                sc.copy(cstR[:, 0:128], cst[:, C_TGF:C_TGF + 128], "act")
                sc.copy(cstR[:, 128:256], cst[:, C_TGB:C_TGB + 128], "act")
                mixer_cr[0] = cstR
                fill(lraT[32:33, :], x[32:33, 0, :], 1.0)
                mixer_lra[0] = lraT
                for i in range(4):
                    gla_head(i, wgt[0])
                    if stop == "gla%d" % i:
                        raise Stop()
            with sc.scope():
                for i in range(4):
                    mlstm_head(i)
                    if stop == "mlstm%d" % i:
                        raise Stop()
                sc.dma("sp", mm_o, mcol[0:1, :])

        def mixer1():
            def warm(n_=None):
                for _w in range(WARM_ATT if n_ is None else n_):
                    sc.mm(banks[3][:, 0:512], onesR[:, :], ckvC[:, 0, 0:512])

            gqkv = tile("gqkv", [128, 5])
            rope = tile("rope", [128, T])
            kmx = tile("kmx", [128, 4])
            kmax2 = tile("kmax2", [128, 1])
            krmax = tile("krmax", [128, 1])
            ckvC = tile("ckvC", [128, 2, 512], F32R)
            KRc = tile("KRc", [70, 512], F32R)
            QR = tile("QR", [70, T], F32R)
            sc.dma("sp", gqkv[:], gqkvd)
            sc.dma("sp", rope[:], ropeT)
            sc.dma("pool", KRc[64:70, :], ktab[:, 0:512])
            sc.dma("pool", QR[65:70, :], qtab)
            with sc.scope():
                cstage = tile("cstage", [128, 4, 256])
                kstage = tile("kstage", [128, 4, 64])
                sc.dma("sp", cstage[:], ckvc.rearrange("(a p) n -> p a n", p=128))
                sc.dma("sp", kstage[:], krc.rearrange("(a p) n -> p a n", p=128))
                for a_ in range(4):
                    for rc in range(2):
                        pt = pX.get()
                        sc.tr(pt[:, 0:128], cstage[:, a_, rc * 128:(rc + 1) * 128], ident)
                        sc.copy(ckvC[:, rc, a_ * 128:(a_ + 1) * 128], pt[:, 0:128], "act")
                    pt = pX.get()
                    sc.tr(pt[0:64, 0:128], kstage[:, a_, :], ident)
                    sc.copy(KRc[0:64, a_ * 128:(a_ + 1) * 128], pt[0:64, 0:128], "act")
            with sc.scope():
                qa_t = tile("qa_t", [128, 3, 512])
                kv_t = tile("kv_t", [128, 2, 512])
                kr_t = tile("kr_t", [128, 512])
                kr_s = tile("kr_s", [64, 512])
                kr_u = tile("kr_u", [64, 512])
                stg = tile("stg", [128, 4, 256])
                stg2 = tile("stg2", [128, 4, 64])
                nt = dict(sq=[tile("sq", [128, 512], F32R)], R=[tile("R", [128, 512])], lnt=tile("lnt", [128, 512]), c=[0, 0, 0])
                slot0 = ring.load(w1in[:, 0:512].rearrange("(k p) n -> p k n", p=128), 8, 512, key=("w1in", 0))
                slot1 = ring.load(w1in[:, 512:768].rearrange("(k p) n -> p k n", p=128), 8, 256, key=("w1in", 1))

                def fproj(slot, coff, c0, n):
                    ps = pX.get()
                    for kc in range(8):
                        sc.mm(ps[:, 0:n], slot.w(kc, coff, coff + 128), A[:, kc, c0:c0 + n], start=(kc == 0), stop=(kc == 7))
                    warm(WARM_PROJ)
                    return ps

                for ti, (c0, n) in enumerate(TT):
                    for b_ in range(3):
                        ps = fproj(slot0, b_ * 128, c0, n)
                        sc.copy(qa_t[:, b_, 0:n], ps[:, 0:n], "act")
                    ps = fproj(slot0, 384, c0, n)
                    sc.copy(kv_t[:, 0, 0:n], ps[:, 0:n], "act")
                    ps = fproj(slot1, 0, c0, n)
                    sc.copy(kv_t[:, 1, 0:n], ps[:, 0:n], "act")
                    ps = fproj(slot1, 128, c0, n)
                    sc.copy(kr_u[:, 0:n], ps[0:64, 0:n], "dve")
                    sc.tt(kr_t[:, 0:n], ps[:, 0:n], rope[:, c0:c0 + n], ALU.mult)
                    sc.copy(kr_s[:, 0:n], kr_t[64:128, 0:n], "act")
                    sc.tt(A[0:64, 5, c0:c0 + n], kr_t[0:64, 0:n], kr_s[:, 0:n], ALU.add)
                    sc.dma("pool", A[64:70, 5, c0:c0 + n], ktab[:, 512 + c0:512 + c0 + n])
                    R = compute_R(nt, lambda ch: qa_t[:, ch, 0:n], 3, 384, n)
                    for ch in range(3):
                        sc.stt(A[:, ch, c0:c0 + n], qa_t[:, ch, 0:n], gqkv[:, ch:ch + 1], R[:, 0:n], ALU.mult, ALU.mult)
                    R = compute_R(nt, lambda ch: kv_t[:, ch, 0:n], 2, 256, n)
                    for ch in range(2):
                        sc.stt(kv_t[:, ch, 0:n], kv_t[:, ch, 0:n], gqkv[:, 3 + ch:4 + ch], R[:, 0:n], ALU.mult, ALU.mult)
                        sc.copy(A[:, 3 + ch, c0:c0 + n], kv_t[:, ch, 0:n], "act")
                    na = n // 128
                    for a_ in range(na):
                        pt = pX.get()
                        for ch in range(2):
                            sc.tr(pt[:, ch * 128:(ch + 1) * 128], kv_t[:, ch, a_ * 128:(a_ + 1) * 128], ident)
                        sc.copy(stg[:, a_, :], pt[:, 0:256], "dve")
                        pt2 = pX.get()
                        sc.tr(pt2[:, 0:64], kr_u[:, a_ * 128:(a_ + 1) * 128], cst[0:64, C_ID:C_ID + 64])
                        sc.copy(stg2[:, a_, :], pt2[:, 0:64], "dve")
                    sc.dma("sp", ckv_o[c0:c0 + n, :].rearrange("(a p) n -> p a n", p=128), stg[:, 0:na, :])
                    sc.dma("sp", kr_o[c0:c0 + n, :].rearrange("(a p) n -> p a n", p=128), stg2[:, 0:na, :])
            phase_end("att_proj")
            with sc.scope():
                Kh = tile("Kh", [128, NKEY], F32R)
                Vh = tile("Vh", [128, 14, 128], F32R)
                sqa = tile("sqa", [128, 512], F32R)
                sqb = tile("sqb", [64, 512], F32R)
                sqq = tile("sqq", [128, 512], F32R)
                rdt = tile("rdt", [128, 512])
                lnd = tile("lnd", [128, 512])
                qrt = tile("qrt", [128, 512])
                qrs = tile("qrs", [64, 512])

                def Qn(c0, n):
                    return A[:, 6, c0:c0 + n]

                def PT(i, n):
                    return A[:, 7, i * 512:i * 512 + n]

                def ckv_src(rc, k0, n):
                    return ckvC[:, rc, k0:k0 + n] if k0 < 512 else A[:, 3 + rc, k0 - 512:k0 - 512 + n]

                def KR_src(k0, n, rows=70):
                    return KRc[0:rows, k0:k0 + n] if k0 < 512 else A[0:rows, 5, k0 - 512:k0 - 512 + n]

                KT4 = [(0, 512), (512, 512), (1024, 512), (1536, 256)]
                po_i = [0]
                for ki, (k0, n) in enumerate(KT4):
                    sc.act(sqb[:, 0:n], (KRc.v(F32)[0:64, k0:k0 + n] if k0 < 512 else Af[0:64, 5, k0 - 512:k0 - 512 + n]), AF.Square)
                    sc.mm(pS[:, 0:n], onesR[0:64, :], sqb[:, 0:n])
                    sc.rmax(kmx[:, ki:ki + 1], pS[:, 0:n])
                sc.rmax(krmax[:], kmx[:, 0:4])
                slotkv = ring.load(wkvbd.rearrange("(k p) n -> p k n", p=128), 2, 2048, idx=0)
                slotq = None
                for h in range(8):
                    if h % 4 == 0:
                        slotq = ring.load(wqbd[:, (h // 4) * 1024:(h // 4 + 1) * 1024].rearrange("(k p) n -> p k n", p=128), 3, 1024, idx=1)
                    hh = h % 4
                    def k_chain():
                        for ki, (k0, n) in enumerate(KT4):
                            ps = pX.get()
                            for rc in range(2):
                                sc.mm(ps[:, 0:n], slotkv.w(rc, h * 256, h * 256 + 128), ckv_src(rc, k0, n), start=(rc == 0), stop=(rc == 1))
                            warm()
                            sc.copy(Kh[:, k0:k0 + n], ps[:, 0:n], "act")
                            sc.act(sqa[:, 0:n], ps[:, 0:n], AF.Square)
                            yield
                            sc.mm(pS[:, 0:n], onesR[:, :], sqa[:, 0:n])
                            sc.rmax(kmx[:, ki:ki + 1], pS[:, 0:n])
                            yield
                        sc.rmax(kmax2[:], kmx[:, 0:4])
                        sc.tt(kmax2[:], kmax2[:], krmax[:], ALU.add)
                        yield

                    def v_chain():
                        for kt in range(14):
                            ps = pX.get()
                            for rc in range(2):
                                sc.mm(ps[:, 0:128], ckv_src(rc, kt * 128, 128), slotkv.w(rc, h * 256 + 128, h * 256 + 256),
                                      start=(rc == 0), stop=(rc == 1))
                            sc.copy(Vh[:, kt, :], ps[:, 0:128], "dve")
                            yield

                    def q_chain():
                        for (c0, n) in TT:
                            ps = pX.get()
                            for kc in range(3):
                                sc.mm(ps[:, 0:n], slotq.w(kc, hh * 256, hh * 256 + 128), A[:, kc, c0:c0 + n], start=(kc == 0), stop=(kc == 2))
                            warm()
                            sc.copy(Qn(c0, n), ps[:, 0:n], "act")
                            sc.act(sqq[:, 0:n], ps[:, 0:n], AF.Square)
                            yield
                            ps2 = pX.get()
                            for kc in range(3):
                                sc.mm(ps2[:, 0:n], slotq.w(kc, hh * 256 + 128, hh * 256 + 256), A[:, kc, c0:c0 + n], start=(kc == 0), stop=(kc == 2))
                            warm()
                            sc.tt(qrt[:, 0:n], ps2[:, 0:n], rope[:, c0:c0 + n], ALU.mult)
                            yield
                            sc.copy(qrs[:, 0:n], qrt[64:128, 0:n], "act")
                            sc.tt(QR[0:64, c0:c0 + n], qrt[0:64, 0:n], qrs[:, 0:n], ALU.add)
                            sc.act(sqb[:, 0:n], QR.v(F32)[0:64, c0:c0 + n], AF.Square)
                            yield
                            sc.mm(pD[:, 0:n], onesR[:, :], sqq[:, 0:n], start=True, stop=False)
                            sc.mm(pD[:, 0:n], onesR[0:64, :], sqb[:, 0:n], start=False, stop=True)
                            sc.copy(QR[64:65, c0:c0 + n], pD[64:65, 0:n], "act")
                            yield

                    gens = [k_chain(), v_chain(), q_chain()]
                    while gens:
                        for g in list(gens):
                            try:
                                next(g)
                            except StopIteration:
                                gens.remove(g)
                    if h == 7:
                        ring.preload(("wout", 1, 0), wout_ap(1, 0), 8, 512, idx=1)
                    qrow = QR.v(F32)[64:65, :]
                    sc.act(QR[64:65, :], qrow, AF.Ln, scale=kmax2[64:65, :])
                    sc.act(QR[64:65, :], qrow, AF.Exp, scale=0.5)
                    sc.act(QR[64:65, :], qrow, AF.Copy, scale=-1.001)
                    for (c0, n, kbs) in ((0, 512, list(range(12))), (512, 512, list(range(12))), (1024, 256, [12, 13])):
                        po_i[0] += 1
                        pO = (banks[5], banks[3])[po_i[0] % 2]
                        pend = None
                        for idx, kb in enumerate(kbs):
                            ps = pX.get()
                            sc.mm(ps[:, 0:n], Kh[:, kb * 128:(kb + 1) * 128], Qn(c0, n), start=True, stop=False)
                            sc.mm(ps[:, 0:n], KR_src(kb * 128, 128), QR[0:70, c0:c0 + n], start=False, stop=True)
                            if pend is not None:
                                pidx, pkb, ppt = pend
                                sc.mm(pO[:, 0:n], Vh[:, pkb, :], ppt, start=(pidx == 0), stop=False)
                            pt = PT(idx % 2, n)
                            sc.act(pt, ps[:, 0:n], AF.Exp, scale=ATT_SCALE)
                            ptf = Af[:, 7, (idx % 2) * 512:(idx % 2) * 512 + n]
                            if idx == 0:
                                sc.copy(lnd[:, 0:n], ptf, "dve")
                            else:
                                sc.tt(lnd[:, 0:n], lnd[:, 0:n], ptf, ALU.add)
                            pend = (idx, kb, pt)
                        pidx, pkb, ppt = pend
                        sc.mm(pO[:, 0:n], Vh[:, pkb, :], ppt, start=(pidx == 0), stop=True)
                        sc.mm(pD[:, 0:n], ones, lnd[:, 0:n])
                        sc.act(rdt[:, 0:n], pD[:, 0:n], AF.Ln)
                        sc.act(rdt[:, 0:n], rdt[:, 0:n], AF.Exp, scale=-1.0)
                        sc.tt(B[:, h, c0:c0 + n], pO[:, 0:n], rdt[:, 0:n], ALU.mult)
                    if stop == "att_h%d" % h:
                        raise Stop()

        try:
            sc.dma("sp", cst[:], cstd)
            sc.dma("sp", chain[:], chaind)
            sc.dma("sp", bmod[:], bmodT)
            sc.dma("sp", gv[:], gvecT)
            sc.copy(onesR[:], ones, "act")
            sc.dma("sp", c2[:], cond2T)
            sc.act(scond[:], c2[:], AF.Silu)
            with sc.scope():
                xt = [tile("xt", [128, D]) for _ in range(2)]
                mrow_ref[0] = tile("mrow", [2, 512])
                nt = norm_tiles()
                R3 = [tile("R3", [128, 512]) for _ in range(3)]
                g0 = mod_gen([(0, s_) for s_ in range(4)], idle=2)

                def adv(g, n_=1):
                    for _ in range(n_):
                        try:
                            next(g)
                        except StopIteration:
                            return

                for tt in range(10):
                    t = xt[tt % 2]
                    sc.dma("sp", t[:], xin[tt * 128:(tt + 1) * 128, :])
                    for half in range(2):
                        ps = pG.get()
                        for j in range(4):
                            ch = half * 4 + j
                            sc.tr(ps[:, j * 128:(j + 1) * 128], t[:, ch * 128:(ch + 1) * 128], ident)
                        sc.copy(x[:, half * 4:half * 4 + 4, tt * 128:(tt + 1) * 128],
                                ps[:, 0:512].re("p (a b) -> p a b", a=4), "act" if half == 0 else "dve")
                    adv(g0)
                    if tt in (3, 7, 9):
                        ti = {3: 0, 7: 1, 9: 2}[tt]
                        c0, n = TT[ti]
                        Rr = compute_R(nt, lambda ch, c0=c0, n=n: x[:, ch, c0:c0 + n], 8, D, n, bank=banks[5])
                        sc.copy(R3[ti][:, 0:n], Rr[:, 0:n], "dve")
                adv(g0, 1000)
                ring.preload(("w0", 0), w0cat[:, 0:480].rearrange("(k p) n -> p k n", p=128), 8, 480, idx=0)
                modbg[0] = mod_gen([(0, s_) for s_ in range(4, 12)] + [(1, s_) for s_ in range(12)], idle=MOD_IDLE)
                for ti, (c0, n) in enumerate(TT):
                    cond = 0 if ti < 2 else 1
                    for ch in range(8):
                        nt["c"][2] += 1
                        tm = nt["tmp"][nt["c"][2] % 2]
                        sc.tt(tm[:, 0:n], x[:, ch, c0:c0 + n], R3[ti][:, 0:n], ALU.mult)
                        k = ch * 2 + cond
                        sc.act(A[:, ch, c0:c0 + n], tm[:, 0:n], AF.Identity,
                               bias=shiftv(0, 0, ch, cond), scale=A1[:, 0, 0, k:k + 1])
            dump("x0", x[:], [128, 8, T])
            phase_end("norm0")
            dump("h0", Af[:], [128, 8, T])
            phase_end("norm0")
            with sc.scope():
                mrow_ref[0] = tile("mrow", [2, 512])
                mixer0()
                mod_slot[0] = None
                if modbg[0] is not None:
                    for _ in modbg[0]:
                        pass
            dump("modv", modv[:], [128, 2, 96])
            dump("mixed0", Bf[:], [128, 8, T])
            phase_end("mixer0")
            post_pre(0, 0, Af, 0, 1, B, oproj=(0, B))
            dump("x1", x[:], [128, 8, T])
            phase_end("mix0_done")
            ffn(0)
            dump("x2", x[:], [128, 8, T])
            phase_end("ffn0")
            with sc.scope():
                mixer1()
            dump("attn", Bf[:], [128, 8, T])
            phase_end("mixer1")
            post_pre(1, 0, Af, 1, 1, B, oproj=(1, B))
            dump("x3", x[:], [128, 8, T])
            phase_end("mix1_done")
            ffn(1)
            phase_end("ffn1")
        except Stop:
            pass
        sc.flush()
        dump("endx", x[:], [128, 8, T])
        dump("endA", Af[:], [128, 8, T])
        dump("endB", Bf[:], [128, 8, T])
        if stop is not None and stop != "ffn1":
            sc.flush()
            with sc.scope():
                ys = [tile("ys", [128, D]) for _ in range(2)]
                for tt in range(10):
                    yt_ = ys[tt % 2]
                    for half in range(2):
                        ps = pG.get()
                        for j in range(4):
                            ch = half * 4 + j
                            sc.tr(ps[:, j * 128:(j + 1) * 128], x[:, ch, tt * 128:(tt + 1) * 128], ident)
                        sc.copy(yt_[:, half * 512:(half + 1) * 512], ps[:, 0:512], "act" if half == 0 else "dve")
                    sc.dma("sp", yout[tt * 128:(tt + 1) * 128, :], yt_[:])
        sc.flush(final=True)
        stats = dict(sc.stats)
    return nc, dbg_out, stats


def _core_units(c):
    if c < 6:
        return "prompt", [5 * c + j for j in range(4)], 5 * c + 4
    return "sample", c - 6, 30 + (c - 6)


_SW = np.array([(d + 16) if (d % 32) < 16 else (d - 16) for d in range(64)])


def _rope_table(mode):
    tab = np.zeros((128, T), np.float32)
    tab[0:64, :] = 1.0
    if mode == "sample":
        pos = np.arange(1024)
        row = (pos // 64).astype(np.float32)
        col = (pos % 64).astype(np.float32)
        inv = (1.0 / (np.float32(10000.0) ** (np.arange(0, 32, 2, dtype=np.float32) / np.float32(32)))).astype(np.float32)
        for d in range(64):
            base = row if d < 32 else col
            ang = (base * inv[d % 16]).astype(np.float32)
            tab[d, 0:1024] = np.cos(ang)
            sgn = -1.0 if (d % 32) < 16 else 1.0
            tab[64 + d, 0:1024] = sgn * np.sin(ang)
    return tab


def prep_weights(inp):
    f = lambda a: np.ascontiguousarray(np.asarray(a, dtype=np.float32))
    W = {}
    W["wmod"] = f(np.stack([inp["l0_w_mod"], inp["l1_w_mod"]]))
    bm = np.stack([inp["l0_b_mod"], inp["l1_b_mod"]])
    bmT = bm.reshape(2, 48, 128).transpose(2, 0, 1)
    W["bmodT"] = f(np.repeat(bmT[:, :, :, None], 2, axis=3).reshape(128, 2, 96))
    g = np.stack([np.stack([inp["l0_g_pre_mix"], inp["l0_g_post_mix"], inp["l0_g_pre_ffn"], inp["l0_g_post_ffn"]]),
                  np.stack([inp["l1_g_pre_mix"], inp["l1_g_post_mix"], inp["l1_g_pre_ffn"], inp["l1_g_post_ffn"]])])
    gT = g.reshape(2, 4, 8, 128).transpose(3, 0, 1, 2)
    W["gvecT"] = f(np.repeat(gT[..., None], 2, axis=4).reshape(128, 2, 4, 16))
    w = np.asarray(inp["l0_w_in"], np.float32)
    qa, ka, va, ga, lra = w[:, 0:256], w[:, 256:512], w[:, 512:1024], w[:, 1024:1536], w[:, 1536:1568]
    qb, kb, vb, ob, gts = w[:, 1568:1824], w[:, 1824:2080], w[:, 2080:2592], w[:, 2592:3104], w[:, 3104:3120]
    w0 = np.zeros((D, 4096), np.float32)
    for i in range(4):
        s = i * 512
        w0[:, s:s + 64] = qa[:, i * 64:(i + 1) * 64]
        w0[:, s + 64:s + 128] = ka[:, i * 64:(i + 1) * 64]
        w0[:, s + 128:s + 192] = ka[:, i * 64:(i + 1) * 64]
        w0[:, s + 192:s + 320] = va[:, i * 128:(i + 1) * 128]
        w0[:, s + 320:s + 448] = ga[:, i * 128:(i + 1) * 128]
        s = (4 + i) * 512
        w0[:, s:s + 64] = qb[:, i * 64:(i + 1) * 64]
        w0[:, s + 64:s + 128] = kb[:, i * 64:(i + 1) * 64]
        w0[:, s + 128:s + 192] = kb[:, i * 64:(i + 1) * 64]
        w0[:, s + 192:s + 320] = vb[:, i * 128:(i + 1) * 128]
        w0[:, s + 320:s + 448] = ob[:, i * 128:(i + 1) * 128]
    w0[:, 448:480] = lra
    w0[:, 4 * 512 + 448:4 * 512 + 464] = gts
    W["w0cat"] = f(w0)
    wg = np.zeros((33, 2, 512), np.float32)
    for d, (wn, bn, r0) in enumerate((("l0_gla_w_gate_f", "l0_gla_b_gate_f", 0), ("l0_gla_w_gate_b", "l0_gla_b_gate_b", 16))):
        wgd, bgd = np.asarray(inp[wn], np.float32), np.asarray(inp[bn], np.float32)
        for h in range(4):
            for rep in range(2):
                wg[r0:r0 + 16, d, h * 128 + rep * 64:h * 128 + rep * 64 + 64] = wgd[:, h * 64:(h + 1) * 64]
                wg[32, d, h * 128 + rep * 64:h * 128 + rep * 64 + 64] = bgd[h * 64:(h + 1) * 64]
    W["wg"] = f(wg)
    W["gn"] = f(np.broadcast_to(np.stack([inp["l0_gla_g_norm"], inp["l0_mlstm_g_norm"]])[None], (128, 2, 128)))
    W["bgates"] = f(np.broadcast_to(np.asarray(inp["l0_mlstm_b_gates"])[None], (128, 16)))
    W["wout"] = f(np.stack([inp["l0_w_out"], inp["l1_w_out"]]))
    wups, cvs = [], []
    ffn_in = ((inp["l0_ffn_w_up"], inp["l0_ffn_conv_w"], inp["l0_ffn_conv_b"]),
              (inp["l1_ffn_w_up"], inp["l1_ffn_conv_w"], inp["l1_ffn_conv_b"]))
    for l in range(2):
        wu = np.asarray(ffn_in[l][0], np.float32)
        cols = []
        for s in range(11):
            cols.append(wu[:, s * 256:(s + 1) * 256])
            cols.append(wu[:, 2816 + s * 256:2816 + (s + 1) * 256])
        wups.append(np.concatenate(cols, axis=1))
        cw = np.asarray(ffn_in[l][1], np.float32)
        cb = np.asarray(ffn_in[l][2], np.float32)
        cc = np.concatenate([cw, cb[None]], axis=0)
        cvs.append(cc.reshape(4, 44, 128).transpose(2, 1, 0))
    W["wup"] = f(np.stack(wups))
    W["convT"] = f(np.stack(cvs, axis=1))
    W["wdown"] = f(np.stack([inp["l0_ffn_w_down"], inp["l1_ffn_w_down"]]))
    w1 = np.asarray(inp["l1_w_in"], np.float32)
    W["w1in"] = f(np.concatenate([w1[:, 0:704], w1[:, 640:704][:, _SW]], axis=1))
    wq = np.asarray(inp["l1_w_qb"], np.float32)
    qcols = []
    for h in range(8):
        qcols += [wq[:, h * 192:h * 192 + 128], wq[:, h * 192 + 128:h * 192 + 192], wq[:, h * 192 + 128:h * 192 + 192][:, _SW]]
    W["wqb"] = f(np.concatenate(qcols, axis=1))
    W["wkvb"] = f(inp["l1_w_kvb"])
    gq = np.asarray(inp["l1_g_q_norm"], np.float32).reshape(3, 128).T
    gkv = np.asarray(inp["l1_g_kv_norm"], np.float32).reshape(2, 128).T
    W["gqkv"] = f(np.concatenate([gq, gkv], axis=1))
    W["cst"] = make_consts()
    return W


def prep_core(inp, c):
    f = lambda a: np.ascontiguousarray(np.asarray(a, dtype=np.float32))
    mode, grp, sa = _core_units(c)
    xp, xs = np.asarray(inp["x_prompt"]), np.asarray(inp["x_sample"])
    m = {}
    if mode == "prompt":
        xg = np.concatenate([xp[s] for s in grp], axis=0)
        cond0 = np.asarray(inp["c_ctx"])
    else:
        xg = xs[grp]
        cond0 = np.asarray(inp["c"])[grp]
    m["xin"] = f(np.concatenate([xg, xp[sa]], axis=0))
    cond2 = np.stack([cond0, np.asarray(inp["c_ctx"])])
    m["cond2T"] = f(cond2.reshape(2, 8, 128).transpose(2, 1, 0))
    chain = np.zeros((128, 2, NU), np.float32)
    ginit = np.zeros((2, NU, 4, 64, 128), np.float32)
    minit = np.zeros((2, NU, 4, 64, 129), np.float32)
    mminit = np.zeros((128, 2, NU, 4), np.float32)
    ktab = np.zeros((6, NKEY), np.float32)
    qtab = np.zeros((5, T), np.float32)
    ckvc = np.zeros((512, 256), np.float32)
    krc = np.zeros((512, 64), np.float32)
    ktab[0, :] = 1.0
    for j in range(4):
        ktab[1 + j, 512 + 256 * j:512 + 256 * (j + 1)] = 1.0
    ktab[5, 0:512] = 1.0
    if mode == "sample":
        b = grp
        chain[:, 0, 1:4] = 1.0
        chain[:, 1, 0:3] = 1.0
        ginit[0, 0] = inp["state_l0_gla_fwd"][b]
        ginit[1, 3] = inp["state_l0_gla_bwd"][b]
        minit[0, 0, :, :, 0:128] = inp["state_l0_mlstm_c_fwd"][b]
        minit[0, 0, :, :, 128] = inp["state_l0_mlstm_n_fwd"][b]
        minit[1, 3, :, :, 0:128] = inp["state_l0_mlstm_c_bwd"][b]
        minit[1, 3, :, :, 128] = inp["state_l0_mlstm_n_bwd"][b]
        mminit[:, 0, 0, :] = np.asarray(inp["state_l0_mlstm_m_fwd"])[b][None, :]
        mminit[:, 1, 3, :] = np.asarray(inp["state_l0_mlstm_m_bwd"])[b][None, :]
        ckvc = inp["cache_l1_ckv"][b]
        krc = inp["cache_l1_krope"][b]
    else:
        for u in range(4):
            for j in range(4):
                if j != u:
                    qtab[j, 256 * u:256 * (u + 1)] = NEG
        qtab[4, 0:1024] = NEG
    m["chain"], m["ginit"], m["minit"], m["mminit"] = chain, ginit, minit, mminit
    m["ktab"], m["qtab"], m["ckvc"], m["krc"] = ktab, qtab, f(ckvc), f(krc)
    m["ropeT"] = _rope_table(mode)
    return m


def assemble(results):
    yp = np.zeros((32, 256, D), np.float32)
    ysm = np.zeros((2, 1024, D), np.float32)
    gla = np.zeros((2, 32, 4, 64, 128), np.float32)
    mC = np.zeros((2, 32, 4, 64, 128), np.float32)
    mn = np.zeros((2, 32, 4, 64), np.float32)
    mm = np.zeros((2, 32, 4), np.float32)
    ckv = np.zeros((32, 256, 256), np.float32)
    kr = np.zeros((32, 256, 64), np.float32)
    for c, r in enumerate(results):
        mode, grp, sa = _core_units(c)
        units = [(4, sa)]
        if mode == "prompt":
            units += [(j, grp[j]) for j in range(4)]
        else:
            ysm[grp] = r["y"][0:1024]
        mmo = r["mm_o"].reshape(2, NU, 4)
        for u, s in units:
            yp[s] = r["y"][u * 256:(u + 1) * 256]
            ckv[s] = r["ckv_o"][u * 256:(u + 1) * 256]
            kr[s] = r["kr_o"][u * 256:(u + 1) * 256]
            for d in range(2):
                gla[d, s] = r["gla_o"][d, u]
                mC[d, s] = r["mC_o"][d, u, :, :, 0:128]
                mn[d, s] = r["mC_o"][d, u, :, :, 128]
                mm[d, s] = mmo[d, u]
    return (yp, ysm, gla[0], gla[1], mC[0], mn[0], mm[0], mC[1], mn[1], mm[1], ckv, kr)


_PROG = {}


def kernel(**inputs):
    if "nc" not in _PROG:
        _PROG["nc"] = build_program()[0]
    nc = _PROG["nc"]
    W = prep_weights(inputs)
    in_maps = []
    for c in range(8):
        m = dict(W)
        m.update(prep_core(inputs, c))
        in_maps.append(m)
    res = run_bass_kernel_spmd(nc, in_maps, core_ids=list(range(8)))
    return assemble(res.results)
```

```python
import contextlib
import numpy as np
import concourse.bass as bass
import concourse.mybir as mybir
from concourse.bass_utils import run_bass_kernel_spmd
from concourse.alu_op_type import AluOpType as ALU

F32 = mybir.dt.float32
F32R = mybir.dt.float32r
AF = mybir.ActivationFunctionType
AX = mybir.AxisListType

SAME_ENGINE_SYNC = True
class Tile:
    def __init__(self, sc, name, shape, dtype=F32, space="sb"):
        self.name, self.shape, self.dtype, self.space = name, list(shape), dtype, space
        alloc = sc.nc.sbuf_tensor if space == "sb" else sc.nc.psum_tensor
        self.h = sc.es.enter_context(alloc(name, list(shape), dtype))
        self.wr = {}
        self.rd = {}

    def __getitem__(self, idx):
        return Ref(self, idx)

    def v(self, dtype):
        return _View(self, dtype)


class _View:
    def __init__(self, tile, dtype):
        self.tile, self.dtype = tile, dtype

    def __getitem__(self, idx):
        return Ref(self.tile, idx, self.dtype)


class Ref:
    def __init__(self, tile, idx, dtype=None):
        if not isinstance(idx, tuple):
            idx = (idx,)
        idx = idx + (slice(None),) * (len(tile.shape) - len(idx))
        self.tile = tile
        ap = tile.h[idx]
        if dtype is not None and dtype != tile.dtype:
            ap = ap.bitcast(dtype)
        self.ap = ap
        box = []
        for i, n in zip(idx, tile.shape):
            if isinstance(i, int):
                box.append((i, i + 1))
            else:
                a, b, st = i.indices(n)
                box.append((a, b))
        self.box = tuple(box)


def _overlap(a, b):
    return all(x[0] < y[1] and y[0] < x[1] for x, y in zip(a, b))


def _contains(a, b):
    return all(x[0] <= y[0] and x[1] >= y[1] for x, y in zip(a, b))


class Op:
    __slots__ = ("eng", "fn", "waits", "is_dma", "sem", "target", "signal", "rank", "seq", "clock", "selfwait")


class Sched:
    def __init__(self, nc, es, n_dma_sems=20):
        self.nc = nc
        self.stacks = [es]
        self.E = {"pe": nc.tensor, "act": nc.scalar, "dve": nc.vector, "pool": nc.gpsimd, "sp": nc.sync}
        self.ops = []
        self.seq = {e: 0 for e in self.E}
        self.clock = {e: {} for e in self.E}
        self.esem = {e: es.enter_context(nc.semaphore("es_" + e)) for e in ("pe", "act", "dve", "pool")}
        self.dsem = {}
        for q in ("sp", "pool"):
            self.dsem[q] = [[es.enter_context(nc.semaphore("ds_%s_%d" % (q, i))), 0] for i in range(n_dma_sems)]
        self.dnext = {q: 0 for q in self.dsem}
        self.ntile = 0
        self.emitted = 0
        self.dma_barrier = 0
        self.cnt = {e: 0 for e in self.E}
        self.waited = {}
        self.stats = dict(nops=0, nwait=0)

    @property
    def es(self):
        return self.stacks[-1]

    def tile(self, name, shape, dtype=F32, space="sb"):
        self.ntile += 1
        return Tile(self, "%s_%d" % (name, self.ntile), shape, dtype, space)

    @contextlib.contextmanager
    def scope(self):
        sub = contextlib.ExitStack()
        self.stacks.append(sub)
        try:
            yield sub
        finally:
            self.flush()
            self.stacks.pop()
            sub.close()

    def carve(self, parent, name, off, shape, dtype=F32):
        t = Tile.__new__(Tile)
        t.name, t.shape, t.dtype, t.space = name, list(shape), dtype, parent.space
        n = 1
        for d_ in shape[1:]:
            n *= d_
        names = "abcdefg"[:len(parent.shape) - 1]
        flat = parent.h[:].rearrange("p %s -> p (%s)" % (" ".join(names), " ".join(names)))
        ap = flat[0:shape[0], off:off + n]
        if dtype != parent.dtype:
            ap = ap.bitcast(dtype)
        if len(shape) > 2:
            nm = "abcdefg"[:len(shape) - 1]
            kw = {nm[i]: shape[1 + i] for i in range(len(shape) - 2)}
            ap = ap.rearrange("p (%s) -> p %s" % (" ".join(nm), " ".join(nm)), **kw)
        t.h = ap
        t.wr, t.rd = {}, {}
        return t

    def _record(self, eng, fn, reads, writes, is_dma=False):
        op = Op()
        op.eng, op.fn, op.is_dma = eng, fn, is_dma
        op.signal, op.rank, op.sem, op.target, op.selfwait = False, 0, None, 0, None
        oid = len(self.ops)
        deps = set()
        ps_seen = {}
        for r, isw in [(r, False) for r in reads] + [(w, True) for w in writes]:
            if isinstance(r, Ref) and r.tile.space == "ps":
                ps_seen[id(r.tile)] = (r.tile, ps_seen.get(id(r.tile), (None, False))[1] or isw)
        for t, isw in ps_seen.values():
            acc = t.__dict__.get("acc", {})
            for f, (o, w) in acc.items():
                if f != eng or w or isw:
                    deps.add(o)
            t.acc = {eng: (oid, isw)}
        reads = [r for r in reads if isinstance(r, Ref) and r.tile.space != "ps"]
        writes = [w for w in writes if isinstance(w, Ref) and w.tile.space != "ps"]
        for r in reads:
            if not isinstance(r, Ref):
                continue
            for box, w in r.tile.wr.items():
                if _overlap(box, r.box):
                    deps.add(w)
        for w in writes:
            if not isinstance(w, Ref):
                continue
            for box, o in w.tile.wr.items():
                if _overlap(box, w.box):
                    deps.add(o)
            for (box, _e), o in w.tile.rd.items():
                if _overlap(box, w.box):
                    deps.add(o)
        for w in writes:
            if not isinstance(w, Ref):
                continue
            t = w.tile
            t.wr = {b: o for b, o in t.wr.items() if not _contains(w.box, b)}
            t.rd = {k: o for k, o in t.rd.items() if not _contains(w.box, k[0])}
            t.wr[w.box] = oid
        for r in reads:
            if not isinstance(r, Ref):
                continue
            r.tile.rd[(r.box, eng if not is_dma else ("dma", oid))] = oid
        clk = self.clock[eng]
        waits = []
        for d in sorted(deps):
            p = self.ops[d]
            if p.is_dma:
                if d < self.dma_barrier:
                    continue
                key = ("dma", d)
                if clk.get(key, -1) >= 0:
                    continue
                waits.append(d)
                clk[key] = 0
            else:
                if p.eng == eng and (eng == "pe" or not SAME_ENGINE_SYNC):
                    continue
                if clk.get(p.eng, -1) >= p.seq:
                    continue
                waits.append(d)
                if clk.get(p.eng, -1) < p.seq:
                    clk[p.eng] = p.seq
            for k, v in p.clock.items():
                if clk.get(k, -1) < v:
                    clk[k] = v
        op.waits = waits
        op.seq = self.seq[eng]
        self.seq[eng] += 1
        op.clock = dict(clk)
        if not is_dma:
            op.clock[eng] = op.seq
        self.ops.append(op)
        return op

    def op(self, eng, fn, reads=(), writes=()):
        return self._record(eng, fn, list(reads), list(writes))

    def dma(self, q, out, in_, **kw):
        oap = out.ap if isinstance(out, Ref) else out
        iap = in_.ap if isinstance(in_, Ref) else in_
        op = self._record(q, lambda e: e.dma_start(out=oap, in_=iap, **kw),
                          [in_] if isinstance(in_, Ref) else [], [out] if isinstance(out, Ref) else [], is_dma=True)
        i = self.dnext[q]
        self.dnext[q] = (i + 1) % len(self.dsem[q])
        ent = self.dsem[q][i]
        if ent[1] > 0:
            op.selfwait = (ent[0], ent[1])
        ent[1] += 16
        op.sem, op.target = ent[0], ent[1]
        return op

    def _wait(self, engname, key, val):
        wk = (engname, id(key))
        if self.waited.get(wk, -1) >= val:
            return
        self.waited[wk] = val
        self.E[engname].wait_ge(key, val)
        self.stats["nwait"] += 1

    def flush(self, final=False):
        pend = self.ops[self.emitted:]
        last = {}
        for op in pend:
            for d in op.waits:
                p = self.ops[d]
                if not p.is_dma:
                    assert d >= self.emitted, "dependency on pre-barrier op"
                    p.signal = True
            if not op.is_dma:
                last[op.eng] = op
        for op in last.values():
            op.signal = True
        for op in pend:
            if op.signal:
                self.cnt[op.eng] += 1
                op.rank = self.cnt[op.eng]
        for op in pend:
            eng = self.E[op.eng]
            need = {}
            for d in op.waits:
                p = self.ops[d]
                key, val = (p.sem, p.target) if p.is_dma else (self.esem[p.eng], p.rank)
                if need.get(id(key), (None, -1))[1] < val:
                    need[id(key)] = (key, val)
            if op.selfwait is not None:
                key, val = op.selfwait
                if need.get(id(key), (None, -1))[1] < val:
                    need[id(key)] = (key, val)
            for key, val in need.values():
                self._wait(op.eng, key, val)
            ins = op.fn(eng)
            if op.is_dma:
                ins.then_inc(op.sem, 16)
            elif op.signal:
                ins.then_inc(self.esem[op.eng], 1)
            op.fn = None
        self.stats["nops"] += len(pend)
        self.emitted = len(self.ops)
        engs = ["sp"] if final else ["pe", "act", "dve", "sp", "pool"]
        for e in engs:
            for f in ("pe", "act", "dve", "pool"):
                if self.cnt[f] > 0:
                    self._wait(e, self.esem[f], self.cnt[f])
            for q in self.dsem:
                for sem, tgt in self.dsem[q]:
                    if tgt > 0:
                        self._wait(e, sem, tgt)
        for e in self.E:
            for f in ("pe", "act", "dve", "pool"):
                self.clock[e][f] = self.seq[f] - 1
        self.dma_barrier = len(self.ops)

    def mm(self, out, lhsT, rhs, start=True, stop=True):
        return self.op("pe", lambda e: e.matmul(out.ap, lhsT.ap, rhs.ap, start=start, stop=stop), [lhsT, rhs], [out])

    def tr(self, out, in_, ident):
        return self.op("pe", lambda e: e.transpose(out.ap, in_.ap, ident.ap), [in_, ident], [out])

    def act(self, out, in_, func, bias=None, scale=None, accum=None, eng="act"):
        kw = {}
        rd = [in_]
        wr = [out]
        if bias is not None:
            kw["bias"] = bias.ap if isinstance(bias, Ref) else bias
            if isinstance(bias, Ref):
                rd.append(bias)
        if scale is not None:
            kw["scale"] = scale.ap if isinstance(scale, Ref) else scale
            if isinstance(scale, Ref):
                rd.append(scale)
        if accum is not None:
            kw["accum_out"] = accum.ap
            wr.append(accum)
        return self.op(eng, lambda e: e.activation(out=out.ap, in_=in_.ap, func=func, **kw), rd, wr)

    def tt(self, out, a, b, op, eng="dve"):
        return self.op(eng, lambda e: e.tensor_tensor(out=out.ap, in0=a.ap, in1=b.ap, op=op), [a, b], [out])

    def ts(self, out, a, s1, op0, s2=None, op1=None, eng="dve", accum=None):
        rd = [a]
        wr = [out]
        v1 = s1.ap if isinstance(s1, Ref) else s1
        v2 = s2.ap if isinstance(s2, Ref) else s2
        if isinstance(s1, Ref):
            rd.append(s1)
        if isinstance(s2, Ref):
            rd.append(s2)
        kw = {}
        if op1 is not None:
            kw["op1"] = op1
        if accum is not None:
            kw["accum_out"] = accum.ap
            wr.append(accum)
        return self.op(eng, lambda e: e.tensor_scalar(out=out.ap, in0=a.ap, scalar1=v1, scalar2=v2, op0=op0, **kw), rd, wr)

    def stt(self, out, a, scalar, b, op0, op1):
        rd = [a, b]
        v = scalar.ap if isinstance(scalar, Ref) else scalar
        if isinstance(scalar, Ref):
            rd.append(scalar)
        return self.op("dve", lambda e: e.scalar_tensor_tensor(out=out.ap, in0=a.ap, scalar=v, in1=b.ap, op0=op0, op1=op1), rd, [out])

    def copy(self, out, in_, eng="act"):
        if eng == "act":
            return self.op("act", lambda e: e.copy(out.ap, in_.ap), [in_], [out])
        return self.op(eng, lambda e: e.tensor_copy(out.ap, in_.ap), [in_], [out])

    def rmax(self, out, in_):
        return self.op("dve", lambda e: e.reduce_max(out.ap, in_.ap, axis=AX.X), [in_], [out])

    def rsum(self, out, in_):
        return self.op("dve", lambda e: e.reduce_sum(out.ap, in_.ap, axis=AX.X), [in_], [out])

    def memset(self, out, val, eng="dve"):
        return self.op(eng, lambda e: e.memset(out.ap, val), [], [out])


def _reref(ref, pattern, **kw):
    r = Ref.__new__(Ref)
    r.tile, r.box = ref.tile, ref.box
    r.ap = ref.ap.rearrange(pattern, **kw)
    return r


Ref.re = _reref

D = 1024
T = 1280
NU = 5
EPS = 1e-6
TT = [(0, 512), (512, 512), (1024, 256)]
NKEY = 1792
SLOTW = 4096
ATT_SCALE = 192.0 ** -0.5
NEG = -30000.0
WARM_ATT = 0
WARM_PROJ = 0
BOTH_SWAP = 0
FFN_POOL_ACC = 0
OUT_DELAY = 6
MOD_IDLE = 15
WARM_MIX = 0

C_ID, C_ONE, C_TGF, C_TGB, C_TMF, C_TMB, C_MPF, C_MPB, C_MAF, C_MAB, C_SLF, C_SLB = [i * 128 for i in range(12)]
NCST = 12 * 128


def make_consts():
    c = np.zeros((128, NCST), np.float32)
    s = np.arange(128)[:, None]
    t = np.arange(128)[None, :]
    le = (s <= t).astype(np.float32)
    ge = (s >= t).astype(np.float32)
    c[:, C_ID:C_ID + 128] = np.eye(128, dtype=np.float32)
    c[:, C_ONE:C_ONE + 128] = 1.0
    c[:, C_TGF:C_TGF + 128] = -le / 16.0
    c[:, C_TGB:C_TGB + 128] = -ge / 16.0
    c[:, C_TMF:C_TMF + 128] = -le
    c[:, C_TMB:C_TMB + 128] = -ge
    c[:, C_MPF:C_MPF + 128] = le
    c[:, C_MPB:C_MPB + 128] = ge
    c[:, C_MAF:C_MAF + 128] = np.where(s >= t, 0.0, -1e30)
    c[:, C_MAB:C_MAB + 128] = np.where(s <= t, 0.0, -1e30)
    c[127, C_SLF:C_SLF + 128] = 1.0
    c[0, C_SLB:C_SLB + 128] = 1.0
    return c


class Ring:
    def __init__(self, sc, nslot):
        self.sc = sc
        self.slots = [sc.tile("ring", [128, SLOTW], F32R) for _ in range(nslot)]
        self.i = 0
        self.pre = {}

    def load(self, dram3d, nk, W, q="pool", idx=None, key=None):
        if key is not None and key in self.pre:
            return self.pre.pop(key)
        reserved = {id(s_.t) for s_ in self.pre.values()}
        if idx is None:
            for _ in range(len(self.slots)):
                idx = self.i
                self.i = (self.i + 1) % len(self.slots)
                if id(self.slots[idx]) not in reserved:
                    break
        assert id(self.slots[idx]) not in reserved, "ring slot holds preloaded weights that were not consumed yet"
        t = self.slots[idx]
        dst = t[:, 0:nk * W].re("p (k n) -> p k n", k=nk)
        self.sc.dma(q, dst, dram3d)
        s_ = _Slot(t, nk, W)
        s_.idx = idx
        return s_

    def preload(self, key, dram3d, nk, W, idx=None):
        self.pre[key] = self.load(dram3d, nk, W, idx=idx)


class _Slot:
    def __init__(self, t, nk, W):
        self.t, self.nk, self.W = t, nk, W

    def w(self, kc, a, b, p0=0, p1=128):
        return self.t[p0:p1, kc * self.W + a: kc * self.W + b]


class PsPool:
    def __init__(self, banks, width):
        self.banks, self.width = banks, width
        self.slots = [(b, o) for b in banks for o in range(0, 512, width)]
        self.i = 0

    def get(self):
        b, o = self.slots[self.i]
        self.i = (self.i + 1) % len(self.slots)
        return _PsView(b, o)


class _PsView:
    def __init__(self, bank, off):
        self.bank, self.off = bank, off

    def __getitem__(self, idx):
        p, c = idx
        a, b, _ = c.indices(512 - self.off)
        return self.bank[p, self.off + a:self.off + b]


def build_program(stop=None, dumps=(), nslot=2):
    nc = bass.Bass("TRN2", target_bir_lowering=False)

    def din(name, shape):
        return nc.dram_tensor(name, list(shape), F32, kind="ExternalInput").ap()

    def dout(name, shape):
        return nc.dram_tensor(name, list(shape), F32, kind="ExternalOutput").ap()

    xin = din("xin", [T, D])
    cond2T = din("cond2T", [128, 8, 2])
    cstd = din("cst", [128, NCST])
    chaind = din("chain", [128, 2, NU])
    ginit = din("ginit", [2, NU, 4, 64, 128])
    minit = din("minit", [2, NU, 4, 64, 129])
    mminit = din("mminit", [128, 2, NU, 4])
    ropeT = din("ropeT", [128, T])
    ktab = din("ktab", [6, NKEY])
    qtab = din("qtab", [5, T])
    ckvc = din("ckvc", [512, 256])
    krc = din("krc", [512, 64])
    wmod = din("wmod", [2, D, 6144])
    bmodT = din("bmodT", [128, 2, 96])
    gvecT = din("gvecT", [128, 2, 4, 16])
    w0cat = din("w0cat", [D, 4096])
    wgd = din("wg", [33, 2, 512])
    gnd = din("gn", [128, 2, 128])
    bgd = din("bgates", [128, 16])
    woutd = din("wout", [2, D, D])
    wupd = din("wup", [2, D, 5632])
    convd = din("convT", [128, 2, 44, 4])
    wdownd = din("wdown", [2, 2816, D])
    w1in = din("w1in", [D, 768])
    wqbd = din("wqb", [384, 2048])
    wkvbd = din("wkvb", [256, 2048])
    gqkvd = din("gqkv", [128, 5])
    yout = dout("y", [T, D])
    gla_o = dout("gla_o", [2, NU, 4, 64, 128])
    mC_o = dout("mC_o", [2, NU, 4, 64, 129])
    mm_o = dout("mm_o", [1, 40])
    ckv_o = dout("ckv_o", [T, 256])
    kr_o = dout("kr_o", [T, 64])
    dbg_out = {}

    class Stop(Exception):
        pass

    es = contextlib.ExitStack()
    with es:
        sc = Sched(nc, es)
        tile = sc.tile
        x = tile("x", [128, 8, T])
        A = tile("A", [128, 8, T], F32R)
        B = tile("B", [128, 8, T], F32R)
        Af, Bf = A.v(F32), B.v(F32)
        ring = Ring(sc, nslot)
        cst = tile("cst", [128, NCST])
        onesR = tile("onesR", [128, 128], F32R)
        chain = tile("chain", [128, 2, NU])
        modv = tile("modv", [128, 2, 96])
        bmod = tile("bmod", [128, 2, 96])
        gv = tile("gv", [128, 2, 4, 16])
        A1 = tile("A1", [128, 2, 2, 16])
        GG = tile("GG", [128, 2, 2, 16])
        banks = [tile("ps", [128, 512], F32, "ps") for _ in range(8)]

        ident = cst[:, C_ID:C_ID + 128]
        ones = cst[:, C_ONE:C_ONE + 128]

        pG = PsPool(banks[0:4], 512)
        pS = banks[4]
        pQ = PsPool(banks[5:8] + banks[0:4], 512)
        pO, pD = banks[5], banks[6]
        pX = PsPool([banks[7]] + banks[0:3], 512)
        pF = PsPool(banks[0:4] + banks[5:8], 512)

        def dump(name, ref, shape):
            if name in dumps:
                o = dout("dbg_" + name, shape)
                sc.dma("sp", o, ref)
                dbg_out[name] = shape

        def phase_end(name):
            if stop == name:
                raise Stop()

        def norm_tiles():
            return dict(sq=[tile("sq", [128, 512], F32R) for _ in range(2)], R=[tile("R", [128, 512]) for _ in range(2)],
                        lnt=tile("lnt", [128, 512]), tmp=[tile("tmp512", [128, 512]) for _ in range(2)], c=[0, 0, 0])

        def compute_R(nt, src_fn, nch, dim, n, bank=None):
            bank = pS if bank is None else bank
            for ch in range(nch):
                nt["c"][0] += 1
                sq = nt["sq"][nt["c"][0] % len(nt["sq"])]
                sc.act(sq[:, 0:n], src_fn(ch), AF.Square)
                sc.mm(bank[:, 0:n], onesR[:, :], sq[:, 0:n], start=(ch == 0), stop=(ch == nch - 1))
            sc.act(nt["lnt"][:, 0:n], bank[:, 0:n], AF.Ln, bias=EPS, scale=1.0 / dim)
            nt["c"][1] += 1
            R = nt["R"][nt["c"][1] % len(nt["R"])]
            sc.act(R[:, 0:n], nt["lnt"][:, 0:n], AF.Exp, scale=-0.5)
            return R

        def shiftv(l, which, ch, cond):
            c = (which * 3) * 16 + ch * 2 + cond
            return modv[:, l, c:c + 1]

        def norm_mod(l, which, dst):
            with sc.scope():
                nt = norm_tiles()
                for ti, (c0, n) in enumerate(TT):
                    cond = 0 if ti < 2 else 1
                    R = compute_R(nt, lambda ch: x[:, ch, c0:c0 + n], 8, D, n)
                    for ch in range(8):
                        nt["c"][2] += 1
                        tm = nt["tmp"][nt["c"][2] % 2]
                        sc.tt(tm[:, 0:n], x[:, ch, c0:c0 + n], R[:, 0:n], ALU.mult)
                        k = ch * 2 + cond
                        sc.act(dst[:, ch, c0:c0 + n], tm[:, 0:n], AF.Identity,
                               bias=shiftv(l, which, ch, cond), scale=A1[:, l, which, k:k + 1])

        def post_pre(lp, wp, src, ln_, wn, dst, oproj=None):
            with sc.scope():
                sqs = [tile("sq", [128, 512], F32R) for _ in range(3)]
                tms = [tile("tmp512", [128, 512]) for _ in range(3)]
                Ra = [tile("Ra", [128, 512]) for _ in range(3)]
                Rb = [tile("Rb", [128, 512]) for _ in range(3)]
                lnts = [tile("lnt", [128, 512]) for _ in range(3)]

                if oproj is not None:
                    ol, osrc = oproj
                    s0_ = ring.load(wout_ap(ol, 0), 8, 512, key=("wout", ol, 0))
                    slots_ = [s0_, ring.load(wout_ap(ol, 1), 8, 512, idx=1 - s0_.idx)]
                    pend = []

                    def stats_mm(item):
                        nb_, ti_, n_ = item
                        sc.mm(banks[5 + ti_][:, 0:n_], onesR[:, :], sqs[ti_][:, 0:n_], start=(nb_ == 0), stop=(nb_ == 7))

                    for half in range(2):
                        slot = slots_[half]
                        for nb4 in range(4):
                            nb = half * 4 + nb4
                            for ti, (c0, n) in enumerate(TT):
                                ps = pG.get()
                                for kc in range(8):
                                    sc.mm(ps[:, 0:n], slot.w(kc, nb4 * 128, nb4 * 128 + 128), osrc[:, kc, c0:c0 + n],
                                          start=(kc == 0), stop=(kc == 7))
                                while len(pend) > 2:
                                    stats_mm(pend.pop(0))
                                sc.copy(A[:, nb, c0:c0 + n], ps[:, 0:n], "act")
                                sc.act(sqs[ti][:, 0:n], ps[:, 0:n], AF.Square)
                                pend.append((nb, ti, n))
                        if half == 0:
                            ring.preload(("wup", ol, 0), wup_ap(ol, 0), 8, 512)
                    while pend:
                        stats_mm(pend.pop(0))

                def tile_gen(ti):
                    c0, n = TT[ti]
                    cond = 0 if ti < 2 else 1
                    bA, bB = (banks[5 + ti], banks[ti]) if oproj is not None else (banks[ti], banks[3 + ti])
                    sq, tm, R, R2, lnt_ = sqs[ti], tms[ti], Ra[ti], Rb[ti], lnts[ti]
                    if oproj is None:
                        for ch in range(8):
                            sc.act(sq[:, 0:n], src[:, ch, c0:c0 + n], AF.Square)
                            sc.mm(bA[:, 0:n], onesR[:, :], sq[:, 0:n], start=(ch == 0), stop=(ch == 7))
                            yield
                    sc.act(lnt_[:, 0:n], bA[:, 0:n], AF.Ln, bias=EPS, scale=1.0 / D)
                    sc.act(R[:, 0:n], lnt_[:, 0:n], AF.Exp, scale=-0.5)
                    yield
                    for ch in range(8):
                        sc.tt(tm[:, 0:n], src[:, ch, c0:c0 + n], R[:, 0:n], ALU.mult)
                        k = ch * 2 + cond
                        sc.stt(x[:, ch, c0:c0 + n], tm[:, 0:n], GG[:, lp, wp, k:k + 1], x[:, ch, c0:c0 + n],
                               ALU.mult, ALU.add)
                        sc.act(sq[:, 0:n], x[:, ch, c0:c0 + n], AF.Square)
                        sc.mm(bB[:, 0:n], onesR[:, :], sq[:, 0:n], start=(ch == 0), stop=(ch == 7))
                        yield
                    sc.act(lnt_[:, 0:n], bB[:, 0:n], AF.Ln, bias=EPS, scale=1.0 / D)
                    sc.act(R2[:, 0:n], lnt_[:, 0:n], AF.Exp, scale=-0.5)
                    yield
                    for ch in range(8):
                        sc.tt(tm[:, 0:n], x[:, ch, c0:c0 + n], R2[:, 0:n], ALU.mult)
                        k = ch * 2 + cond
                        sc.act(dst[:, ch, c0:c0 + n], tm[:, 0:n], AF.Identity,
                               bias=shiftv(ln_, wn, ch, cond), scale=A1[:, ln_, wn, k:k + 1])
                        yield

                gens = [tile_gen(ti) for ti in range(3)]
                while gens:
                    for g in list(gens):
                        try:
                            next(g)
                        except StopIteration:
                            gens.remove(g)

        def post_norm(l, which, src, final=False):
            with sc.scope():
                sqs = [tile("sq", [128, 512], F32R) for _ in range(3)]
                tms = [tile("tmp512", [128, 512]) for _ in range(3)]
                Ra = [tile("Ra", [128, 512]) for _ in range(3)]
                lnts = [tile("lnt", [128, 512]) for _ in range(3)]
                ys = [tile("ys", [128, D]) for _ in range(3)] if final else None
                pT_ = PsPool(banks[3:8], 512)

                def tile_gen(ti):
                    c0, n = TT[ti]
                    cond = 0 if ti < 2 else 1
                    bA = banks[ti]
                    sq, tm, R, lnt_ = sqs[ti], tms[ti], Ra[ti], lnts[ti]
                    for ch in range(8):
                        sc.act(sq[:, 0:n], src[:, ch, c0:c0 + n], AF.Square)
                        sc.mm(bA[:, 0:n], onesR[:, :], sq[:, 0:n], start=(ch == 0), stop=(ch == 7))
                        yield
                    sc.act(lnt_[:, 0:n], bA[:, 0:n], AF.Ln, bias=EPS, scale=1.0 / D)
                    sc.act(R[:, 0:n], lnt_[:, 0:n], AF.Exp, scale=-0.5)
                    yield
                    for ch in range(8):
                        sc.tt(tm[:, 0:n], src[:, ch, c0:c0 + n], R[:, 0:n], ALU.mult)
                        k = ch * 2 + cond
                        sc.stt(x[:, ch, c0:c0 + n], tm[:, 0:n], GG[:, l, which, k:k + 1], x[:, ch, c0:c0 + n],
                               ALU.mult, ALU.add)
                        yield
                    if final:
                        for tt in range(c0 // 128, (c0 + n) // 128):
                            yt_ = ys[ti]
                            for half in range(2):
                                ps = pT_.get()
                                for j in range(4):
                                    ch = half * 4 + j
                                    sc.tr(ps[:, j * 128:(j + 1) * 128], x[:, ch, tt * 128:(tt + 1) * 128], ident)
                                sc.copy(yt_[:, half * 512:(half + 1) * 512], ps[:, 0:512], "act" if half == 0 else "dve")
                                yield
                            sc.dma("sp", yout[tt * 128:(tt + 1) * 128, :], yt_[:])

                gens = [tile_gen(ti) for ti in range(3)]
                while gens:
                    for g in list(gens):
                        try:
                            next(g)
                        except StopIteration:
                            gens.remove(g)

        def wout_ap(l, half):
            return woutd[l, :, half * 512:(half + 1) * 512].rearrange("(k p) n -> p k n", p=128)

        def wup_ap(l, s):
            return wupd[l, :, s * 512:(s + 1) * 512].rearrange("(k p) n -> p k n", p=128)

        def out_proj(l, src, dst):
            s0_ = ring.load(wout_ap(l, 0), 8, 512, key=("wout", l, 0))
            slots_ = [s0_, ring.load(wout_ap(l, 1), 8, 512, idx=1 - s0_.idx)]
            for half in range(2):
                slot = slots_[half]
                for nb4 in range(4):
                    nb = half * 4 + nb4
                    for (c0, n) in TT:
                        ps = pG.get()
                        for kc in range(8):
                            sc.mm(ps[:, 0:n], slot.w(kc, nb4 * 128, nb4 * 128 + 128), src[:, kc, c0:c0 + n],
                                  start=(kc == 0), stop=(kc == 7))
                        sc.copy(dst[:, nb, c0:c0 + n], ps[:, 0:n], "act")
                if half == 0:
                    ring.preload(("wup", l, 0), wup_ap(l, 0), 8, 512)

        c2 = tile("c2", [128, 8, 2])
        scond = tile("scond", [128, 8, 2], F32R)
        mrow_ref = [None]

        def mod_gen(slabs, idle=0):
            for (l, sl) in slabs:
                slot = ring.load(wmod[l, :, sl * 512:(sl + 1) * 512].rearrange("(k p) n -> p k n", p=128), 8, 512, idx=mod_slot[0])
                mod_inflight[0] = True
                for _ in range(idle):
                    yield
                for kc in range(8):
                    sc.mm(pS[0:2, 0:512], scond[:, kc, :], slot.w(kc, 0, 512), start=(kc == 0), stop=(kc == 7))
                yield
                sc.copy(mrow_ref[0][:], pS[0:2, 0:512], "act")
                yield
                for j4 in range(4):
                    sc.tr(pS[:, j4 * 2:j4 * 2 + 2], mrow_ref[0][0:2, j4 * 128:(j4 + 1) * 128], cst[0:2, C_ID:C_ID + 2])
                c0_ = sl * 8
                sc.tt(modv[:, l, c0_:c0_ + 8], pS[:, 0:8], bmod[:, l, c0_:c0_ + 8], ALU.add)
                j = sl // 2
                if sl % 2 == 1 and j in (1, 4):
                    which = 0 if j == 1 else 1
                    sc.stt(A1[:, l, which, :], modv[:, l, j * 16:(j + 1) * 16], 1.0, gv[:, l, which * 2, :], ALU.add, ALU.mult)
                if sl % 2 == 1 and j in (2, 5):
                    which = 0 if j == 2 else 1
                    sc.tt(GG[:, l, which, :], modv[:, l, j * 16:(j + 1) * 16], gv[:, l, which * 2 + 1, :], ALU.mult)
                mod_inflight[0] = False
                yield

        def take(g, n):
            for _ in range(n):
                try:
                    next(g)
                except StopIteration:
                    return
                yield

        modbg = [None]
        mod_slot = [None]
        mod_inflight = [False]

        def mod_settle():
            while mod_inflight[0] and modbg[0] is not None:
                try:
                    next(modbg[0])
                except StopIteration:
                    modbg[0] = None

        def ffn(l):
            with sc.scope():
                U = [tile("U", [128, NU, 258]) for _ in range(2)]
                ft = [tile("ft", [128, NU, 256]) for _ in range(2)]
                actg = tile("actg", [128, 4, T], F32R)
                cv = tile("cv", [128, 44, 4])
                tmpw = [tile("tmpw", [128, 512]) for _ in range(2)] if FFN_POOL_ACC else None
                wi = [0]
                sc.dma("sp", cv[:], convd[:, l, :, :])
                for u_ in U:
                    sc.memset(u_[:], 0.0)
                for grp in range(6):
                    npair = 4 if grp < 5 else 2
                    for half in range(npair // 2):
                        s = grp * 2 + half
                        slot = ring.load(wup_ap(l, s), 8, 512, key=("wup", l, s))
                        for pp in range(2):
                            j = s * 2 + pp
                            jj = j - grp * 4
                            for bi, coff in ((0, pp * 128), (1, 256 + pp * 128)):
                                Ub = U[bi]
                                for ti, (c0, n) in enumerate(TT):
                                    ps = pF.get()
                                    for kc in range(8):
                                        sc.mm(ps[:, 0:n], slot.w(kc, coff, coff + 128), B[:, kc, c0:c0 + n],
                                              start=(kc == 0), stop=(kc == 7))
                                    u0, nu = c0 // 256, n // 256
                                    sc.copy(Ub[:, u0:u0 + nu, 1:257], ps[:, 0:n].re("p (a b) -> p a b", a=nu), "act")
                                sc.tt(Ub[:, 1:5, 0:1], Ub[:, 0:4, 256:257], chain[:, 0, 1:5].re("p (a b) -> p a b", b=1), ALU.mult)
                                sc.tt(Ub[:, 0:4, 257:258], Ub[:, 1:5, 1:2], chain[:, 1, 0:4].re("p (a b) -> p a b", b=1), ALU.mult)
                                blk = j if bi == 0 else 22 + j
                                t1 = ft[bi]
                                sc.act(t1[:], Ub[:, :, 1:257], AF.Identity, bias=cv[:, blk, 3:4], scale=cv[:, blk, 1:2])
                                sc.stt(t1[:], Ub[:, :, 0:256], cv[:, blk, 0:1], t1[:], ALU.mult, ALU.add)
                                sc.stt(t1[:], Ub[:, :, 2:258], cv[:, blk, 2:3], t1[:], ALU.mult, ALU.add)
                            sc.act(ft[1][:], ft[1][:], AF.Silu)
                            sc.tt(actg[:, jj, :].re("p (a b) -> p a b", a=NU), ft[1][:], ft[0][:], ALU.mult)
                    slotd = ring.load(wdownd[l, grp * 512:grp * 512 + npair * 128, :].rearrange("(k p) n -> p k n", p=128),
                                      npair, 1024)
                    if l == 0 and grp == 5:
                        ring.preload(("w1in", 0), w1in[:, 0:512].rearrange("(k p) n -> p k n", p=128), 8, 512)
                    for nb in range(8):
                        for (c0, n) in TT:
                            ps = pF.get()
                            for kk in range(npair):
                                sc.mm(ps[:, 0:n], slotd.w(kk, nb * 128, nb * 128 + 128), actg[:, kk, c0:c0 + n],
                                      start=(kk == 0), stop=(kk == npair - 1))
                            if grp == 0:
                                sc.copy(A[:, nb, c0:c0 + n], ps[:, 0:n], "act")
                            elif FFN_POOL_ACC:
                                wi[0] += 1
                                tw = tmpw[wi[0] % 2]
                                sc.copy(tw[:, 0:n], ps[:, 0:n], "act")
                                sc.tt(A[:, nb, c0:c0 + n], tw[:, 0:n], Af[:, nb, c0:c0 + n], ALU.add, eng="pool")
                            else:
                                sc.tt(A[:, nb, c0:c0 + n], ps[:, 0:n], Af[:, nb, c0:c0 + n], ALU.add)
                if l == 0:
                    ring.preload(("w1in", 1), w1in[:, 512:768].rearrange("(k p) n -> p k n", p=128), 8, 256)
            if l == 0:
                post_pre(0, 1, Af, 1, 0, A)
            else:
                post_norm(l, 1, Af, final=True)

        def mixer0():
            qk = tile("qk", [128, T], F32R)
            qkf = qk.v(F32)
            ktok = tile("ktok", [128, 10, 64])
            vtok = tile("vtok", [128, 10, 130], F32R)
            gate = tile("gate", [128, 10, 128])
            mixer_lra = [None]
            mixer_cr = [None]
            Sinit = [[tile("Sinit", [64, 129]) for _ in range(2)] for _ in range(2)]
            Sst = [[tile("Sst", [64, 130], F32R) for _ in range(2)] for _ in range(2)]
            mrep = [[tile("mrep", [128, 1]) for _ in range(2)] for _ in range(2)]
            gn = tile("gn", [128, 2, 128])
            bgt = tile("bgt", [128, 16])
            mmi = tile("mmi", [128, 2, NU, 4])
            gsb = tile("gsb", [128, 10, 16])
            lnf = tile("lnf", [128, 10, 8])
            bsb = tile("bsb", [128, 10, 8])
            csb = tile("csb", [128, 10, 8])
            mcol = tile("mcol", [128, 40])
            sm = {}
            si = [0]

            def ofw(chn, c, wr=False):
                return (B if wr else Bf)[:, chn, c * 128:(c + 1) * 128]

            sc.dma("sp", gn[:], gnd)
            sc.dma("sp", bgt[:], bgd)
            sc.dma("sp", mmi[:], mminit)
            def fill(out, src, val):
                sc.act(out, src, AF.Identity, bias=float(val), scale=0.0)

            fill(vtok[:, :, 128:129], x[:, 0, 0:10].re("p (a b) -> p a b", b=1), 1.0)
            fill(vtok[:, :, 129:130], x[:, 0, 0:10].re("p (a b) -> p a b", b=1), 0.0)
            for a_ in Sst:
                for b_ in a_:
                    fill(b_[:], x[0:64, 0, 0:130], 0.0)
            for a_ in mrep:
                for b_ in a_:
                    sc.memset(b_[:], 0.0)

            if stop == "mixsetup":
                raise Stop()

            def w0_used(slot_idx):
                return 480 if slot_idx == 0 else (464 if slot_idx == 4 else 448)

            def w0_ap(slot_idx):
                return w0cat[:, slot_idx * 512:slot_idx * 512 + w0_used(slot_idx)].rearrange("(k p) n -> p k n", p=128)

            def project(slot_idx, is_gla, head):
                mod_settle()
                used = w0_used(slot_idx)
                slot = ring.load(w0_ap(slot_idx), 8, used, idx=slot_idx % 2, key=("w0", slot_idx))
                mod_slot[0] = slot_idx % 2
                qs, ks = (0.125, 1.0) if is_gla else (1.0, 0.125)
                for (c0, n) in TT:
                    ps = pG.get()
                    for kc in range(8):
                        sc.mm(ps[:, 0:n], slot.w(kc, 0, 128), A[:, kc, c0:c0 + n], start=(kc == 0), stop=(kc == 7))
                    sc.act(qk[0:64, c0:c0 + n], ps[0:64, 0:n], AF.Copy, scale=qs)
                    sc.act(qk[64:128, c0:c0 + n], ps[64:128, 0:n], AF.Copy, scale=ks)
                    if slot_idx == 0:
                        ps2 = pG.get()
                        for kc in range(8):
                            sc.mm(ps2[0:32, 0:n], slot.w(kc, 448, 480), A[:, kc, c0:c0 + n], start=(kc == 0), stop=(kc == 7))
                        sc.copy(mixer_lra[0][0:32, c0:c0 + n], ps2[0:32, 0:n], "dve")
                if stop == "projF":
                    raise Stop()
                NT = 336 if (not is_gla and head == 0) else 320
                for tt in range(10):
                    ps = pG.get()
                    for kc in range(8):
                        sc.mm(ps[:, 0:NT], A[:, kc, tt * 128:(tt + 1) * 128], slot.w(kc, 128, 128 + NT),
                              start=(kc == 0), stop=(kc == 7))
                    sc.act(ktok[:, tt, :], ps[:, 0:64], AF.Copy, scale=ks)
                    sc.act(gate[:, tt, :], ps[:, 192:320], AF.Silu if is_gla else AF.Sigmoid)
                    sc.copy(vtok[:, tt, 0:128], ps[:, 64:192], "dve")
                    if NT == 336:
                        sc.tt(gsb[:, tt, :], ps[:, 320:336], bgt[:, :], ALU.add)

            tasks = []

            def interleave(gens):
                tasks[:] = list(gens)
                while tasks:
                    for g in list(tasks):
                        try:
                            next(g)
                        except StopIteration:
                            tasks.remove(g)

            def both(ga, gb, res):
                act = [gb, ga] if BOTH_SWAP else [ga, gb]
                while act:
                    for g in list(act):
                        if g is None:
                            act.remove(g)
                            continue
                        try:
                            next(g)
                        except StopIteration as e_:
                            if g is ga:
                                res["v"] = e_.value
                            act.remove(g)
                    yield

            def warm_gen():
                while True:
                    for _w in range(WARM_MIX):
                        sc.mm(banks[3][:, 0:512], onesR[:, :], A[:, 0, 0:512])
                    yield

            def run_head(chains):
                tasks[:] = list(chains)
                while tasks:
                    for g in list(tasks):
                        try:
                            next(g)
                        except StopIteration:
                            tasks.remove(g)
                    if modbg[0] is not None:
                        try:
                            next(modbg[0])
                        except StopIteration:
                            modbg[0] = None

            def delayed(n, g):
                for _ in range(n):
                    yield
                yield from g

            def out_stage(smt, osum, c, is_gla, head):
                y1 = smt("y1", [128, 128])
                ss = smt("ss", [128, 1])
                sc.act(y1[:], osum[:, 0:128], AF.Square, accum=ss[:])
                lr_ = smt("lr_", [128, 1])
                sc.act(lr_[:], ss[:], AF.Ln, bias=EPS, scale=1.0 / 128)
                rs = smt("rs", [128, 1])
                sc.act(rs[:], lr_[:], AF.Exp, scale=-0.5)
                yield
                sc.stt(y1[:], osum[:, 0:128], rs[:], gn[:, 0 if is_gla else 1, :], ALU.mult, ALU.mult)
                sc.tt(y1[:], y1[:], gate[:, c, :], ALU.mult)
                yield
                pt = pQ.get()
                sc.tr(pt[:, 0:128], y1[:], ident)
                chn = head if is_gla else 4 + head
                sc.copy(B[:, chn, c * 128:(c + 1) * 128], pt[:, 0:128], "act")
                yield

            def chunk_order(d):
                return list(range(10)) if d == 0 else list(range(9, -1, -1))

            def cslice(d, a, b):
                o = a if d == 0 else b
                return cst[:, o:o + 128]

            def mk_smt(prefix):
                def f(name, shape, nbuf=1, par=0, dtype=F32):
                    key = prefix + name
                    if key not in sm:
                        sm[key] = [tile(key, shape, dtype) for _ in range(nbuf)]
                    return sm[key][par % nbuf]
                return f

            def gla_chain(i, d, wg):
                smt = mk_smt("g%d_" % d)
                triG = mixer_cr[0][:, d * 128:(d + 1) * 128]
                maskP = cslice(d, C_MPF, C_MPB)
                order = chunk_order(d)
                st = {"cur": 0}

                def PRE(k):
                    c = order[k]
                    cs = slice(c * 128, (c + 1) * 128)
                    if ((c % 2 == 0) if d == 0 else (c % 2 == 1)):
                        sc.dma("sp", Sinit[d][(c // 2) % 2][:, 0:128], ginit[d, c // 2, i, :, :])
                    pz = pQ.get()
                    sc.mm(pz[:, 0:128], mixer_lra[0][0:33, cs], wg[0:33, d, :])
                    lnv = smt("lnv", [128, 128], dtype=F32R)
                    e1 = lnv.v(F32)
                    sc.act(lnv[:], pz[:, 0:128], AF.Exp, scale=-1.0)
                    yield
                    sc.act(lnv[:], e1[:], AF.Ln, bias=1.0)
                    yield
                    pbT = pQ.get()
                    sc.mm(pbT[:, 0:128], lnv[:, :], triG)
                    eqk = smt("eqk", [128, 128])
                    sc.act(eqk[0:64, :], pbT[0:64, 0:128], AF.Exp)
                    sc.act(eqk[64:128, :], pbT[64:128, 0:128], AF.Exp, scale=-1.0)
                    yield
                    pbt = pQ.get()
                    sc.mm(pbt[:, 0:64], triG, lnv[:, 0:64])
                    ekt = smt("ekt", [128, 64])
                    sc.act(ekt[:], pbt[:, 0:64], AF.Exp, scale=-1.0)
                    yield
                    qt = smt("qt", [64, 128], 2, k, dtype=F32R)
                    sc.tt(qt[:], qkf[0:64, cs], eqk[0:64, :], ALU.mult)
                    kt = smt("kt", [64, 128], dtype=F32R)
                    sc.tt(kt[:], qkf[64:128, cs], eqk[64:128, :], ALU.mult)
                    ktk = smt("ktk", [128, 64], 2, k, dtype=F32R)
                    sc.tt(ktk[:], ktok[:, c, :], ekt[:], ALU.mult)
                    yield
                    psT = pQ.get()
                    sc.mm(psT[:, 0:128], kt[:, :], qt[:, :])
                    P = smt("P", [128, 128], 2, k, dtype=F32R)
                    sc.tt(P[:], psT[:, 0:128], maskP, ALU.mult)
                    yield
                    ecl = smt("ecl", [64, 1], 2, k)
                    sc.copy(ecl[:], eqk[0:64, 127:128] if d == 0 else eqk[0:64, 0:1], "dve")
                    return dict(qt=qt, P=P, ktk=ktk, ecl=ecl)

                def POST(k, pre):
                    c = order[k]
                    u = c // 2
                    first = (c % 2 == 0) if d == 0 else (c % 2 == 1)
                    last = not first
                    cur = st["cur"]
                    if first:
                        Si = Sinit[d][u % 2]
                        sc.stt(Sst[d][1 - cur][:, 0:128], Sst[d][cur].v(F32)[:, 0:128], chain[0:64, d, u:u + 1],
                               Si[:, 0:128], ALU.mult, ALU.add)
                        cur = 1 - cur
                    S = Sst[d][cur]
                    qt, P, ktk, ecl = pre["qt"], pre["P"], pre["ktk"], pre["ecl"]
                    po = pQ.get()
                    sc.mm(po[:, 0:128], qt[:, :], S[:, 0:128], start=True, stop=False)
                    sc.mm(po[:, 0:128], P[:, :], vtok[:, c, 0:128], start=False, stop=True)
                    pdS = pQ.get()
                    sc.mm(pdS[0:64, 0:128], ktk[:, :], vtok[:, c, 0:128])
                    yield
                    Sn = Sst[d][1 - cur]
                    sc.tt(Sn[:, 0:128], pdS[0:64, 0:128], S.v(F32)[:, 0:128], ALU.add)
                    sc.ts(Sn[:, 0:128], Sn.v(F32)[:, 0:128], ecl[:], ALU.mult)
                    st["cur"] = 1 - cur
                    if last:
                        sc.dma("sp", gla_o[d, u, i, :, :], Sn.v(F32)[:, 0:128])
                    yield
                    if k < 5:
                        sc.copy(ofw(i, c, True), po[:, 0:128], "act")
                        yield
                    else:
                        osum = smt("osum", [128, 128], 2, k)
                        sc.tt(osum[:], po[:, 0:128], ofw(i, c), ALU.add)
                        yield
                        tasks.append(delayed(OUT_DELAY, out_stage(smt, osum, c, True, i)))

                cur_pre = yield from PRE(0)
                for k in range(10):
                    res = {}
                    yield from both(PRE(k + 1) if k + 1 < 10 else None, POST(k, cur_pre), res)
                    cur_pre = res.get("v")

            def gla_head(i, wg):
                project(i, True, i)
                ring.preload(("w0", i + 1), w0_ap(i + 1), 8, w0_used(i + 1), idx=(i + 1) % 2)
                sc.dma("pool", wg[:], wgd[:, :, i * 128:(i + 1) * 128])
                run_head([gla_chain(i, 0, wg), gla_chain(i, 1, wg)])

            def mlstm_gates():
                e8 = tile("e8", [128, 10, 8])
                sc.act(e8[:], gsb[:, :, 8:16], AF.Exp, scale=-1.0)
                sc.act(lnf[:], e8[:], AF.Ln, bias=1.0)
                for d in range(2):
                    triM = cslice(d, C_TMF, C_TMB)
                    for tt in range(10):
                        pb = pQ.get()
                        sc.mm(pb[:, 0:4], triM, lnf[:, tt, d * 4:(d + 1) * 4])
                        sc.copy(bsb[:, tt, d * 4:(d + 1) * 4], pb[:, 0:4], "act")
                sc.tt(csb[:], gsb[:, :, 0:8], bsb[:], ALU.subtract)

            def mlstm_chain(i, d):
                smt = mk_smt("m%d_" % d)
                maskadd = cslice(d, C_MAF, C_MAB)
                sel = cslice(d, C_SLF, C_SLB)
                kk = d * 4 + i
                order = chunk_order(d)
                st = {"cur": 0}

                def PRE(k):
                    c = order[k]
                    cs = slice(c * 128, (c + 1) * 128)
                    if ((c % 2 == 0) if d == 0 else (c % 2 == 1)):
                        sc.dma("sp", Sinit[d][(c // 2) % 2][:], minit[d, c // 2, i, :, :])
                    ccol = csb[:, c, kk:kk + 1]
                    bcol = bsb[:, c, kk:kk + 1]
                    kTc = smt("kTc", [64, 128], 2, k, dtype=F32R)
                    sc.copy(kTc[:], qkf[64:128, cs], "act")
                    diagc = smt("diagc", [128, 128])
                    sc.ts(diagc[:], ident, ccol, ALU.mult)
                    yield
                    pcb = pQ.get()
                    sc.mm(pcb[:, 0:128], ones, diagc[:, :])
                    Dm = smt("Dm", [128, 128], 2, k)
                    sc.stt(Dm[:], pcb[:, 0:128], bcol, maskadd, ALU.add, ALU.add)
                    yield
                    rmx = smt("rmx", [128, 1], 2, k)
                    sc.rmax(rmx[:], Dm[:])
                    yield
                    return dict(kTc=kTc, Dm=Dm, rmx=rmx)

                def POST(k, pre):
                    c = order[k]
                    cs = slice(c * 128, (c + 1) * 128)
                    u = c // 2
                    first = (c % 2 == 0) if d == 0 else (c % 2 == 1)
                    last = not first
                    cur = st["cur"]
                    ccol = csb[:, c, kk:kk + 1]
                    bcol = bsb[:, c, kk:kk + 1]
                    kTc, Dm, rmx = pre["kTc"], pre["Dm"], pre["rmx"]
                    if first:
                        Si = Sinit[d][u % 2]
                        sc.stt(Sst[d][1 - cur][:, 0:129], Sst[d][cur].v(F32)[:, 0:129], chain[0:64, d, u:u + 1], Si[:], ALU.mult, ALU.add)
                        sc.stt(mrep[d][1 - cur][:], mrep[d][cur][:], chain[:, d, u:u + 1], mmi[:, d, u, i:i + 1],
                               ALU.mult, ALU.add)
                        cur = 1 - cur
                    Cg = Sst[d][cur]
                    mr = mrep[d][cur]
                    bm = smt("bm", [128, 1])
                    sc.tt(bm[:], bcol, mr[:], ALU.add)
                    mb = smt("mb", [128, 2])
                    mbf = mb
                    sc.tt(mb[:, 0:1], bm[:], rmx[:], ALU.max)
                    sc.copy(mb[:, 1:2], bcol, "dve")
                    yield
                    psl = pQ.get()
                    sc.mm(psl[:, 0:2], sel, mb[:, :])
                    mbn = smt("mbn", [128, 2])
                    sc.copy(mbn[:], psl[:, 0:2], "act")
                    yield
                    dlt = smt("dlt", [128, 1])
                    sc.tt(dlt[:], mbn[:, 1:2], mbn[:, 0:1], ALU.subtract)
                    t2 = smt("t2", [128, 1])
                    sc.tt(t2[:], dlt[:], mr[:], ALU.add)
                    sc.copy(mrep[d][1 - cur][:], mbn[:, 0:1], "dve")
                    nm = smt("nm", [128, 1])
                    sc.ts(nm[:], mbf[:, 0:1], -1.0, ALU.mult)
                    yield
                    srcw = smt("srcw", [128, 1])
                    sc.act(srcw[:], ccol, AF.Exp, bias=dlt[:])
                    cw = smt("cw", [128, 1])
                    sc.act(cw[:], t2[:], AF.Exp)
                    Wi = smt("Wi", [128, 128])
                    sc.act(Wi[:], Dm[:], AF.Exp, bias=nm[:])
                    winter = smt("winter", [128, 1])
                    sc.act(winter[:], bm[:], AF.Exp, bias=nm[:])
                    enm = smt("enm", [128, 1])
                    sc.act(enm[:], nm[:], AF.Exp)
                    yield
                    ksw = smt("ksw", [128, 64], dtype=F32R)
                    sc.ts(ksw[:], ktok[:, c, :], srcw[:], ALU.mult)
                    pit = pG.get()
                    sc.mm(pit[:, 0:130], qk[0:64, cs], Cg[:, 0:130])
                    its = smt("its", [128, 129])
                    sc.act(its[:], pit[:, 0:129], AF.Copy, scale=winter[:])
                    yield
                    pdC = pG.get()
                    sc.mm(pdC[0:64, 0:130], ksw[:, :], vtok[:, c, 0:130])
                    Cn = Sst[d][1 - cur]
                    sc.stt(Cn[:], Cg.v(F32)[:], cw[0:64, :], pdC[0:64, 0:130], ALU.mult, ALU.add)
                    st["cur"] = 1 - cur
                    if last:
                        sc.dma("sp", mC_o[d, u, i, :, :], Cn.v(F32)[:, 0:129])
                        col = (d * NU + u) * 4 + i
                        sc.copy(mcol[:, col:col + 1], mbn[:, 0:1], "dve")
                    yield
                    pqk = pQ.get()
                    sc.mm(pqk[:, 0:128], qk[0:64, cs], kTc[:, :])
                    qkw = smt("qkw", [128, 128])
                    sc.tt(qkw[:], pqk[:, 0:128], Wi[:], ALU.mult)
                    yield
                    pT = pQ.get()
                    sc.tr(pT[:, 0:128], qkw[:], ident)
                    qkT = smt("qkT", [128, 128], dtype=F32R)
                    sc.copy(qkT[:], pT[:, 0:128], "act")
                    yield
                    pin = pG.get()
                    sc.mm(pin[:, 0:130], qkT[:, :], vtok[:, c, 0:130])
                    nd = smt("nd", [128, 129])
                    sc.tt(nd[:], pin[:, 0:129], its[:], ALU.add)
                    yield
                    nden = smt("nden", [128, 1])
                    sc.ts(nden[:], nd[:, 128:129], -1.0, ALU.mult)
                    aden = smt("aden", [128, 1])
                    sc.tt(aden[:], nden[:], nd[:, 128:129], ALU.max)
                    dd = smt("dd", [128, 1])
                    sc.tt(dd[:], aden[:], enm[:], ALU.max)
                    rden = smt("rden", [128, 1])
                    sc.op("dve", lambda e, o=rden[:], i_=dd[:]: e.reciprocal(o.ap, i_.ap), [dd[:]], [rden[:]])
                    yield
                    if k < 5:
                        sc.ts(ofw(4 + i, c, True), nd[:, 0:128], rden[:], ALU.mult)
                        yield
                    else:
                        osum = smt("osum", [128, 128], 2, k)
                        sc.stt(osum[:], nd[:, 0:128], rden[:], ofw(4 + i, c), ALU.mult, ALU.add)
                        yield
                        tasks.append(delayed(OUT_DELAY, out_stage(smt, osum, c, False, i)))

                cur_pre = yield from PRE(0)
                for k in range(10):
                    res = {}
                    yield from both(PRE(k + 1) if k + 1 < 10 else None, POST(k, cur_pre), res)
                    cur_pre = res.get("v")

            def mlstm_head(i):
                project(4 + i, False, i)
                if i < 3:
                    ring.preload(("w0", 5 + i), w0_ap(5 + i), 8, w0_used(5 + i), idx=(5 + i) % 2)
                else:
                    ring.preload(("wout", 0, 0), wout_ap(0, 0), 8, 512, idx=0)
                if i == 0:
                    mlstm_gates()
                run_head([mlstm_chain(i, 0), mlstm_chain(i, 1)])

            with sc.scope():
                lraT = tile("lraT", [33, T], F32R)
                wgt = [tile("wgt", [33, 2, 128], F32R) for _ in range(1)]
                cstR = tile("cstR", [128, 256], F32R)
                sc.copy(cstR[:, 0:128], cst[:, C_TGF:C_TGF + 128], "act")
                sc.copy(cstR[:, 128:256], cst[:, C_TGB:C_TGB + 128], "act")
                mixer_cr[0] = cstR
                fill(lraT[32:33, :], x[32:33, 0, :], 1.0)
                mixer_lra[0] = lraT
                for i in range(4):
                    gla_head(i, wgt[0])
                    if stop == "gla%d" % i:
                        raise Stop()
            with sc.scope():
                for i in range(4):
                    mlstm_head(i)
                    if stop == "mlstm%d" % i:
                        raise Stop()
                sc.dma("sp", mm_o, mcol[0:1, :])

        def mixer1():
            def warm(n_=None):
                for _w in range(WARM_ATT if n_ is None else n_):
                    sc.mm(banks[3][:, 0:512], onesR[:, :], ckvC[:, 0, 0:512])

            gqkv = tile("gqkv", [128, 5])
            rope = tile("rope", [128, T])
            kmx = tile("kmx", [128, 4])
            kmax2 = tile("kmax2", [128, 1])
            krmax = tile("krmax", [128, 1])
            ckvC = tile("ckvC", [128, 2, 512], F32R)
            KRc = tile("KRc", [70, 512], F32R)
            QR = tile("QR", [70, T], F32R)
            sc.dma("sp", gqkv[:], gqkvd)
            sc.dma("sp", rope[:], ropeT)
            sc.dma("pool", KRc[64:70, :], ktab[:, 0:512])
            sc.dma("pool", QR[65:70, :], qtab)
            with sc.scope():
                cstage = tile("cstage", [128, 4, 256])
                kstage = tile("kstage", [128, 4, 64])
                sc.dma("sp", cstage[:], ckvc.rearrange("(a p) n -> p a n", p=128))
                sc.dma("sp", kstage[:], krc.rearrange("(a p) n -> p a n", p=128))
                for a_ in range(4):
                    for rc in range(2):
                        pt = pX.get()
                        sc.tr(pt[:, 0:128], cstage[:, a_, rc * 128:(rc + 1) * 128], ident)
                        sc.copy(ckvC[:, rc, a_ * 128:(a_ + 1) * 128], pt[:, 0:128], "act")
                    pt = pX.get()
                    sc.tr(pt[0:64, 0:128], kstage[:, a_, :], ident)
                    sc.copy(KRc[0:64, a_ * 128:(a_ + 1) * 128], pt[0:64, 0:128], "act")
            with sc.scope():
                qa_t = tile("qa_t", [128, 3, 512])
                kv_t = tile("kv_t", [128, 2, 512])
                kr_t = tile("kr_t", [128, 512])
                kr_s = tile("kr_s", [64, 512])
                kr_u = tile("kr_u", [64, 512])
                stg = tile("stg", [128, 4, 256])
                stg2 = tile("stg2", [128, 4, 64])
                nt = dict(sq=[tile("sq", [128, 512], F32R)], R=[tile("R", [128, 512])], lnt=tile("lnt", [128, 512]), c=[0, 0, 0])
                slot0 = ring.load(w1in[:, 0:512].rearrange("(k p) n -> p k n", p=128), 8, 512, key=("w1in", 0))
                slot1 = ring.load(w1in[:, 512:768].rearrange("(k p) n -> p k n", p=128), 8, 256, key=("w1in", 1))

                def fproj(slot, coff, c0, n):
                    ps = pX.get()
                    for kc in range(8):
                        sc.mm(ps[:, 0:n], slot.w(kc, coff, coff + 128), A[:, kc, c0:c0 + n], start=(kc == 0), stop=(kc == 7))
                    warm(WARM_PROJ)
                    return ps

                for ti, (c0, n) in enumerate(TT):
                    for b_ in range(3):
                        ps = fproj(slot0, b_ * 128, c0, n)
                        sc.copy(qa_t[:, b_, 0:n], ps[:, 0:n], "act")
                    ps = fproj(slot0, 384, c0, n)
                    sc.copy(kv_t[:, 0, 0:n], ps[:, 0:n], "act")
                    ps = fproj(slot1, 0, c0, n)
                    sc.copy(kv_t[:, 1, 0:n], ps[:, 0:n], "act")
                    ps = fproj(slot1, 128, c0, n)
                    sc.copy(kr_u[:, 0:n], ps[0:64, 0:n], "dve")
                    sc.tt(kr_t[:, 0:n], ps[:, 0:n], rope[:, c0:c0 + n], ALU.mult)
                    sc.copy(kr_s[:, 0:n], kr_t[64:128, 0:n], "act")
                    sc.tt(A[0:64, 5, c0:c0 + n], kr_t[0:64, 0:n], kr_s[:, 0:n], ALU.add)
                    sc.dma("pool", A[64:70, 5, c0:c0 + n], ktab[:, 512 + c0:512 + c0 + n])
                    R = compute_R(nt, lambda ch: qa_t[:, ch, 0:n], 3, 384, n)
                    for ch in range(3):
                        sc.stt(A[:, ch, c0:c0 + n], qa_t[:, ch, 0:n], gqkv[:, ch:ch + 1], R[:, 0:n], ALU.mult, ALU.mult)
                    R = compute_R(nt, lambda ch: kv_t[:, ch, 0:n], 2, 256, n)
                    for ch in range(2):
                        sc.stt(kv_t[:, ch, 0:n], kv_t[:, ch, 0:n], gqkv[:, 3 + ch:4 + ch], R[:, 0:n], ALU.mult, ALU.mult)
                        sc.copy(A[:, 3 + ch, c0:c0 + n], kv_t[:, ch, 0:n], "act")
                    na = n // 128
                    for a_ in range(na):
                        pt = pX.get()
                        for ch in range(2):
                            sc.tr(pt[:, ch * 128:(ch + 1) * 128], kv_t[:, ch, a_ * 128:(a_ + 1) * 128], ident)
                        sc.copy(stg[:, a_, :], pt[:, 0:256], "dve")
                        pt2 = pX.get()
                        sc.tr(pt2[:, 0:64], kr_u[:, a_ * 128:(a_ + 1) * 128], cst[0:64, C_ID:C_ID + 64])
                        sc.copy(stg2[:, a_, :], pt2[:, 0:64], "dve")
                    sc.dma("sp", ckv_o[c0:c0 + n, :].rearrange("(a p) n -> p a n", p=128), stg[:, 0:na, :])
                    sc.dma("sp", kr_o[c0:c0 + n, :].rearrange("(a p) n -> p a n", p=128), stg2[:, 0:na, :])
            phase_end("att_proj")
            with sc.scope():
                Kh = tile("Kh", [128, NKEY], F32R)
                Vh = tile("Vh", [128, 14, 128], F32R)
                sqa = tile("sqa", [128, 512], F32R)
                sqb = tile("sqb", [64, 512], F32R)
                sqq = tile("sqq", [128, 512], F32R)
                rdt = tile("rdt", [128, 512])
                lnd = tile("lnd", [128, 512])
                qrt = tile("qrt", [128, 512])
                qrs = tile("qrs", [64, 512])

                def Qn(c0, n):
                    return A[:, 6, c0:c0 + n]

                def PT(i, n):
                    return A[:, 7, i * 512:i * 512 + n]

                def ckv_src(rc, k0, n):
                    return ckvC[:, rc, k0:k0 + n] if k0 < 512 else A[:, 3 + rc, k0 - 512:k0 - 512 + n]

                def KR_src(k0, n, rows=70):
                    return KRc[0:rows, k0:k0 + n] if k0 < 512 else A[0:rows, 5, k0 - 512:k0 - 512 + n]

                KT4 = [(0, 512), (512, 512), (1024, 512), (1536, 256)]
                po_i = [0]
                norm_pend = []
                lnds = [lnd, tile("lnd2", [128, 512])]
                for ki, (k0, n) in enumerate(KT4):
                    sc.act(sqb[:, 0:n], (KRc.v(F32)[0:64, k0:k0 + n] if k0 < 512 else Af[0:64, 5, k0 - 512:k0 - 512 + n]), AF.Square)
                    sc.mm(pS[:, 0:n], onesR[0:64, :], sqb[:, 0:n])
                    sc.rmax(kmx[:, ki:ki + 1], pS[:, 0:n])
                sc.rmax(krmax[:], kmx[:, 0:4])
                slotkv = ring.load(wkvbd.rearrange("(k p) n -> p k n", p=128), 2, 2048, idx=0)
                slotq = None
                for h in range(8):
                    if h % 4 == 0:
                        slotq = ring.load(wqbd[:, (h // 4) * 1024:(h // 4 + 1) * 1024].rearrange("(k p) n -> p k n", p=128), 3, 1024, idx=1)
                    hh = h % 4
                    def k_chain():
                        for ki, (k0, n) in enumerate(KT4):
                            ps = pX.get()
                            for rc in range(2):
                                sc.mm(ps[:, 0:n], slotkv.w(rc, h * 256, h * 256 + 128), ckv_src(rc, k0, n), start=(rc == 0), stop=(rc == 1))
                            warm()
                            sc.copy(Kh[:, k0:k0 + n], ps[:, 0:n], "act")
                            sc.act(sqa[:, 0:n], ps[:, 0:n], AF.Square)
                            yield
                            sc.mm(pS[:, 0:n], onesR[:, :], sqa[:, 0:n])
                            sc.rmax(kmx[:, ki:ki + 1], pS[:, 0:n])
                            yield
                        sc.rmax(kmax2[:], kmx[:, 0:4])
                        sc.tt(kmax2[:], kmax2[:], krmax[:], ALU.add)
                        yield

                    def v_chain():
                        for kt in range(14):
                            ps = pX.get()
                            for rc in range(2):
                                sc.mm(ps[:, 0:128], ckv_src(rc, kt * 128, 128), slotkv.w(rc, h * 256 + 128, h * 256 + 256),
                                      start=(rc == 0), stop=(rc == 1))
                            sc.copy(Vh[:, kt, :], ps[:, 0:128], "dve")
                            yield

                    def q_chain():
                        for (c0, n) in TT:
                            ps = pX.get()
                            for kc in range(3):
                                sc.mm(ps[:, 0:n], slotq.w(kc, hh * 256, hh * 256 + 128), A[:, kc, c0:c0 + n], start=(kc == 0), stop=(kc == 2))
                            warm()
                            sc.copy(Qn(c0, n), ps[:, 0:n], "act")
                            sc.act(sqq[:, 0:n], ps[:, 0:n], AF.Square)
                            yield
                            ps2 = pX.get()
                            for kc in range(3):
                                sc.mm(ps2[:, 0:n], slotq.w(kc, hh * 256 + 128, hh * 256 + 256), A[:, kc, c0:c0 + n], start=(kc == 0), stop=(kc == 2))
                            warm()
                            sc.tt(qrt[:, 0:n], ps2[:, 0:n], rope[:, c0:c0 + n], ALU.mult)
                            yield
                            sc.copy(qrs[:, 0:n], qrt[64:128, 0:n], "act")
                            sc.tt(QR[0:64, c0:c0 + n], qrt[0:64, 0:n], qrs[:, 0:n], ALU.add)
                            sc.act(sqb[:, 0:n], QR.v(F32)[0:64, c0:c0 + n], AF.Square)
                            yield
                            sc.mm(pD[:, 0:n], onesR[:, :], sqq[:, 0:n], start=True, stop=False)
                            sc.mm(pD[:, 0:n], onesR[0:64, :], sqb[:, 0:n], start=False, stop=True)
                            sc.copy(QR[64:65, c0:c0 + n], pD[64:65, 0:n], "act")
                            yield

                    gens = [k_chain(), v_chain(), q_chain()]
                    while gens:
                        for g in list(gens):
                            try:
                                next(g)
                            except StopIteration:
                                gens.remove(g)
                    if h == 7:
                        ring.preload(("wout", 1, 0), wout_ap(1, 0), 8, 512, idx=1)
                    qrow = QR.v(F32)[64:65, :]
                    sc.act(QR[64:65, :], qrow, AF.Ln, scale=kmax2[64:65, :])
                    sc.act(QR[64:65, :], qrow, AF.Exp, scale=0.5)
                    sc.act(QR[64:65, :], qrow, AF.Copy, scale=-1.001)
                    for (c0, n, kbs) in ((0, 512, list(range(12))), (512, 512, list(range(12))), (1024, 256, [12, 13])):
                        po_i[0] += 1
                        pO = (banks[5], banks[3])[po_i[0] % 2]
                        lnd = lnds[po_i[0] % 2]
                        pend = None
                        for idx, kb in enumerate(kbs):
                            if idx == min(2, len(kbs) - 1) and norm_pend:
                                norm_pend.pop(0)()
                            ps = pX.get()
                            sc.mm(ps[:, 0:n], Kh[:, kb * 128:(kb + 1) * 128], Qn(c0, n), start=True, stop=False)
                            sc.mm(ps[:, 0:n], KR_src(kb * 128, 128), QR[0:70, c0:c0 + n], start=False, stop=True)
                            if pend is not None:
                                pidx, pkb, ppt = pend
                                sc.mm(pO[:, 0:n], Vh[:, pkb, :], ppt, start=(pidx == 0), stop=False)
                            pt = PT(idx % 2, n)
                            sc.act(pt, ps[:, 0:n], AF.Exp, scale=ATT_SCALE)
                            ptf = Af[:, 7, (idx % 2) * 512:(idx % 2) * 512 + n]
                            if idx == 0:
                                sc.copy(lnd[:, 0:n], ptf, "dve")
                            else:
                                sc.tt(lnd[:, 0:n], lnd[:, 0:n], ptf, ALU.add)
                            pend = (idx, kb, pt)
                        pidx, pkb, ppt = pend
                        sc.mm(pO[:, 0:n], Vh[:, pkb, :], ppt, start=(pidx == 0), stop=True)
                        sc.mm(pD[:, 0:n], ones, lnd[:, 0:n])

                        def finish(c0=c0, n=n, pO=pO, h=h):
                            sc.act(rdt[:, 0:n], pD[:, 0:n], AF.Ln)
                            sc.act(rdt[:, 0:n], rdt[:, 0:n], AF.Exp, scale=-1.0)
                            sc.tt(B[:, h, c0:c0 + n], pO[:, 0:n], rdt[:, 0:n], ALU.mult)

                        while norm_pend:
                            norm_pend.pop(0)()
                        norm_pend.append(finish)
                    while norm_pend:
                        norm_pend.pop(0)()
                    if stop == "att_h%d" % h:
                        raise Stop()

        try:
            sc.dma("sp", cst[:], cstd)
            sc.dma("sp", chain[:], chaind)
            sc.dma("sp", bmod[:], bmodT)
            sc.dma("sp", gv[:], gvecT)
            sc.copy(onesR[:], ones, "act")
            sc.dma("sp", c2[:], cond2T)
            sc.act(scond[:], c2[:], AF.Silu)
            with sc.scope():
                xt = [tile("xt", [128, D]) for _ in range(2)]
                mrow_ref[0] = tile("mrow", [2, 512])
                nt = norm_tiles()
                R3 = [tile("R3", [128, 512]) for _ in range(3)]
                g0 = mod_gen([(0, s_) for s_ in range(4)], idle=2)

                def adv(g, n_=1):
                    for _ in range(n_):
                        try:
                            next(g)
                        except StopIteration:
                            return

                for tt in range(10):
                    t = xt[tt % 2]
                    sc.dma("sp", t[:], xin[tt * 128:(tt + 1) * 128, :])
                    for half in range(2):
                        ps = pG.get()
                        for j in range(4):
                            ch = half * 4 + j
                            sc.tr(ps[:, j * 128:(j + 1) * 128], t[:, ch * 128:(ch + 1) * 128], ident)
                        sc.copy(x[:, half * 4:half * 4 + 4, tt * 128:(tt + 1) * 128],
                                ps[:, 0:512].re("p (a b) -> p a b", a=4), "act" if half == 0 else "dve")
                    adv(g0)
                    if tt in (3, 7, 9):
                        ti = {3: 0, 7: 1, 9: 2}[tt]
                        c0, n = TT[ti]
                        Rr = compute_R(nt, lambda ch, c0=c0, n=n: x[:, ch, c0:c0 + n], 8, D, n, bank=banks[5])
                        sc.copy(R3[ti][:, 0:n], Rr[:, 0:n], "dve")
                adv(g0, 1000)
                ring.preload(("w0", 0), w0cat[:, 0:480].rearrange("(k p) n -> p k n", p=128), 8, 480, idx=0)
                modbg[0] = mod_gen([(0, s_) for s_ in range(4, 12)] + [(1, s_) for s_ in range(12)], idle=MOD_IDLE)
                for ti, (c0, n) in enumerate(TT):
                    cond = 0 if ti < 2 else 1
                    for ch in range(8):
                        nt["c"][2] += 1
                        tm = nt["tmp"][nt["c"][2] % 2]
                        sc.tt(tm[:, 0:n], x[:, ch, c0:c0 + n], R3[ti][:, 0:n], ALU.mult)
                        k = ch * 2 + cond
                        sc.act(A[:, ch, c0:c0 + n], tm[:, 0:n], AF.Identity,
                               bias=shiftv(0, 0, ch, cond), scale=A1[:, 0, 0, k:k + 1])
            dump("x0", x[:], [128, 8, T])
            phase_end("norm0")
            dump("h0", Af[:], [128, 8, T])
            phase_end("norm0")
            with sc.scope():
                mrow_ref[0] = tile("mrow", [2, 512])
                mixer0()
                mod_slot[0] = None
                if modbg[0] is not None:
                    for _ in modbg[0]:
                        pass
            dump("modv", modv[:], [128, 2, 96])
            dump("mixed0", Bf[:], [128, 8, T])
            phase_end("mixer0")
            post_pre(0, 0, Af, 0, 1, B, oproj=(0, B))
            dump("x1", x[:], [128, 8, T])
            phase_end("mix0_done")
            ffn(0)
            dump("x2", x[:], [128, 8, T])
            phase_end("ffn0")
            with sc.scope():
                mixer1()
            dump("attn", Bf[:], [128, 8, T])
            phase_end("mixer1")
            post_pre(1, 0, Af, 1, 1, B, oproj=(1, B))
            dump("x3", x[:], [128, 8, T])
            phase_end("mix1_done")
            ffn(1)
            phase_end("ffn1")
        except Stop:
            pass
        sc.flush()
        dump("endx", x[:], [128, 8, T])
        dump("endA", Af[:], [128, 8, T])
        dump("endB", Bf[:], [128, 8, T])
        if stop is not None and stop != "ffn1":
            sc.flush()
            with sc.scope():
                ys = [tile("ys", [128, D]) for _ in range(2)]
                for tt in range(10):
                    yt_ = ys[tt % 2]
                    for half in range(2):
                        ps = pG.get()
                        for j in range(4):
                            ch = half * 4 + j
                            sc.tr(ps[:, j * 128:(j + 1) * 128], x[:, ch, tt * 128:(tt + 1) * 128], ident)
                        sc.copy(yt_[:, half * 512:(half + 1) * 512], ps[:, 0:512], "act" if half == 0 else "dve")
                    sc.dma("sp", yout[tt * 128:(tt + 1) * 128, :], yt_[:])
        sc.flush(final=True)
        stats = dict(sc.stats)
    return nc, dbg_out, stats


def _core_units(c):
    if c < 6:
        return "prompt", [5 * c + j for j in range(4)], 5 * c + 4
    return "sample", c - 6, 30 + (c - 6)


_SW = np.array([(d + 16) if (d % 32) < 16 else (d - 16) for d in range(64)])


def _rope_table(mode):
    tab = np.zeros((128, T), np.float32)
    tab[0:64, :] = 1.0
    if mode == "sample":
        pos = np.arange(1024)
        row = (pos // 64).astype(np.float32)
        col = (pos % 64).astype(np.float32)
        inv = (1.0 / (np.float32(10000.0) ** (np.arange(0, 32, 2, dtype=np.float32) / np.float32(32)))).astype(np.float32)
        for d in range(64):
            base = row if d < 32 else col
            ang = (base * inv[d % 16]).astype(np.float32)
            tab[d, 0:1024] = np.cos(ang)
            sgn = -1.0 if (d % 32) < 16 else 1.0
            tab[64 + d, 0:1024] = sgn * np.sin(ang)
    return tab


def prep_weights(inp):
    f = lambda a: np.ascontiguousarray(np.asarray(a, dtype=np.float32))
    W = {}
    W["wmod"] = f(np.stack([inp["l0_w_mod"], inp["l1_w_mod"]]))
    bm = np.stack([inp["l0_b_mod"], inp["l1_b_mod"]])
    bmT = bm.reshape(2, 48, 128).transpose(2, 0, 1)
    W["bmodT"] = f(np.repeat(bmT[:, :, :, None], 2, axis=3).reshape(128, 2, 96))
    g = np.stack([np.stack([inp["l0_g_pre_mix"], inp["l0_g_post_mix"], inp["l0_g_pre_ffn"], inp["l0_g_post_ffn"]]),
                  np.stack([inp["l1_g_pre_mix"], inp["l1_g_post_mix"], inp["l1_g_pre_ffn"], inp["l1_g_post_ffn"]])])
    gT = g.reshape(2, 4, 8, 128).transpose(3, 0, 1, 2)
    W["gvecT"] = f(np.repeat(gT[..., None], 2, axis=4).reshape(128, 2, 4, 16))
    w = np.asarray(inp["l0_w_in"], np.float32)
    qa, ka, va, ga, lra = w[:, 0:256], w[:, 256:512], w[:, 512:1024], w[:, 1024:1536], w[:, 1536:1568]
    qb, kb, vb, ob, gts = w[:, 1568:1824], w[:, 1824:2080], w[:, 2080:2592], w[:, 2592:3104], w[:, 3104:3120]
    w0 = np.zeros((D, 4096), np.float32)
    for i in range(4):
        s = i * 512
        w0[:, s:s + 64] = qa[:, i * 64:(i + 1) * 64]
        w0[:, s + 64:s + 128] = ka[:, i * 64:(i + 1) * 64]
        w0[:, s + 128:s + 192] = ka[:, i * 64:(i + 1) * 64]
        w0[:, s + 192:s + 320] = va[:, i * 128:(i + 1) * 128]
        w0[:, s + 320:s + 448] = ga[:, i * 128:(i + 1) * 128]
        s = (4 + i) * 512
        w0[:, s:s + 64] = qb[:, i * 64:(i + 1) * 64]
        w0[:, s + 64:s + 128] = kb[:, i * 64:(i + 1) * 64]
        w0[:, s + 128:s + 192] = kb[:, i * 64:(i + 1) * 64]
        w0[:, s + 192:s + 320] = vb[:, i * 128:(i + 1) * 128]
        w0[:, s + 320:s + 448] = ob[:, i * 128:(i + 1) * 128]
    w0[:, 448:480] = lra
    w0[:, 4 * 512 + 448:4 * 512 + 464] = gts
    W["w0cat"] = f(w0)
    wg = np.zeros((33, 2, 512), np.float32)
    for d, (wn, bn, r0) in enumerate((("l0_gla_w_gate_f", "l0_gla_b_gate_f", 0), ("l0_gla_w_gate_b", "l0_gla_b_gate_b", 16))):
        wgd, bgd = np.asarray(inp[wn], np.float32), np.asarray(inp[bn], np.float32)
        for h in range(4):
            for rep in range(2):
                wg[r0:r0 + 16, d, h * 128 + rep * 64:h * 128 + rep * 64 + 64] = wgd[:, h * 64:(h + 1) * 64]
                wg[32, d, h * 128 + rep * 64:h * 128 + rep * 64 + 64] = bgd[h * 64:(h + 1) * 64]
    W["wg"] = f(wg)
    W["gn"] = f(np.broadcast_to(np.stack([inp["l0_gla_g_norm"], inp["l0_mlstm_g_norm"]])[None], (128, 2, 128)))
    W["bgates"] = f(np.broadcast_to(np.asarray(inp["l0_mlstm_b_gates"])[None], (128, 16)))
    W["wout"] = f(np.stack([inp["l0_w_out"], inp["l1_w_out"]]))
    wups, cvs = [], []
    ffn_in = ((inp["l0_ffn_w_up"], inp["l0_ffn_conv_w"], inp["l0_ffn_conv_b"]),
              (inp["l1_ffn_w_up"], inp["l1_ffn_conv_w"], inp["l1_ffn_conv_b"]))
    for l in range(2):
        wu = np.asarray(ffn_in[l][0], np.float32)
        cols = []
        for s in range(11):
            cols.append(wu[:, s * 256:(s + 1) * 256])
            cols.append(wu[:, 2816 + s * 256:2816 + (s + 1) * 256])
        wups.append(np.concatenate(cols, axis=1))
        cw = np.asarray(ffn_in[l][1], np.float32)
        cb = np.asarray(ffn_in[l][2], np.float32)
        cc = np.concatenate([cw, cb[None]], axis=0)
        cvs.append(cc.reshape(4, 44, 128).transpose(2, 1, 0))
    W["wup"] = f(np.stack(wups))
    W["convT"] = f(np.stack(cvs, axis=1))
    W["wdown"] = f(np.stack([inp["l0_ffn_w_down"], inp["l1_ffn_w_down"]]))
    w1 = np.asarray(inp["l1_w_in"], np.float32)
    W["w1in"] = f(np.concatenate([w1[:, 0:704], w1[:, 640:704][:, _SW]], axis=1))
    wq = np.asarray(inp["l1_w_qb"], np.float32)
    qcols = []
    for h in range(8):
        qcols += [wq[:, h * 192:h * 192 + 128], wq[:, h * 192 + 128:h * 192 + 192], wq[:, h * 192 + 128:h * 192 + 192][:, _SW]]
    W["wqb"] = f(np.concatenate(qcols, axis=1))
    W["wkvb"] = f(inp["l1_w_kvb"])
    gq = np.asarray(inp["l1_g_q_norm"], np.float32).reshape(3, 128).T
    gkv = np.asarray(inp["l1_g_kv_norm"], np.float32).reshape(2, 128).T
    W["gqkv"] = f(np.concatenate([gq, gkv], axis=1))
    W["cst"] = make_consts()
    return W


def prep_core(inp, c):
    f = lambda a: np.ascontiguousarray(np.asarray(a, dtype=np.float32))
    mode, grp, sa = _core_units(c)
    xp, xs = np.asarray(inp["x_prompt"]), np.asarray(inp["x_sample"])
    m = {}
    if mode == "prompt":
        xg = np.concatenate([xp[s] for s in grp], axis=0)
        cond0 = np.asarray(inp["c_ctx"])
    else:
        xg = xs[grp]
        cond0 = np.asarray(inp["c"])[grp]
    m["xin"] = f(np.concatenate([xg, xp[sa]], axis=0))
    cond2 = np.stack([cond0, np.asarray(inp["c_ctx"])])
    m["cond2T"] = f(cond2.reshape(2, 8, 128).transpose(2, 1, 0))
    chain = np.zeros((128, 2, NU), np.float32)
    ginit = np.zeros((2, NU, 4, 64, 128), np.float32)
    minit = np.zeros((2, NU, 4, 64, 129), np.float32)
    mminit = np.zeros((128, 2, NU, 4), np.float32)
    ktab = np.zeros((6, NKEY), np.float32)
    qtab = np.zeros((5, T), np.float32)
    ckvc = np.zeros((512, 256), np.float32)
    krc = np.zeros((512, 64), np.float32)
    ktab[0, :] = 1.0
    for j in range(4):
        ktab[1 + j, 512 + 256 * j:512 + 256 * (j + 1)] = 1.0
    ktab[5, 0:512] = 1.0
    if mode == "sample":
        b = grp
        chain[:, 0, 1:4] = 1.0
        chain[:, 1, 0:3] = 1.0
        ginit[0, 0] = inp["state_l0_gla_fwd"][b]
        ginit[1, 3] = inp["state_l0_gla_bwd"][b]
        minit[0, 0, :, :, 0:128] = inp["state_l0_mlstm_c_fwd"][b]
        minit[0, 0, :, :, 128] = inp["state_l0_mlstm_n_fwd"][b]
        minit[1, 3, :, :, 0:128] = inp["state_l0_mlstm_c_bwd"][b]
        minit[1, 3, :, :, 128] = inp["state_l0_mlstm_n_bwd"][b]
        mminit[:, 0, 0, :] = np.asarray(inp["state_l0_mlstm_m_fwd"])[b][None, :]
        mminit[:, 1, 3, :] = np.asarray(inp["state_l0_mlstm_m_bwd"])[b][None, :]
        ckvc = inp["cache_l1_ckv"][b]
        krc = inp["cache_l1_krope"][b]
    else:
        for u in range(4):
            for j in range(4):
                if j != u:
                    qtab[j, 256 * u:256 * (u + 1)] = NEG
        qtab[4, 0:1024] = NEG
    m["chain"], m["ginit"], m["minit"], m["mminit"] = chain, ginit, minit, mminit
    m["ktab"], m["qtab"], m["ckvc"], m["krc"] = ktab, qtab, f(ckvc), f(krc)
    m["ropeT"] = _rope_table(mode)
    return m


def assemble(results):
    yp = np.zeros((32, 256, D), np.float32)
    ysm = np.zeros((2, 1024, D), np.float32)
    gla = np.zeros((2, 32, 4, 64, 128), np.float32)
    mC = np.zeros((2, 32, 4, 64, 128), np.float32)
    mn = np.zeros((2, 32, 4, 64), np.float32)
    mm = np.zeros((2, 32, 4), np.float32)
    ckv = np.zeros((32, 256, 256), np.float32)
    kr = np.zeros((32, 256, 64), np.float32)
    for c, r in enumerate(results):
        mode, grp, sa = _core_units(c)
        units = [(4, sa)]
        if mode == "prompt":
            units += [(j, grp[j]) for j in range(4)]
        else:
            ysm[grp] = r["y"][0:1024]
        mmo = r["mm_o"].reshape(2, NU, 4)
        for u, s in units:
            yp[s] = r["y"][u * 256:(u + 1) * 256]
            ckv[s] = r["ckv_o"][u * 256:(u + 1) * 256]
            kr[s] = r["kr_o"][u * 256:(u + 1) * 256]
            for d in range(2):
                gla[d, s] = r["gla_o"][d, u]
                mC[d, s] = r["mC_o"][d, u, :, :, 0:128]
                mn[d, s] = r["mC_o"][d, u, :, :, 128]
                mm[d, s] = mmo[d, u]
    return (yp, ysm, gla[0], gla[1], mC[0], mn[0], mm[0], mC[1], mn[1], mm[1], ckv, kr)


_PROG = {}


def kernel(**inputs):
    if "nc" not in _PROG:
        _PROG["nc"] = build_program()[0]
    nc = _PROG["nc"]
    W = prep_weights(inputs)
    in_maps = []
    for c in range(8):
        m = dict(W)
        m.update(prep_core(inputs, c))
        in_maps.append(m)
    res = run_bass_kernel_spmd(nc, in_maps, core_ids=list(range(8)))
    return assemble(res.results)
```

```python
import contextlib
import numpy as np
import concourse.bass as bass
import concourse.mybir as mybir
from concourse.bass_utils import run_bass_kernel_spmd
from concourse.alu_op_type import AluOpType as ALU

F32 = mybir.dt.float32
F32R = mybir.dt.float32r
AF = mybir.ActivationFunctionType
AX = mybir.AxisListType

SAME_ENGINE_SYNC = True
class Tile:
    def __init__(self, sc, name, shape, dtype=F32, space="sb"):
        self.name, self.shape, self.dtype, self.space = name, list(shape), dtype, space
        alloc = sc.nc.sbuf_tensor if space == "sb" else sc.nc.psum_tensor
        self.h = sc.es.enter_context(alloc(name, list(shape), dtype))
        self.wr = {}
        self.rd = {}

    def __getitem__(self, idx):
        return Ref(self, idx)

    def v(self, dtype):
        return _View(self, dtype)


class _View:
    def __init__(self, tile, dtype):
        self.tile, self.dtype = tile, dtype

    def __getitem__(self, idx):
        return Ref(self.tile, idx, self.dtype)


class Ref:
    def __init__(self, tile, idx, dtype=None):
        if not isinstance(idx, tuple):
            idx = (idx,)
        idx = idx + (slice(None),) * (len(tile.shape) - len(idx))
        self.tile = tile
        ap = tile.h[idx]
        if dtype is not None and dtype != tile.dtype:
            ap = ap.bitcast(dtype)
        self.ap = ap
        box = []
        for i, n in zip(idx, tile.shape):
            if isinstance(i, int):
                box.append((i, i + 1))
            else:
                a, b, st = i.indices(n)
                box.append((a, b))
        self.box = tuple(box)


def _overlap(a, b):
    return all(x[0] < y[1] and y[0] < x[1] for x, y in zip(a, b))


def _contains(a, b):
    return all(x[0] <= y[0] and x[1] >= y[1] for x, y in zip(a, b))


class Op:
    __slots__ = ("eng", "fn", "waits", "is_dma", "sem", "target", "signal", "rank", "seq", "clock", "selfwait")


class Sched:
    def __init__(self, nc, es, n_dma_sems=20):
        self.nc = nc
        self.stacks = [es]
        self.E = {"pe": nc.tensor, "act": nc.scalar, "dve": nc.vector, "pool": nc.gpsimd, "sp": nc.sync}
        self.ops = []
        self.seq = {e: 0 for e in self.E}
        self.clock = {e: {} for e in self.E}
        self.esem = {e: es.enter_context(nc.semaphore("es_" + e)) for e in ("pe", "act", "dve", "pool")}
        self.dsem = {}
        for q in ("sp", "pool"):
            self.dsem[q] = [[es.enter_context(nc.semaphore("ds_%s_%d" % (q, i))), 0] for i in range(n_dma_sems)]
        self.dnext = {q: 0 for q in self.dsem}
        self.ntile = 0
        self.emitted = 0
        self.dma_barrier = 0
        self.cnt = {e: 0 for e in self.E}
        self.waited = {}
        self.stats = dict(nops=0, nwait=0)

    @property
    def es(self):
        return self.stacks[-1]

    def tile(self, name, shape, dtype=F32, space="sb"):
        self.ntile += 1
        return Tile(self, "%s_%d" % (name, self.ntile), shape, dtype, space)

    @contextlib.contextmanager
    def scope(self):
        sub = contextlib.ExitStack()
        self.stacks.append(sub)
        try:
            yield sub
        finally:
            self.flush()
            self.stacks.pop()
            sub.close()

    def carve(self, parent, name, off, shape, dtype=F32):
        t = Tile.__new__(Tile)
        t.name, t.shape, t.dtype, t.space = name, list(shape), dtype, parent.space
        n = 1
        for d_ in shape[1:]:
            n *= d_
        names = "abcdefg"[:len(parent.shape) - 1]
        flat = parent.h[:].rearrange("p %s -> p (%s)" % (" ".join(names), " ".join(names)))
        ap = flat[0:shape[0], off:off + n]
        if dtype != parent.dtype:
            ap = ap.bitcast(dtype)
        if len(shape) > 2:
            nm = "abcdefg"[:len(shape) - 1]
            kw = {nm[i]: shape[1 + i] for i in range(len(shape) - 2)}
            ap = ap.rearrange("p (%s) -> p %s" % (" ".join(nm), " ".join(nm)), **kw)
        t.h = ap
        t.wr, t.rd = {}, {}
        return t

    def _record(self, eng, fn, reads, writes, is_dma=False):
        op = Op()
        op.eng, op.fn, op.is_dma = eng, fn, is_dma
        op.signal, op.rank, op.sem, op.target, op.selfwait = False, 0, None, 0, None
        oid = len(self.ops)
        deps = set()
        ps_seen = {}
        for r, isw in [(r, False) for r in reads] + [(w, True) for w in writes]:
            if isinstance(r, Ref) and r.tile.space == "ps":
                ps_seen[id(r.tile)] = (r.tile, ps_seen.get(id(r.tile), (None, False))[1] or isw)
        for t, isw in ps_seen.values():
            acc = t.__dict__.get("acc", {})
            for f, (o, w) in acc.items():
                if f != eng or w or isw:
                    deps.add(o)
            t.acc = {eng: (oid, isw)}
        reads = [r for r in reads if isinstance(r, Ref) and r.tile.space != "ps"]
        writes = [w for w in writes if isinstance(w, Ref) and w.tile.space != "ps"]
        for r in reads:
            if not isinstance(r, Ref):
                continue
            for box, w in r.tile.wr.items():
                if _overlap(box, r.box):
                    deps.add(w)
        for w in writes:
            if not isinstance(w, Ref):
                continue
            for box, o in w.tile.wr.items():
                if _overlap(box, w.box):
                    deps.add(o)
            for (box, _e), o in w.tile.rd.items():
                if _overlap(box, w.box):
                    deps.add(o)
        for w in writes:
            if not isinstance(w, Ref):
                continue
            t = w.tile
            t.wr = {b: o for b, o in t.wr.items() if not _contains(w.box, b)}
            t.rd = {k: o for k, o in t.rd.items() if not _contains(w.box, k[0])}
            t.wr[w.box] = oid
        for r in reads:
            if not isinstance(r, Ref):
                continue
            r.tile.rd[(r.box, eng if not is_dma else ("dma", oid))] = oid
        clk = self.clock[eng]
        waits = []
        for d in sorted(deps):
            p = self.ops[d]
            if p.is_dma:
                if d < self.dma_barrier:
                    continue
                key = ("dma", d)
                if clk.get(key, -1) >= 0:
                    continue
                waits.append(d)
                clk[key] = 0
            else:
                if p.eng == eng and (eng == "pe" or not SAME_ENGINE_SYNC):
                    continue
                if clk.get(p.eng, -1) >= p.seq:
                    continue
                waits.append(d)
                if clk.get(p.eng, -1) < p.seq:
                    clk[p.eng] = p.seq
            for k, v in p.clock.items():
                if clk.get(k, -1) < v:
                    clk[k] = v
        op.waits = waits
        op.seq = self.seq[eng]
        self.seq[eng] += 1
        op.clock = dict(clk)
        if not is_dma:
            op.clock[eng] = op.seq
        self.ops.append(op)
        return op

    def op(self, eng, fn, reads=(), writes=()):
        return self._record(eng, fn, list(reads), list(writes))

    def dma(self, q, out, in_, **kw):
        oap = out.ap if isinstance(out, Ref) else out
        iap = in_.ap if isinstance(in_, Ref) else in_
        op = self._record(q, lambda e: e.dma_start(out=oap, in_=iap, **kw),
                          [in_] if isinstance(in_, Ref) else [], [out] if isinstance(out, Ref) else [], is_dma=True)
        i = self.dnext[q]
        self.dnext[q] = (i + 1) % len(self.dsem[q])
        ent = self.dsem[q][i]
        if ent[1] > 0:
            op.selfwait = (ent[0], ent[1])
        ent[1] += 16
        op.sem, op.target = ent[0], ent[1]
        return op

    def _wait(self, engname, key, val):
        wk = (engname, id(key))
        if self.waited.get(wk, -1) >= val:
            return
        self.waited[wk] = val
        self.E[engname].wait_ge(key, val)
        self.stats["nwait"] += 1

    def flush(self, final=False):
        pend = self.ops[self.emitted:]
        last = {}
        for op in pend:
            for d in op.waits:
                p = self.ops[d]
                if not p.is_dma:
                    assert d >= self.emitted, "dependency on pre-barrier op"
                    p.signal = True
            if not op.is_dma:
                last[op.eng] = op
        for op in last.values():
            op.signal = True
        for op in pend:
            if op.signal:
                self.cnt[op.eng] += 1
                op.rank = self.cnt[op.eng]
        for op in pend:
            eng = self.E[op.eng]
            need = {}
            for d in op.waits:
                p = self.ops[d]
                key, val = (p.sem, p.target) if p.is_dma else (self.esem[p.eng], p.rank)
                if need.get(id(key), (None, -1))[1] < val:
                    need[id(key)] = (key, val)
            if op.selfwait is not None:
                key, val = op.selfwait
                if need.get(id(key), (None, -1))[1] < val:
                    need[id(key)] = (key, val)
            for key, val in need.values():
                self._wait(op.eng, key, val)
            ins = op.fn(eng)
            if op.is_dma:
                ins.then_inc(op.sem, 16)
            elif op.signal:
                ins.then_inc(self.esem[op.eng], 1)
            op.fn = None
        self.stats["nops"] += len(pend)
        self.emitted = len(self.ops)
        engs = ["sp"] if final else ["pe", "act", "dve", "sp", "pool"]
        for e in engs:
            for f in ("pe", "act", "dve", "pool"):
                if self.cnt[f] > 0:
                    self._wait(e, self.esem[f], self.cnt[f])
            for q in self.dsem:
                for sem, tgt in self.dsem[q]:
                    if tgt > 0:
                        self._wait(e, sem, tgt)
        for e in self.E:
            for f in ("pe", "act", "dve", "pool"):
                self.clock[e][f] = self.seq[f] - 1
        self.dma_barrier = len(self.ops)

    def mm(self, out, lhsT, rhs, start=True, stop=True):
        return self.op("pe", lambda e: e.matmul(out.ap, lhsT.ap, rhs.ap, start=start, stop=stop), [lhsT, rhs], [out])

    def tr(self, out, in_, ident):
        return self.op("pe", lambda e: e.transpose(out.ap, in_.ap, ident.ap), [in_, ident], [out])

    def act(self, out, in_, func, bias=None, scale=None, accum=None, eng="act"):
        kw = {}
        rd = [in_]
        wr = [out]
        if bias is not None:
            kw["bias"] = bias.ap if isinstance(bias, Ref) else bias
            if isinstance(bias, Ref):
                rd.append(bias)
        if scale is not None:
            kw["scale"] = scale.ap if isinstance(scale, Ref) else scale
            if isinstance(scale, Ref):
                rd.append(scale)
        if accum is not None:
            kw["accum_out"] = accum.ap
            wr.append(accum)
        return self.op(eng, lambda e: e.activation(out=out.ap, in_=in_.ap, func=func, **kw), rd, wr)

    def tt(self, out, a, b, op, eng="dve"):
        return self.op(eng, lambda e: e.tensor_tensor(out=out.ap, in0=a.ap, in1=b.ap, op=op), [a, b], [out])

    def ts(self, out, a, s1, op0, s2=None, op1=None, eng="dve", accum=None):
        rd = [a]
        wr = [out]
        v1 = s1.ap if isinstance(s1, Ref) else s1
        v2 = s2.ap if isinstance(s2, Ref) else s2
        if isinstance(s1, Ref):
            rd.append(s1)
        if isinstance(s2, Ref):
            rd.append(s2)
        kw = {}
        if op1 is not None:
            kw["op1"] = op1
        if accum is not None:
            kw["accum_out"] = accum.ap
            wr.append(accum)
        return self.op(eng, lambda e: e.tensor_scalar(out=out.ap, in0=a.ap, scalar1=v1, scalar2=v2, op0=op0, **kw), rd, wr)

    def stt(self, out, a, scalar, b, op0, op1):
        rd = [a, b]
        v = scalar.ap if isinstance(scalar, Ref) else scalar
        if isinstance(scalar, Ref):
            rd.append(scalar)
        return self.op("dve", lambda e: e.scalar_tensor_tensor(out=out.ap, in0=a.ap, scalar=v, in1=b.ap, op0=op0, op1=op1), rd, [out])

    def copy(self, out, in_, eng="act"):
        if eng == "act":
            return self.op("act", lambda e: e.copy(out.ap, in_.ap), [in_], [out])
        return self.op(eng, lambda e: e.tensor_copy(out.ap, in_.ap), [in_], [out])

    def rmax(self, out, in_):
        return self.op("dve", lambda e: e.reduce_max(out.ap, in_.ap, axis=AX.X), [in_], [out])

    def rsum(self, out, in_):
        return self.op("dve", lambda e: e.reduce_sum(out.ap, in_.ap, axis=AX.X), [in_], [out])

    def memset(self, out, val, eng="dve"):
        return self.op(eng, lambda e: e.memset(out.ap, val), [], [out])


def _reref(ref, pattern, **kw):
    r = Ref.__new__(Ref)
    r.tile, r.box = ref.tile, ref.box
    r.ap = ref.ap.rearrange(pattern, **kw)
    return r


Ref.re = _reref

D = 1024
T = 1280
NU = 5
EPS = 1e-6
TT = [(0, 512), (512, 512), (1024, 256)]
NKEY = 1792
SLOTW = 4096
ATT_SCALE = 192.0 ** -0.5
NEG = -30000.0
WARM_ATT = 0
WARM_PROJ = 0
BOTH_SWAP = 0
FFN_POOL_ACC = 0
OUT_DELAY = 6
MOD_IDLE = 15
WARM_MIX = 0

C_ID, C_ONE, C_TGF, C_TGB, C_TMF, C_TMB, C_MPF, C_MPB, C_MAF, C_MAB, C_SLF, C_SLB = [i * 128 for i in range(12)]
NCST = 12 * 128


def make_consts():
    c = np.zeros((128, NCST), np.float32)
    s = np.arange(128)[:, None]
    t = np.arange(128)[None, :]
    le = (s <= t).astype(np.float32)
    ge = (s >= t).astype(np.float32)
    c[:, C_ID:C_ID + 128] = np.eye(128, dtype=np.float32)
    c[:, C_ONE:C_ONE + 128] = 1.0
    c[:, C_TGF:C_TGF + 128] = -le / 16.0
    c[:, C_TGB:C_TGB + 128] = -ge / 16.0
    c[:, C_TMF:C_TMF + 128] = -le
    c[:, C_TMB:C_TMB + 128] = -ge
    c[:, C_MPF:C_MPF + 128] = le
    c[:, C_MPB:C_MPB + 128] = ge
    c[:, C_MAF:C_MAF + 128] = np.where(s >= t, 0.0, -1e30)
    c[:, C_MAB:C_MAB + 128] = np.where(s <= t, 0.0, -1e30)
    c[127, C_SLF:C_SLF + 128] = 1.0
    c[0, C_SLB:C_SLB + 128] = 1.0
    return c


class Ring:
    def __init__(self, sc, nslot):
        self.sc = sc
        self.slots = [sc.tile("ring", [128, SLOTW], F32R) for _ in range(nslot)]
        self.i = 0
        self.pre = {}

    def load(self, dram3d, nk, W, q="pool", idx=None, key=None):
        if key is not None and key in self.pre:
            return self.pre.pop(key)
        reserved = {id(s_.t) for s_ in self.pre.values()}
        if idx is None:
            for _ in range(len(self.slots)):
                idx = self.i
                self.i = (self.i + 1) % len(self.slots)
                if id(self.slots[idx]) not in reserved:
                    break
        assert id(self.slots[idx]) not in reserved, "ring slot holds preloaded weights that were not consumed yet"
        t = self.slots[idx]
        dst = t[:, 0:nk * W].re("p (k n) -> p k n", k=nk)
        self.sc.dma(q, dst, dram3d)
        s_ = _Slot(t, nk, W)
        s_.idx = idx
        return s_

    def preload(self, key, dram3d, nk, W, idx=None):
        self.pre[key] = self.load(dram3d, nk, W, idx=idx)


class _Slot:
    def __init__(self, t, nk, W):
        self.t, self.nk, self.W = t, nk, W

    def w(self, kc, a, b, p0=0, p1=128):
        return self.t[p0:p1, kc * self.W + a: kc * self.W + b]


class PsPool:
    def __init__(self, banks, width):
        self.banks, self.width = banks, width
        self.slots = [(b, o) for b in banks for o in range(0, 512, width)]
        self.i = 0

    def get(self):
        b, o = self.slots[self.i]
        self.i = (self.i + 1) % len(self.slots)
        return _PsView(b, o)


class _PsView:
    def __init__(self, bank, off):
        self.bank, self.off = bank, off

    def __getitem__(self, idx):
        p, c = idx
        a, b, _ = c.indices(512 - self.off)
        return self.bank[p, self.off + a:self.off + b]


def build_program(stop=None, dumps=(), nslot=2):
    nc = bass.Bass("TRN2", target_bir_lowering=False)

    def din(name, shape):
        return nc.dram_tensor(name, list(shape), F32, kind="ExternalInput").ap()

    def dout(name, shape):
        return nc.dram_tensor(name, list(shape), F32, kind="ExternalOutput").ap()

    xin = din("xin", [T, D])
    cond2T = din("cond2T", [128, 8, 2])
    cstd = din("cst", [128, NCST])
    chaind = din("chain", [128, 2, NU])
    ginit = din("ginit", [2, NU, 4, 64, 128])
    minit = din("minit", [2, NU, 4, 64, 129])
    mminit = din("mminit", [128, 2, NU, 4])
    ropeT = din("ropeT", [128, T])
    ktab = din("ktab", [6, NKEY])
    qtab = din("qtab", [5, T])
    ckvc = din("ckvc", [512, 256])
    krc = din("krc", [512, 64])
    wmod = din("wmod", [2, D, 6144])
    bmodT = din("bmodT", [128, 2, 96])
    gvecT = din("gvecT", [128, 2, 4, 16])
    w0cat = din("w0cat", [D, 4096])
    wgd = din("wg", [33, 2, 512])
    gnd = din("gn", [128, 2, 128])
    bgd = din("bgates", [128, 16])
    woutd = din("wout", [2, D, D])
    wupd = din("wup", [2, D, 5632])
    convd = din("convT", [128, 2, 44, 4])
    wdownd = din("wdown", [2, 2816, D])
    w1in = din("w1in", [D, 768])
    wqbd = din("wqb", [384, 2048])
    wkvbd = din("wkvb", [256, 2048])
    gqkvd = din("gqkv", [128, 5])
    yout = dout("y", [T, D])
    gla_o = dout("gla_o", [2, NU, 4, 64, 128])
    mC_o = dout("mC_o", [2, NU, 4, 64, 129])
    mm_o = dout("mm_o", [1, 40])
    ckv_o = dout("ckv_o", [T, 256])
    kr_o = dout("kr_o", [T, 64])
    dbg_out = {}

    class Stop(Exception):
        pass

    es = contextlib.ExitStack()
    with es:
        sc = Sched(nc, es)
        tile = sc.tile
        x = tile("x", [128, 8, T])
        A = tile("A", [128, 8, T], F32R)
        B = tile("B", [128, 8, T], F32R)
        Af, Bf = A.v(F32), B.v(F32)
        ring = Ring(sc, nslot)
        cst = tile("cst", [128, NCST])
        onesR = tile("onesR", [128, 128], F32R)
        chain = tile("chain", [128, 2, NU])
        modv = tile("modv", [128, 2, 96])
        bmod = tile("bmod", [128, 2, 96])
        gv = tile("gv", [128, 2, 4, 16])
        A1 = tile("A1", [128, 2, 2, 16])
        GG = tile("GG", [128, 2, 2, 16])
        banks = [tile("ps", [128, 512], F32, "ps") for _ in range(8)]

        ident = cst[:, C_ID:C_ID + 128]
        ones = cst[:, C_ONE:C_ONE + 128]

        pG = PsPool(banks[0:4], 512)
        pS = banks[4]
        pQ = PsPool(banks[5:8] + banks[0:4], 512)
        pO, pD = banks[5], banks[6]
        pX = PsPool([banks[7]] + banks[0:3], 512)
        pF = PsPool(banks[0:4] + banks[5:8], 512)

        def dump(name, ref, shape):
            if name in dumps:
                o = dout("dbg_" + name, shape)
                sc.dma("sp", o, ref)
                dbg_out[name] = shape

        def phase_end(name):
            if stop == name:
                raise Stop()

        def norm_tiles():
            return dict(sq=[tile("sq", [128, 512], F32R) for _ in range(2)], R=[tile("R", [128, 512]) for _ in range(2)],
                        lnt=tile("lnt", [128, 512]), tmp=[tile("tmp512", [128, 512]) for _ in range(2)], c=[0, 0, 0])

        def compute_R(nt, src_fn, nch, dim, n, bank=None):
            bank = pS if bank is None else bank
            for ch in range(nch):
                nt["c"][0] += 1
                sq = nt["sq"][nt["c"][0] % len(nt["sq"])]
                sc.act(sq[:, 0:n], src_fn(ch), AF.Square)
                sc.mm(bank[:, 0:n], onesR[:, :], sq[:, 0:n], start=(ch == 0), stop=(ch == nch - 1))
            sc.act(nt["lnt"][:, 0:n], bank[:, 0:n], AF.Ln, bias=EPS, scale=1.0 / dim)
            nt["c"][1] += 1
            R = nt["R"][nt["c"][1] % len(nt["R"])]
            sc.act(R[:, 0:n], nt["lnt"][:, 0:n], AF.Exp, scale=-0.5)
            return R

        def shiftv(l, which, ch, cond):
            c = (which * 3) * 16 + ch * 2 + cond
            return modv[:, l, c:c + 1]

        def norm_mod(l, which, dst):
            with sc.scope():
                nt = norm_tiles()
                for ti, (c0, n) in enumerate(TT):
                    cond = 0 if ti < 2 else 1
                    R = compute_R(nt, lambda ch: x[:, ch, c0:c0 + n], 8, D, n)
                    for ch in range(8):
                        nt["c"][2] += 1
                        tm = nt["tmp"][nt["c"][2] % 2]
                        sc.tt(tm[:, 0:n], x[:, ch, c0:c0 + n], R[:, 0:n], ALU.mult)
                        k = ch * 2 + cond
                        sc.act(dst[:, ch, c0:c0 + n], tm[:, 0:n], AF.Identity,
                               bias=shiftv(l, which, ch, cond), scale=A1[:, l, which, k:k + 1])

        def post_pre(lp, wp, src, ln_, wn, dst, oproj=None):
            with sc.scope():
                sqs = [tile("sq", [128, 512], F32R) for _ in range(3)]
                tms = [tile("tmp512", [128, 512]) for _ in range(3)]
                Ra = [tile("Ra", [128, 512]) for _ in range(3)]
                Rb = [tile("Rb", [128, 512]) for _ in range(3)]
                lnts = [tile("lnt", [128, 512]) for _ in range(3)]

                if oproj is not None:
                    ol, osrc = oproj
                    s0_ = ring.load(wout_ap(ol, 0), 8, 512, key=("wout", ol, 0))
                    slots_ = [s0_, ring.load(wout_ap(ol, 1), 8, 512, idx=1 - s0_.idx)]
                    pend = []

                    def stats_mm(item):
                        nb_, ti_, n_ = item
                        sc.mm(banks[5 + ti_][:, 0:n_], onesR[:, :], sqs[ti_][:, 0:n_], start=(nb_ == 0), stop=(nb_ == 7))

                    for half in range(2):
                        slot = slots_[half]
                        for nb4 in range(4):
                            nb = half * 4 + nb4
                            for ti, (c0, n) in enumerate(TT):
                                ps = pG.get()
                                for kc in range(8):
                                    sc.mm(ps[:, 0:n], slot.w(kc, nb4 * 128, nb4 * 128 + 128), osrc[:, kc, c0:c0 + n],
                                          start=(kc == 0), stop=(kc == 7))
                                while len(pend) > 2:
                                    stats_mm(pend.pop(0))
                                sc.copy(A[:, nb, c0:c0 + n], ps[:, 0:n], "act")
                                sc.act(sqs[ti][:, 0:n], ps[:, 0:n], AF.Square)
                                pend.append((nb, ti, n))
                        if half == 0:
                            ring.preload(("wup", ol, 0), wup_ap(ol, 0), 8, 512)
                    while pend:
                        stats_mm(pend.pop(0))

                def tile_gen(ti):
                    c0, n = TT[ti]
                    cond = 0 if ti < 2 else 1
                    bA, bB = (banks[5 + ti], banks[ti]) if oproj is not None else (banks[ti], banks[3 + ti])
                    sq, tm, R, R2, lnt_ = sqs[ti], tms[ti], Ra[ti], Rb[ti], lnts[ti]
                    if oproj is None:
                        for ch in range(8):
                            sc.act(sq[:, 0:n], src[:, ch, c0:c0 + n], AF.Square)
                            sc.mm(bA[:, 0:n], onesR[:, :], sq[:, 0:n], start=(ch == 0), stop=(ch == 7))
                            yield
                    sc.act(lnt_[:, 0:n], bA[:, 0:n], AF.Ln, bias=EPS, scale=1.0 / D)
                    sc.act(R[:, 0:n], lnt_[:, 0:n], AF.Exp, scale=-0.5)
                    yield
                    for ch in range(8):
                        sc.tt(tm[:, 0:n], src[:, ch, c0:c0 + n], R[:, 0:n], ALU.mult)
                        k = ch * 2 + cond
                        sc.stt(x[:, ch, c0:c0 + n], tm[:, 0:n], GG[:, lp, wp, k:k + 1], x[:, ch, c0:c0 + n],
                               ALU.mult, ALU.add)
                        sc.act(sq[:, 0:n], x[:, ch, c0:c0 + n], AF.Square)
                        sc.mm(bB[:, 0:n], onesR[:, :], sq[:, 0:n], start=(ch == 0), stop=(ch == 7))
                        yield
                    sc.act(lnt_[:, 0:n], bB[:, 0:n], AF.Ln, bias=EPS, scale=1.0 / D)
                    sc.act(R2[:, 0:n], lnt_[:, 0:n], AF.Exp, scale=-0.5)
                    yield
                    for ch in range(8):
                        sc.tt(tm[:, 0:n], x[:, ch, c0:c0 + n], R2[:, 0:n], ALU.mult)
                        k = ch * 2 + cond
                        sc.act(dst[:, ch, c0:c0 + n], tm[:, 0:n], AF.Identity,
                               bias=shiftv(ln_, wn, ch, cond), scale=A1[:, ln_, wn, k:k + 1])
                        yield

                gens = [tile_gen(ti) for ti in range(3)]
                while gens:
                    for g in list(gens):
                        try:
                            next(g)
                        except StopIteration:
                            gens.remove(g)

        def post_norm(l, which, src, final=False):
            with sc.scope():
                sqs = [tile("sq", [128, 512], F32R) for _ in range(3)]
                tms = [tile("tmp512", [128, 512]) for _ in range(3)]
                Ra = [tile("Ra", [128, 512]) for _ in range(3)]
                lnts = [tile("lnt", [128, 512]) for _ in range(3)]
                ys = [tile("ys", [128, D]) for _ in range(3)] if final else None
                pT_ = PsPool(banks[3:8], 512)

                def tile_gen(ti):
                    c0, n = TT[ti]
                    cond = 0 if ti < 2 else 1
                    bA = banks[ti]
                    sq, tm, R, lnt_ = sqs[ti], tms[ti], Ra[ti], lnts[ti]
                    for ch in range(8):
                        sc.act(sq[:, 0:n], src[:, ch, c0:c0 + n], AF.Square)
                        sc.mm(bA[:, 0:n], onesR[:, :], sq[:, 0:n], start=(ch == 0), stop=(ch == 7))
                        yield
                    sc.act(lnt_[:, 0:n], bA[:, 0:n], AF.Ln, bias=EPS, scale=1.0 / D)
                    sc.act(R[:, 0:n], lnt_[:, 0:n], AF.Exp, scale=-0.5)
                    yield
                    for ch in range(8):
                        sc.tt(tm[:, 0:n], src[:, ch, c0:c0 + n], R[:, 0:n], ALU.mult)
                        k = ch * 2 + cond
                        sc.stt(x[:, ch, c0:c0 + n], tm[:, 0:n], GG[:, l, which, k:k + 1], x[:, ch, c0:c0 + n],
                               ALU.mult, ALU.add)
                        yield
                    if final:
                        for tt in range(c0 // 128, (c0 + n) // 128):
                            yt_ = ys[ti]
                            for half in range(2):
                                ps = pT_.get()
                                for j in range(4):
                                    ch = half * 4 + j
                                    sc.tr(ps[:, j * 128:(j + 1) * 128], x[:, ch, tt * 128:(tt + 1) * 128], ident)
                                sc.copy(yt_[:, half * 512:(half + 1) * 512], ps[:, 0:512], "act" if half == 0 else "dve")
                                yield
                            sc.dma("sp", yout[tt * 128:(tt + 1) * 128, :], yt_[:])

                gens = [tile_gen(ti) for ti in range(3)]
                while gens:
                    for g in list(gens):
                        try:
                            next(g)
                        except StopIteration:
                            gens.remove(g)

        def wout_ap(l, half):
            return woutd[l, :, half * 512:(half + 1) * 512].rearrange("(k p) n -> p k n", p=128)

        def wup_ap(l, s):
            return wupd[l, :, s * 512:(s + 1) * 512].rearrange("(k p) n -> p k n", p=128)

        def out_proj(l, src, dst):
            s0_ = ring.load(wout_ap(l, 0), 8, 512, key=("wout", l, 0))
            slots_ = [s0_, ring.load(wout_ap(l, 1), 8, 512, idx=1 - s0_.idx)]
            for half in range(2):
                slot = slots_[half]
                for nb4 in range(4):
                    nb = half * 4 + nb4
                    for (c0, n) in TT:
                        ps = pG.get()
                        for kc in range(8):
                            sc.mm(ps[:, 0:n], slot.w(kc, nb4 * 128, nb4 * 128 + 128), src[:, kc, c0:c0 + n],
                                  start=(kc == 0), stop=(kc == 7))
                        sc.copy(dst[:, nb, c0:c0 + n], ps[:, 0:n], "act")
                if half == 0:
                    ring.preload(("wup", l, 0), wup_ap(l, 0), 8, 512)

        c2 = tile("c2", [128, 8, 2])
        scond = tile("scond", [128, 8, 2], F32R)
        mrow_ref = [None]

        def mod_gen(slabs, idle=0):
            for (l, sl) in slabs:
                slot = ring.load(wmod[l, :, sl * 512:(sl + 1) * 512].rearrange("(k p) n -> p k n", p=128), 8, 512, idx=mod_slot[0])
                mod_inflight[0] = True
                for _ in range(idle):
                    yield
                for kc in range(8):
                    sc.mm(pS[0:2, 0:512], scond[:, kc, :], slot.w(kc, 0, 512), start=(kc == 0), stop=(kc == 7))
                yield
                sc.copy(mrow_ref[0][:], pS[0:2, 0:512], "act")
                yield
                for j4 in range(4):
                    sc.tr(pS[:, j4 * 2:j4 * 2 + 2], mrow_ref[0][0:2, j4 * 128:(j4 + 1) * 128], cst[0:2, C_ID:C_ID + 2])
                c0_ = sl * 8
                sc.tt(modv[:, l, c0_:c0_ + 8], pS[:, 0:8], bmod[:, l, c0_:c0_ + 8], ALU.add)
                j = sl // 2
                if sl % 2 == 1 and j in (1, 4):
                    which = 0 if j == 1 else 1
                    sc.stt(A1[:, l, which, :], modv[:, l, j * 16:(j + 1) * 16], 1.0, gv[:, l, which * 2, :], ALU.add, ALU.mult)
                if sl % 2 == 1 and j in (2, 5):
                    which = 0 if j == 2 else 1
                    sc.tt(GG[:, l, which, :], modv[:, l, j * 16:(j + 1) * 16], gv[:, l, which * 2 + 1, :], ALU.mult)
                mod_inflight[0] = False
                yield

        def take(g, n):
            for _ in range(n):
                try:
                    next(g)
                except StopIteration:
                    return
                yield

        modbg = [None]
        mod_slot = [None]
        mod_inflight = [False]

        def mod_settle():
            while mod_inflight[0] and modbg[0] is not None:
                try:
                    next(modbg[0])
                except StopIteration:
                    modbg[0] = None

        def ffn(l):
            with sc.scope():
                U = [tile("U", [128, NU, 258]) for _ in range(2)]
                ft = [tile("ft", [128, NU, 256]) for _ in range(2)]
                actg = tile("actg", [128, 4, T], F32R)
                cv = tile("cv", [128, 44, 4])
                tmpw = [tile("tmpw", [128, 512]) for _ in range(2)] if FFN_POOL_ACC else None
                wi = [0]
                sc.dma("sp", cv[:], convd[:, l, :, :])
                for u_ in U:
                    sc.memset(u_[:], 0.0)
                for grp in range(6):
                    npair = 4 if grp < 5 else 2
                    for half in range(npair // 2):
                        s = grp * 2 + half
                        slot = ring.load(wup_ap(l, s), 8, 512, key=("wup", l, s))
                        for pp in range(2):
                            j = s * 2 + pp
                            jj = j - grp * 4
                            for bi, coff in ((1, 256 + pp * 128), (0, pp * 128)):
                                Ub = U[bi]
                                for ti, (c0, n) in enumerate(TT):
                                    ps = pF.get()
                                    for kc in range(8):
                                        sc.mm(ps[:, 0:n], slot.w(kc, coff, coff + 128), B[:, kc, c0:c0 + n],
                                              start=(kc == 0), stop=(kc == 7))
                                    u0, nu = c0 // 256, n // 256
                                    sc.copy(Ub[:, u0:u0 + nu, 1:257], ps[:, 0:n].re("p (a b) -> p a b", a=nu), "act")
                                sc.tt(Ub[:, 1:5, 0:1], Ub[:, 0:4, 256:257], chain[:, 0, 1:5].re("p (a b) -> p a b", b=1), ALU.mult)
                                sc.tt(Ub[:, 0:4, 257:258], Ub[:, 1:5, 1:2], chain[:, 1, 0:4].re("p (a b) -> p a b", b=1), ALU.mult)
                                blk = j if bi == 0 else 22 + j
                                t1 = ft[bi]
                                sc.act(t1[:], Ub[:, :, 1:257], AF.Identity, bias=cv[:, blk, 3:4], scale=cv[:, blk, 1:2])
                                sc.stt(t1[:], Ub[:, :, 0:256], cv[:, blk, 0:1], t1[:], ALU.mult, ALU.add)
                                sc.stt(t1[:], Ub[:, :, 2:258], cv[:, blk, 2:3], t1[:], ALU.mult, ALU.add)
                            sc.act(ft[1][:], ft[1][:], AF.Silu)
                            sc.tt(actg[:, jj, :].re("p (a b) -> p a b", a=NU), ft[1][:], ft[0][:], ALU.mult)
                    slotd = ring.load(wdownd[l, grp * 512:grp * 512 + npair * 128, :].rearrange("(k p) n -> p k n", p=128),
                                      npair, 1024)
                    if l == 0 and grp == 5:
                        ring.preload(("w1in", 0), w1in[:, 0:512].rearrange("(k p) n -> p k n", p=128), 8, 512)
                    for nb in range(8):
                        for (c0, n) in TT:
                            ps = pF.get()
                            for kk in range(npair):
                                sc.mm(ps[:, 0:n], slotd.w(kk, nb * 128, nb * 128 + 128), actg[:, kk, c0:c0 + n],
                                      start=(kk == 0), stop=(kk == npair - 1))
                            if grp == 0:
                                sc.copy(A[:, nb, c0:c0 + n], ps[:, 0:n], "act")
                            elif FFN_POOL_ACC:
                                wi[0] += 1
                                tw = tmpw[wi[0] % 2]
                                sc.copy(tw[:, 0:n], ps[:, 0:n], "act")
                                sc.tt(A[:, nb, c0:c0 + n], tw[:, 0:n], Af[:, nb, c0:c0 + n], ALU.add, eng="pool")
                            else:
                                sc.tt(A[:, nb, c0:c0 + n], ps[:, 0:n], Af[:, nb, c0:c0 + n], ALU.add)
                if l == 0:
                    ring.preload(("w1in", 1), w1in[:, 512:768].rearrange("(k p) n -> p k n", p=128), 8, 256)
            if l == 0:
                post_pre(0, 1, Af, 1, 0, A)
            else:
                post_norm(l, 1, Af, final=True)

        def mixer0():
            qk = tile("qk", [128, T], F32R)
            qkf = qk.v(F32)
            ktok = tile("ktok", [128, 10, 64])
            vtok = tile("vtok", [128, 10, 130], F32R)
            gate = tile("gate", [128, 10, 128])
            mixer_lra = [None]
            mixer_cr = [None]
            Sinit = [[tile("Sinit", [64, 129]) for _ in range(2)] for _ in range(2)]
            Sst = [[tile("Sst", [64, 130], F32R) for _ in range(2)] for _ in range(2)]
            mrep = [[tile("mrep", [128, 1]) for _ in range(2)] for _ in range(2)]
            gn = tile("gn", [128, 2, 128])
            bgt = tile("bgt", [128, 16])
            mmi = tile("mmi", [128, 2, NU, 4])
            gsb = tile("gsb", [128, 10, 16])
            lnf = tile("lnf", [128, 10, 8])
            bsb = tile("bsb", [128, 10, 8])
            csb = tile("csb", [128, 10, 8])
            mcol = tile("mcol", [128, 40])
            sm = {}
            si = [0]

            def ofw(chn, c, wr=False):
                return (B if wr else Bf)[:, chn, c * 128:(c + 1) * 128]

            sc.dma("sp", gn[:], gnd)
            sc.dma("sp", bgt[:], bgd)
            sc.dma("sp", mmi[:], mminit)
            def fill(out, src, val):
                sc.act(out, src, AF.Identity, bias=float(val), scale=0.0)

            fill(vtok[:, :, 128:129], x[:, 0, 0:10].re("p (a b) -> p a b", b=1), 1.0)
            fill(vtok[:, :, 129:130], x[:, 0, 0:10].re("p (a b) -> p a b", b=1), 0.0)
            for a_ in Sst:
                for b_ in a_:
                    fill(b_[:], x[0:64, 0, 0:130], 0.0)
            for a_ in mrep:
                for b_ in a_:
                    sc.memset(b_[:], 0.0)

            if stop == "mixsetup":
                raise Stop()

            def w0_used(slot_idx):
                return 480 if slot_idx == 0 else (464 if slot_idx == 4 else 448)

            def w0_ap(slot_idx):
                return w0cat[:, slot_idx * 512:slot_idx * 512 + w0_used(slot_idx)].rearrange("(k p) n -> p k n", p=128)

            def project(slot_idx, is_gla, head):
                mod_settle()
                used = w0_used(slot_idx)
                slot = ring.load(w0_ap(slot_idx), 8, used, idx=slot_idx % 2, key=("w0", slot_idx))
                mod_slot[0] = slot_idx % 2
                qs, ks = (0.125, 1.0) if is_gla else (1.0, 0.125)
                for (c0, n) in TT:
                    ps = pG.get()
                    for kc in range(8):
                        sc.mm(ps[:, 0:n], slot.w(kc, 0, 128), A[:, kc, c0:c0 + n], start=(kc == 0), stop=(kc == 7))
                    sc.act(qk[0:64, c0:c0 + n], ps[0:64, 0:n], AF.Copy, scale=qs)
                    sc.act(qk[64:128, c0:c0 + n], ps[64:128, 0:n], AF.Copy, scale=ks)
                    if slot_idx == 0:
                        ps2 = pG.get()
                        for kc in range(8):
                            sc.mm(ps2[0:32, 0:n], slot.w(kc, 448, 480), A[:, kc, c0:c0 + n], start=(kc == 0), stop=(kc == 7))
                        sc.copy(mixer_lra[0][0:32, c0:c0 + n], ps2[0:32, 0:n], "dve")
                if stop == "projF":
                    raise Stop()
                NT = 336 if (not is_gla and head == 0) else 320
                for tt in range(10):
                    ps = pG.get()
                    for kc in range(8):
                        sc.mm(ps[:, 0:NT], A[:, kc, tt * 128:(tt + 1) * 128], slot.w(kc, 128, 128 + NT),
                              start=(kc == 0), stop=(kc == 7))
                    sc.act(ktok[:, tt, :], ps[:, 0:64], AF.Copy, scale=ks)
                    sc.act(gate[:, tt, :], ps[:, 192:320], AF.Silu if is_gla else AF.Sigmoid)
                    sc.copy(vtok[:, tt, 0:128], ps[:, 64:192], "dve")
                    if NT == 336:
                        sc.tt(gsb[:, tt, :], ps[:, 320:336], bgt[:, :], ALU.add)

            tasks = []

            def interleave(gens):
                tasks[:] = list(gens)
                while tasks:
                    for g in list(tasks):
                        try:
                            next(g)
                        except StopIteration:
                            tasks.remove(g)

            def both(ga, gb, res):
                act = [gb, ga] if BOTH_SWAP else [ga, gb]
                while act:
                    for g in list(act):
                        if g is None:
                            act.remove(g)
                            continue
                        try:
                            next(g)
                        except StopIteration as e_:
                            if g is ga:
                                res["v"] = e_.value
                            act.remove(g)
                    yield

            def warm_gen():
                while True:
                    for _w in range(WARM_MIX):
                        sc.mm(banks[3][:, 0:512], onesR[:, :], A[:, 0, 0:512])
                    yield

            def run_head(chains):
                tasks[:] = list(chains)
                while tasks:
                    for g in list(tasks):
                        try:
                            next(g)
                        except StopIteration:
                            tasks.remove(g)
                    if modbg[0] is not None:
                        try:
                            next(modbg[0])
                        except StopIteration:
                            modbg[0] = None

            def delayed(n, g):
                for _ in range(n):
                    yield
                yield from g

            def out_stage(smt, osum, c, is_gla, head):
                y1 = smt("y1", [128, 128])
                ss = smt("ss", [128, 1])
                sc.act(y1[:], osum[:, 0:128], AF.Square, accum=ss[:])
                lr_ = smt("lr_", [128, 1])
                sc.act(lr_[:], ss[:], AF.Ln, bias=EPS, scale=1.0 / 128)
                rs = smt("rs", [128, 1])
                sc.act(rs[:], lr_[:], AF.Exp, scale=-0.5)
                yield
                sc.stt(y1[:], osum[:, 0:128], rs[:], gn[:, 0 if is_gla else 1, :], ALU.mult, ALU.mult)
                sc.tt(y1[:], y1[:], gate[:, c, :], ALU.mult)
                yield
                pt = pQ.get()
                sc.tr(pt[:, 0:128], y1[:], ident)
                chn = head if is_gla else 4 + head
                sc.copy(B[:, chn, c * 128:(c + 1) * 128], pt[:, 0:128], "act")
                yield

            def chunk_order(d):
                return list(range(10)) if d == 0 else list(range(9, -1, -1))

            def cslice(d, a, b):
                o = a if d == 0 else b
                return cst[:, o:o + 128]

            def mk_smt(prefix):
                def f(name, shape, nbuf=1, par=0, dtype=F32):
                    key = prefix + name
                    if key not in sm:
                        sm[key] = [tile(key, shape, dtype) for _ in range(nbuf)]
                    return sm[key][par % nbuf]
                return f

            def gla_chain(i, d, wg):
                smt = mk_smt("g%d_" % d)
                triG = mixer_cr[0][:, d * 128:(d + 1) * 128]
                maskP = cslice(d, C_MPF, C_MPB)
                order = chunk_order(d)
                st = {"cur": 0}

                def PRE(k):
                    c = order[k]
                    cs = slice(c * 128, (c + 1) * 128)
                    if ((c % 2 == 0) if d == 0 else (c % 2 == 1)):
                        sc.dma("sp", Sinit[d][(c // 2) % 2][:, 0:128], ginit[d, c // 2, i, :, :])
                    pz = pQ.get()
                    sc.mm(pz[:, 0:128], mixer_lra[0][0:33, cs], wg[0:33, d, :])
                    lnv = smt("lnv", [128, 128], dtype=F32R)
                    e1 = lnv.v(F32)
                    sc.act(lnv[:], pz[:, 0:128], AF.Exp, scale=-1.0)
                    yield
                    sc.act(lnv[:], e1[:], AF.Ln, bias=1.0)
                    yield
                    pbT = pQ.get()
                    sc.mm(pbT[:, 0:128], lnv[:, :], triG)
                    eqk = smt("eqk", [128, 128])
                    sc.act(eqk[0:64, :], pbT[0:64, 0:128], AF.Exp)
                    sc.act(eqk[64:128, :], pbT[64:128, 0:128], AF.Exp, scale=-1.0)
                    yield
                    pbt = pQ.get()
                    sc.mm(pbt[:, 0:64], triG, lnv[:, 0:64])
                    ekt = smt("ekt", [128, 64])
                    sc.act(ekt[:], pbt[:, 0:64], AF.Exp, scale=-1.0)
                    yield
                    qt = smt("qt", [64, 128], 2, k, dtype=F32R)
                    sc.tt(qt[:], qkf[0:64, cs], eqk[0:64, :], ALU.mult)
                    kt = smt("kt", [64, 128], dtype=F32R)
                    sc.tt(kt[:], qkf[64:128, cs], eqk[64:128, :], ALU.mult)
                    ktk = smt("ktk", [128, 64], 2, k, dtype=F32R)
                    sc.tt(ktk[:], ktok[:, c, :], ekt[:], ALU.mult)
                    yield
                    psT = pQ.get()
                    sc.mm(psT[:, 0:128], kt[:, :], qt[:, :])
                    P = smt("P", [128, 128], 2, k, dtype=F32R)
                    sc.tt(P[:], psT[:, 0:128], maskP, ALU.mult)
                    yield
                    ecl = smt("ecl", [64, 1], 2, k)
                    sc.copy(ecl[:], eqk[0:64, 127:128] if d == 0 else eqk[0:64, 0:1], "dve")
                    return dict(qt=qt, P=P, ktk=ktk, ecl=ecl)

                def POST(k, pre):
                    c = order[k]
                    u = c // 2
                    first = (c % 2 == 0) if d == 0 else (c % 2 == 1)
                    last = not first
                    cur = st["cur"]
                    if first:
                        Si = Sinit[d][u % 2]
                        sc.stt(Sst[d][1 - cur][:, 0:128], Sst[d][cur].v(F32)[:, 0:128], chain[0:64, d, u:u + 1],
                               Si[:, 0:128], ALU.mult, ALU.add)
                        cur = 1 - cur
                    S = Sst[d][cur]
                    qt, P, ktk, ecl = pre["qt"], pre["P"], pre["ktk"], pre["ecl"]
                    po = pQ.get()
                    sc.mm(po[:, 0:128], qt[:, :], S[:, 0:128], start=True, stop=False)
                    sc.mm(po[:, 0:128], P[:, :], vtok[:, c, 0:128], start=False, stop=True)
                    pdS = pQ.get()
                    sc.mm(pdS[0:64, 0:128], ktk[:, :], vtok[:, c, 0:128])
                    yield
                    Sn = Sst[d][1 - cur]
                    sc.tt(Sn[:, 0:128], pdS[0:64, 0:128], S.v(F32)[:, 0:128], ALU.add)
                    sc.ts(Sn[:, 0:128], Sn.v(F32)[:, 0:128], ecl[:], ALU.mult)
                    st["cur"] = 1 - cur
                    if last:
                        sc.dma("sp", gla_o[d, u, i, :, :], Sn.v(F32)[:, 0:128])
                    yield
                    if k < 5:
                        sc.copy(ofw(i, c, True), po[:, 0:128], "act")
                        yield
                    else:
                        osum = smt("osum", [128, 128], 2, k)
                        sc.tt(osum[:], po[:, 0:128], ofw(i, c), ALU.add)
                        yield
                        tasks.append(delayed(OUT_DELAY, out_stage(smt, osum, c, True, i)))

                cur_pre = yield from PRE(0)
                for k in range(10):
                    res = {}
                    yield from both(PRE(k + 1) if k + 1 < 10 else None, POST(k, cur_pre), res)
                    cur_pre = res.get("v")

            def gla_head(i, wg):
                project(i, True, i)
                ring.preload(("w0", i + 1), w0_ap(i + 1), 8, w0_used(i + 1), idx=(i + 1) % 2)
                sc.dma("pool", wg[:], wgd[:, :, i * 128:(i + 1) * 128])
                run_head([gla_chain(i, 0, wg), gla_chain(i, 1, wg)])

            def mlstm_gates():
                e8 = tile("e8", [128, 10, 8])
                sc.act(e8[:], gsb[:, :, 8:16], AF.Exp, scale=-1.0)
                sc.act(lnf[:], e8[:], AF.Ln, bias=1.0)
                for d in range(2):
                    triM = cslice(d, C_TMF, C_TMB)
                    for tt in range(10):
                        pb = pQ.get()
                        sc.mm(pb[:, 0:4], triM, lnf[:, tt, d * 4:(d + 1) * 4])
                        sc.copy(bsb[:, tt, d * 4:(d + 1) * 4], pb[:, 0:4], "act")
                sc.tt(csb[:], gsb[:, :, 0:8], bsb[:], ALU.subtract)

            def mlstm_chain(i, d):
                smt = mk_smt("m%d_" % d)
                maskadd = cslice(d, C_MAF, C_MAB)
                sel = cslice(d, C_SLF, C_SLB)
                kk = d * 4 + i
                order = chunk_order(d)
                st = {"cur": 0}

                def PRE(k):
                    c = order[k]
                    cs = slice(c * 128, (c + 1) * 128)
                    if ((c % 2 == 0) if d == 0 else (c % 2 == 1)):
                        sc.dma("sp", Sinit[d][(c // 2) % 2][:], minit[d, c // 2, i, :, :])
                    ccol = csb[:, c, kk:kk + 1]
                    bcol = bsb[:, c, kk:kk + 1]
                    kTc = smt("kTc", [64, 128], 2, k, dtype=F32R)
                    sc.copy(kTc[:], qkf[64:128, cs], "act")
                    diagc = smt("diagc", [128, 128])
                    sc.ts(diagc[:], ident, ccol, ALU.mult)
                    yield
                    pcb = pQ.get()
                    sc.mm(pcb[:, 0:128], ones, diagc[:, :])
                    Dm = smt("Dm", [128, 128], 2, k)
                    sc.stt(Dm[:], pcb[:, 0:128], bcol, maskadd, ALU.add, ALU.add)
                    yield
                    rmx = smt("rmx", [128, 1], 2, k)
                    sc.rmax(rmx[:], Dm[:])
                    yield
                    return dict(kTc=kTc, Dm=Dm, rmx=rmx)

                def POST(k, pre):
                    c = order[k]
                    cs = slice(c * 128, (c + 1) * 128)
                    u = c // 2
                    first = (c % 2 == 0) if d == 0 else (c % 2 == 1)
                    last = not first
                    cur = st["cur"]
                    ccol = csb[:, c, kk:kk + 1]
                    bcol = bsb[:, c, kk:kk + 1]
                    kTc, Dm, rmx = pre["kTc"], pre["Dm"], pre["rmx"]
                    if first:
                        Si = Sinit[d][u % 2]
                        sc.stt(Sst[d][1 - cur][:, 0:129], Sst[d][cur].v(F32)[:, 0:129], chain[0:64, d, u:u + 1], Si[:], ALU.mult, ALU.add)
                        sc.stt(mrep[d][1 - cur][:], mrep[d][cur][:], chain[:, d, u:u + 1], mmi[:, d, u, i:i + 1],
                               ALU.mult, ALU.add)
                        cur = 1 - cur
                    Cg = Sst[d][cur]
                    mr = mrep[d][cur]
                    bm = smt("bm", [128, 1])
                    sc.tt(bm[:], bcol, mr[:], ALU.add)
                    mb = smt("mb", [128, 2])
                    mbf = mb
                    sc.tt(mb[:, 0:1], bm[:], rmx[:], ALU.max)
                    sc.copy(mb[:, 1:2], bcol, "dve")
                    yield
                    psl = pQ.get()
                    sc.mm(psl[:, 0:2], sel, mb[:, :])
                    mbn = smt("mbn", [128, 2])
                    sc.copy(mbn[:], psl[:, 0:2], "act")
                    yield
                    dlt = smt("dlt", [128, 1])
                    sc.tt(dlt[:], mbn[:, 1:2], mbn[:, 0:1], ALU.subtract)
                    t2 = smt("t2", [128, 1])
                    sc.tt(t2[:], dlt[:], mr[:], ALU.add)
                    sc.copy(mrep[d][1 - cur][:], mbn[:, 0:1], "dve")
                    nm = smt("nm", [128, 1])
                    sc.ts(nm[:], mbf[:, 0:1], -1.0, ALU.mult)
                    yield
                    srcw = smt("srcw", [128, 1])
                    sc.act(srcw[:], ccol, AF.Exp, bias=dlt[:])
                    cw = smt("cw", [128, 1])
                    sc.act(cw[:], t2[:], AF.Exp)
                    Wi = smt("Wi", [128, 128])
                    sc.act(Wi[:], Dm[:], AF.Exp, bias=nm[:])
                    winter = smt("winter", [128, 1])
                    sc.act(winter[:], bm[:], AF.Exp, bias=nm[:])
                    enm = smt("enm", [128, 1])
                    sc.act(enm[:], nm[:], AF.Exp)
                    yield
                    ksw = smt("ksw", [128, 64], dtype=F32R)
                    sc.ts(ksw[:], ktok[:, c, :], srcw[:], ALU.mult)
                    pit = pG.get()
                    sc.mm(pit[:, 0:130], qk[0:64, cs], Cg[:, 0:130])
                    its = smt("its", [128, 129])
                    sc.act(its[:], pit[:, 0:129], AF.Copy, scale=winter[:])
                    yield
                    pdC = pG.get()
                    sc.mm(pdC[0:64, 0:130], ksw[:, :], vtok[:, c, 0:130])
                    Cn = Sst[d][1 - cur]
                    sc.stt(Cn[:], Cg.v(F32)[:], cw[0:64, :], pdC[0:64, 0:130], ALU.mult, ALU.add)
                    st["cur"] = 1 - cur
                    if last:
                        sc.dma("sp", mC_o[d, u, i, :, :], Cn.v(F32)[:, 0:129])
                        col = (d * NU + u) * 4 + i
                        sc.copy(mcol[:, col:col + 1], mbn[:, 0:1], "dve")
                    yield
                    pqk = pQ.get()
                    sc.mm(pqk[:, 0:128], qk[0:64, cs], kTc[:, :])
                    qkw = smt("qkw", [128, 128])
                    sc.tt(qkw[:], pqk[:, 0:128], Wi[:], ALU.mult)
                    yield
                    pT = pQ.get()
                    sc.tr(pT[:, 0:128], qkw[:], ident)
                    qkT = smt("qkT", [128, 128], dtype=F32R)
                    sc.copy(qkT[:], pT[:, 0:128], "act")
                    yield
                    pin = pG.get()
                    sc.mm(pin[:, 0:130], qkT[:, :], vtok[:, c, 0:130])
                    nd = smt("nd", [128, 129])
                    sc.tt(nd[:], pin[:, 0:129], its[:], ALU.add)
                    yield
                    nden = smt("nden", [128, 1])
                    sc.ts(nden[:], nd[:, 128:129], -1.0, ALU.mult)
                    aden = smt("aden", [128, 1])
                    sc.tt(aden[:], nden[:], nd[:, 128:129], ALU.max)
                    dd = smt("dd", [128, 1])
                    sc.tt(dd[:], aden[:], enm[:], ALU.max)
                    rden = smt("rden", [128, 1])
                    sc.op("dve", lambda e, o=rden[:], i_=dd[:]: e.reciprocal(o.ap, i_.ap), [dd[:]], [rden[:]])
                    yield
                    if k < 5:
                        sc.ts(ofw(4 + i, c, True), nd[:, 0:128], rden[:], ALU.mult)
                        yield
                    else:
                        osum = smt("osum", [128, 128], 2, k)
                        sc.stt(osum[:], nd[:, 0:128], rden[:], ofw(4 + i, c), ALU.mult, ALU.add)
                        yield
                        tasks.append(delayed(OUT_DELAY, out_stage(smt, osum, c, False, i)))

                cur_pre = yield from PRE(0)
                for k in range(10):
                    res = {}
                    yield from both(PRE(k + 1) if k + 1 < 10 else None, POST(k, cur_pre), res)
                    cur_pre = res.get("v")

            def mlstm_head(i):
                project(4 + i, False, i)
                if i < 3:
                    ring.preload(("w0", 5 + i), w0_ap(5 + i), 8, w0_used(5 + i), idx=(5 + i) % 2)
                else:
                    ring.preload(("wout", 0, 0), wout_ap(0, 0), 8, 512, idx=0)
                if i == 0:
                    mlstm_gates()
                run_head([mlstm_chain(i, 0), mlstm_chain(i, 1)])

            with sc.scope():
                lraT = tile("lraT", [33, T], F32R)
                wgt = [tile("wgt", [33, 2, 128], F32R) for _ in range(1)]
                cstR = tile("cstR", [128, 256], F32R)
                sc.copy(cstR[:, 0:128], cst[:, C_TGF:C_TGF + 128], "act")
                sc.copy(cstR[:, 128:256], cst[:, C_TGB:C_TGB + 128], "act")
                mixer_cr[0] = cstR
                fill(lraT[32:33, :], x[32:33, 0, :], 1.0)
                mixer_lra[0] = lraT
                for i in range(4):
                    gla_head(i, wgt[0])
                    if stop == "gla%d" % i:
                        raise Stop()
            with sc.scope():
                for i in range(4):
                    mlstm_head(i)
                    if stop == "mlstm%d" % i:
                        raise Stop()
                sc.dma("sp", mm_o, mcol[0:1, :])

        def mixer1():
            def warm(n_=None):
                for _w in range(WARM_ATT if n_ is None else n_):
                    sc.mm(banks[3][:, 0:512], onesR[:, :], ckvC[:, 0, 0:512])

            gqkv = tile("gqkv", [128, 5])
            rope = tile("rope", [128, T])
            kmx = tile("kmx", [128, 4])
            kmax2 = tile("kmax2", [128, 1])
            krmax = tile("krmax", [128, 1])
            ckvC = tile("ckvC", [128, 2, 512], F32R)
            KRc = tile("KRc", [70, 512], F32R)
            QR = tile("QR", [70, T], F32R)
            sc.dma("sp", gqkv[:], gqkvd)
            sc.dma("sp", rope[:], ropeT)
            sc.dma("pool", KRc[64:70, :], ktab[:, 0:512])
            sc.dma("pool", QR[65:70, :], qtab)
            with sc.scope():
                cstage = tile("cstage", [128, 4, 256])
                kstage = tile("kstage", [128, 4, 64])
                sc.dma("sp", cstage[:], ckvc.rearrange("(a p) n -> p a n", p=128))
                sc.dma("sp", kstage[:], krc.rearrange("(a p) n -> p a n", p=128))
                for a_ in range(4):
                    for rc in range(2):
                        pt = pX.get()
                        sc.tr(pt[:, 0:128], cstage[:, a_, rc * 128:(rc + 1) * 128], ident)
                        sc.copy(ckvC[:, rc, a_ * 128:(a_ + 1) * 128], pt[:, 0:128], "act")
                    pt = pX.get()
                    sc.tr(pt[0:64, 0:128], kstage[:, a_, :], ident)
                    sc.copy(KRc[0:64, a_ * 128:(a_ + 1) * 128], pt[0:64, 0:128], "act")
            with sc.scope():
                qa_t = tile("qa_t", [128, 3, 512])
                kv_t = tile("kv_t", [128, 2, 512])
                kr_t = tile("kr_t", [128, 512])
                kr_s = tile("kr_s", [64, 512])
                kr_u = tile("kr_u", [64, 512])
                stg = tile("stg", [128, 4, 256])
                stg2 = tile("stg2", [128, 4, 64])
                nt = dict(sq=[tile("sq", [128, 512], F32R)], R=[tile("R", [128, 512])], lnt=tile("lnt", [128, 512]), c=[0, 0, 0])
                slot0 = ring.load(w1in[:, 0:512].rearrange("(k p) n -> p k n", p=128), 8, 512, key=("w1in", 0))
                slot1 = ring.load(w1in[:, 512:768].rearrange("(k p) n -> p k n", p=128), 8, 256, key=("w1in", 1))

                def fproj(slot, coff, c0, n):
                    ps = pX.get()
                    for kc in range(8):
                        sc.mm(ps[:, 0:n], slot.w(kc, coff, coff + 128), A[:, kc, c0:c0 + n], start=(kc == 0), stop=(kc == 7))
                    warm(WARM_PROJ)
                    return ps

                for ti, (c0, n) in enumerate(TT):
                    for b_ in range(3):
                        ps = fproj(slot0, b_ * 128, c0, n)
                        sc.copy(qa_t[:, b_, 0:n], ps[:, 0:n], "act")
                    ps = fproj(slot0, 384, c0, n)
                    sc.copy(kv_t[:, 0, 0:n], ps[:, 0:n], "act")
                    ps = fproj(slot1, 0, c0, n)
                    sc.copy(kv_t[:, 1, 0:n], ps[:, 0:n], "act")
                    ps = fproj(slot1, 128, c0, n)
                    sc.copy(kr_u[:, 0:n], ps[0:64, 0:n], "dve")
                    sc.tt(kr_t[:, 0:n], ps[:, 0:n], rope[:, c0:c0 + n], ALU.mult)
                    sc.copy(kr_s[:, 0:n], kr_t[64:128, 0:n], "act")
                    sc.tt(A[0:64, 5, c0:c0 + n], kr_t[0:64, 0:n], kr_s[:, 0:n], ALU.add)
                    sc.dma("pool", A[64:70, 5, c0:c0 + n], ktab[:, 512 + c0:512 + c0 + n])
                    R = compute_R(nt, lambda ch: qa_t[:, ch, 0:n], 3, 384, n)
                    for ch in range(3):
                        sc.stt(A[:, ch, c0:c0 + n], qa_t[:, ch, 0:n], gqkv[:, ch:ch + 1], R[:, 0:n], ALU.mult, ALU.mult)
                    R = compute_R(nt, lambda ch: kv_t[:, ch, 0:n], 2, 256, n)
                    for ch in range(2):
                        sc.stt(kv_t[:, ch, 0:n], kv_t[:, ch, 0:n], gqkv[:, 3 + ch:4 + ch], R[:, 0:n], ALU.mult, ALU.mult)
                        sc.copy(A[:, 3 + ch, c0:c0 + n], kv_t[:, ch, 0:n], "act")
                    na = n // 128
                    for a_ in range(na):
                        pt = pX.get()
                        for ch in range(2):
                            sc.tr(pt[:, ch * 128:(ch + 1) * 128], kv_t[:, ch, a_ * 128:(a_ + 1) * 128], ident)
                        sc.copy(stg[:, a_, :], pt[:, 0:256], "dve")
                        pt2 = pX.get()
                        sc.tr(pt2[:, 0:64], kr_u[:, a_ * 128:(a_ + 1) * 128], cst[0:64, C_ID:C_ID + 64])
                        sc.copy(stg2[:, a_, :], pt2[:, 0:64], "dve")
                    sc.dma("sp", ckv_o[c0:c0 + n, :].rearrange("(a p) n -> p a n", p=128), stg[:, 0:na, :])
                    sc.dma("sp", kr_o[c0:c0 + n, :].rearrange("(a p) n -> p a n", p=128), stg2[:, 0:na, :])
            phase_end("att_proj")
            with sc.scope():
                Kh = tile("Kh", [128, NKEY], F32R)
                Vh = tile("Vh", [128, 14, 128], F32R)
                sqa = tile("sqa", [128, 512], F32R)
                sqb = tile("sqb", [64, 512], F32R)
                sqq = tile("sqq", [128, 512], F32R)
                rdt = tile("rdt", [128, 512])
                lnd = tile("lnd", [128, 512])
                qrt = tile("qrt", [128, 512])
                qrs = tile("qrs", [64, 512])

                def Qn(c0, n):
                    return A[:, 6, c0:c0 + n]

                def PT(i, n):
                    return A[:, 7, i * 512:i * 512 + n]

                def ckv_src(rc, k0, n):
                    return ckvC[:, rc, k0:k0 + n] if k0 < 512 else A[:, 3 + rc, k0 - 512:k0 - 512 + n]

                def KR_src(k0, n, rows=70):
                    return KRc[0:rows, k0:k0 + n] if k0 < 512 else A[0:rows, 5, k0 - 512:k0 - 512 + n]

                KT4 = [(0, 512), (512, 512), (1024, 512), (1536, 256)]
                po_i = [0]
                norm_pend = []
                lnds = [lnd, tile("lnd2", [128, 512])]
                for ki, (k0, n) in enumerate(KT4):
                    sc.act(sqb[:, 0:n], (KRc.v(F32)[0:64, k0:k0 + n] if k0 < 512 else Af[0:64, 5, k0 - 512:k0 - 512 + n]), AF.Square)
                    sc.mm(pS[:, 0:n], onesR[0:64, :], sqb[:, 0:n])
                    sc.rmax(kmx[:, ki:ki + 1], pS[:, 0:n])
                sc.rmax(krmax[:], kmx[:, 0:4])
                slotkv = ring.load(wkvbd.rearrange("(k p) n -> p k n", p=128), 2, 2048, idx=0)
                slotq = None
                for h in range(8):
                    if h % 4 == 0:
                        slotq = ring.load(wqbd[:, (h // 4) * 1024:(h // 4 + 1) * 1024].rearrange("(k p) n -> p k n", p=128), 3, 1024, idx=1)
                    hh = h % 4
                    def k_chain():
                        for ki, (k0, n) in enumerate(KT4):
                            ps = pX.get()
                            for rc in range(2):
                                sc.mm(ps[:, 0:n], slotkv.w(rc, h * 256, h * 256 + 128), ckv_src(rc, k0, n), start=(rc == 0), stop=(rc == 1))
                            warm()
                            sc.copy(Kh[:, k0:k0 + n], ps[:, 0:n], "act")
                            sc.act(sqa[:, 0:n], ps[:, 0:n], AF.Square)
                            yield
                            sc.mm(pS[:, 0:n], onesR[:, :], sqa[:, 0:n])
                            sc.rmax(kmx[:, ki:ki + 1], pS[:, 0:n])
                            yield
                        sc.rmax(kmax2[:], kmx[:, 0:4])
                        sc.tt(kmax2[:], kmax2[:], krmax[:], ALU.add)
                        yield

                    def v_chain():
                        for kt in range(14):
                            ps = pX.get()
                            for rc in range(2):
                                sc.mm(ps[:, 0:128], ckv_src(rc, kt * 128, 128), slotkv.w(rc, h * 256 + 128, h * 256 + 256),
                                      start=(rc == 0), stop=(rc == 1))
                            sc.copy(Vh[:, kt, :], ps[:, 0:128], "dve")
                            yield

                    def q_chain():
                        for (c0, n) in TT:
                            ps = pX.get()
                            for kc in range(3):
                                sc.mm(ps[:, 0:n], slotq.w(kc, hh * 256, hh * 256 + 128), A[:, kc, c0:c0 + n], start=(kc == 0), stop=(kc == 2))
                            warm()
                            sc.copy(Qn(c0, n), ps[:, 0:n], "act")
                            sc.act(sqq[:, 0:n], ps[:, 0:n], AF.Square)
                            yield
                            ps2 = pX.get()
                            for kc in range(3):
                                sc.mm(ps2[:, 0:n], slotq.w(kc, hh * 256 + 128, hh * 256 + 256), A[:, kc, c0:c0 + n], start=(kc == 0), stop=(kc == 2))
                            warm()
                            sc.tt(qrt[:, 0:n], ps2[:, 0:n], rope[:, c0:c0 + n], ALU.mult)
                            yield
                            sc.copy(qrs[:, 0:n], qrt[64:128, 0:n], "act")
                            sc.tt(QR[0:64, c0:c0 + n], qrt[0:64, 0:n], qrs[:, 0:n], ALU.add)
                            sc.act(sqb[:, 0:n], QR.v(F32)[0:64, c0:c0 + n], AF.Square)
                            yield
                            sc.mm(pD[:, 0:n], onesR[:, :], sqq[:, 0:n], start=True, stop=False)
                            sc.mm(pD[:, 0:n], onesR[0:64, :], sqb[:, 0:n], start=False, stop=True)
                            sc.copy(QR[64:65, c0:c0 + n], pD[64:65, 0:n], "act")
                            yield

                    gens = [k_chain(), v_chain(), q_chain()]
                    while gens:
                        for g in list(gens):
                            try:
                                next(g)
                            except StopIteration:
                                gens.remove(g)
                    if h == 7:
                        ring.preload(("wout", 1, 0), wout_ap(1, 0), 8, 512, idx=1)
                    qrow = QR.v(F32)[64:65, :]
                    sc.act(QR[64:65, :], qrow, AF.Ln, scale=kmax2[64:65, :])
                    sc.act(QR[64:65, :], qrow, AF.Exp, scale=0.5)
                    sc.act(QR[64:65, :], qrow, AF.Copy, scale=-1.001)
                    for (c0, n, kbs) in ((0, 512, list(range(12))), (512, 512, list(range(12))), (1024, 256, [12, 13])):
                        po_i[0] += 1
                        pO = (banks[5], banks[3])[po_i[0] % 2]
                        lnd = lnds[po_i[0] % 2]
                        pend = None
                        for idx, kb in enumerate(kbs):
                            if idx == min(2, len(kbs) - 1) and norm_pend:
                                norm_pend.pop(0)()
                            ps = pX.get()
                            sc.mm(ps[:, 0:n], Kh[:, kb * 128:(kb + 1) * 128], Qn(c0, n), start=True, stop=False)
                            sc.mm(ps[:, 0:n], KR_src(kb * 128, 128), QR[0:70, c0:c0 + n], start=False, stop=True)
                            if pend is not None:
                                pidx, pkb, ppt = pend
                                sc.mm(pO[:, 0:n], Vh[:, pkb, :], ppt, start=(pidx == 0), stop=False)
                            pt = PT(idx % 2, n)
                            sc.act(pt, ps[:, 0:n], AF.Exp, scale=ATT_SCALE)
                            ptf = Af[:, 7, (idx % 2) * 512:(idx % 2) * 512 + n]
                            if idx == 0:
                                sc.copy(lnd[:, 0:n], ptf, "dve")
                            else:
                                sc.tt(lnd[:, 0:n], lnd[:, 0:n], ptf, ALU.add)
                            pend = (idx, kb, pt)
                        pidx, pkb, ppt = pend
                        sc.mm(pO[:, 0:n], Vh[:, pkb, :], ppt, start=(pidx == 0), stop=True)
                        sc.mm(pD[:, 0:n], ones, lnd[:, 0:n])

                        def finish(c0=c0, n=n, pO=pO, h=h):
                            sc.act(rdt[:, 0:n], pD[:, 0:n], AF.Ln)
                            sc.act(rdt[:, 0:n], rdt[:, 0:n], AF.Exp, scale=-1.0)
                            sc.tt(B[:, h, c0:c0 + n], pO[:, 0:n], rdt[:, 0:n], ALU.mult)

                        while norm_pend:
                            norm_pend.pop(0)()
                        norm_pend.append(finish)
                    while norm_pend:
                        norm_pend.pop(0)()
                    if stop == "att_h%d" % h:
                        raise Stop()

        try:
            sc.dma("sp", cst[:], cstd)
            sc.dma("sp", chain[:], chaind)
            sc.dma("sp", bmod[:], bmodT)
            sc.dma("sp", gv[:], gvecT)
            sc.copy(onesR[:], ones, "act")
            sc.dma("sp", c2[:], cond2T)
            sc.act(scond[:], c2[:], AF.Silu)
            with sc.scope():
                xt = [tile("xt", [128, D]) for _ in range(2)]
                mrow_ref[0] = tile("mrow", [2, 512])
                nt = norm_tiles()
                R3 = [tile("R3", [128, 512]) for _ in range(3)]
                g0 = mod_gen([(0, s_) for s_ in range(4)], idle=2)

                def adv(g, n_=1):
                    for _ in range(n_):
                        try:
                            next(g)
                        except StopIteration:
                            return

                for tt in range(10):
                    t = xt[tt % 2]
                    sc.dma("sp", t[:], xin[tt * 128:(tt + 1) * 128, :])
                    for half in range(2):
                        ps = pG.get()
                        for j in range(4):
                            ch = half * 4 + j
                            sc.tr(ps[:, j * 128:(j + 1) * 128], t[:, ch * 128:(ch + 1) * 128], ident)
                        sc.copy(x[:, half * 4:half * 4 + 4, tt * 128:(tt + 1) * 128],
                                ps[:, 0:512].re("p (a b) -> p a b", a=4), "act" if half == 0 else "dve")
                    adv(g0)
                    if tt in (3, 7, 9):
                        ti = {3: 0, 7: 1, 9: 2}[tt]
                        c0, n = TT[ti]
                        Rr = compute_R(nt, lambda ch, c0=c0, n=n: x[:, ch, c0:c0 + n], 8, D, n, bank=banks[5])
                        sc.copy(R3[ti][:, 0:n], Rr[:, 0:n], "dve")
                adv(g0, 1000)
                ring.preload(("w0", 0), w0cat[:, 0:480].rearrange("(k p) n -> p k n", p=128), 8, 480, idx=0)
                modbg[0] = mod_gen([(0, s_) for s_ in range(4, 12)] + [(1, s_) for s_ in range(12)], idle=MOD_IDLE)
                for ti, (c0, n) in enumerate(TT):
                    cond = 0 if ti < 2 else 1
                    for ch in range(8):
                        nt["c"][2] += 1
                        tm = nt["tmp"][nt["c"][2] % 2]
                        sc.tt(tm[:, 0:n], x[:, ch, c0:c0 + n], R3[ti][:, 0:n], ALU.mult)
                        k = ch * 2 + cond
                        sc.act(A[:, ch, c0:c0 + n], tm[:, 0:n], AF.Identity,
                               bias=shiftv(0, 0, ch, cond), scale=A1[:, 0, 0, k:k + 1])
            dump("x0", x[:], [128, 8, T])
            phase_end("norm0")
            dump("h0", Af[:], [128, 8, T])
            phase_end("norm0")
            with sc.scope():
                mrow_ref[0] = tile("mrow", [2, 512])
                mixer0()
                mod_slot[0] = None
                if modbg[0] is not None:
                    for _ in modbg[0]:
                        pass
            dump("modv", modv[:], [128, 2, 96])
            dump("mixed0", Bf[:], [128, 8, T])
            phase_end("mixer0")
            post_pre(0, 0, Af, 0, 1, B, oproj=(0, B))
            dump("x1", x[:], [128, 8, T])
            phase_end("mix0_done")
            ffn(0)
            dump("x2", x[:], [128, 8, T])
            phase_end("ffn0")
            with sc.scope():
                mixer1()
            dump("attn", Bf[:], [128, 8, T])
            phase_end("mixer1")
            post_pre(1, 0, Af, 1, 1, B, oproj=(1, B))
            dump("x3", x[:], [128, 8, T])
            phase_end("mix1_done")
            ffn(1)
            phase_end("ffn1")
        except Stop:
            pass
        sc.flush()
        dump("endx", x[:], [128, 8, T])
        dump("endA", Af[:], [128, 8, T])
        dump("endB", Bf[:], [128, 8, T])
        if stop is not None and stop != "ffn1":
            sc.flush()
            with sc.scope():
                ys = [tile("ys", [128, D]) for _ in range(2)]
                for tt in range(10):
                    yt_ = ys[tt % 2]
                    for half in range(2):
                        ps = pG.get()
                        for j in range(4):
                            ch = half * 4 + j
                            sc.tr(ps[:, j * 128:(j + 1) * 128], x[:, ch, tt * 128:(tt + 1) * 128], ident)
                        sc.copy(yt_[:, half * 512:(half + 1) * 512], ps[:, 0:512], "act" if half == 0 else "dve")
                    sc.dma("sp", yout[tt * 128:(tt + 1) * 128, :], yt_[:])
        sc.flush(final=True)
        stats = dict(sc.stats)
    return nc, dbg_out, stats


def _core_units(c):
    if c < 6:
        return "prompt", [5 * c + j for j in range(4)], 5 * c + 4
    return "sample", c - 6, 30 + (c - 6)


_SW = np.array([(d + 16) if (d % 32) < 16 else (d - 16) for d in range(64)])


def _rope_table(mode):
    tab = np.zeros((128, T), np.float32)
    tab[0:64, :] = 1.0
    if mode == "sample":
        pos = np.arange(1024)
        row = (pos // 64).astype(np.float32)
        col = (pos % 64).astype(np.float32)
        inv = (1.0 / (np.float32(10000.0) ** (np.arange(0, 32, 2, dtype=np.float32) / np.float32(32)))).astype(np.float32)
        for d in range(64):
            base = row if d < 32 else col
            ang = (base * inv[d % 16]).astype(np.float32)
            tab[d, 0:1024] = np.cos(ang)
            sgn = -1.0 if (d % 32) < 16 else 1.0
            tab[64 + d, 0:1024] = sgn * np.sin(ang)
    return tab


def prep_weights(inp):
    f = lambda a: np.ascontiguousarray(np.asarray(a, dtype=np.float32))
    W = {}
    W["wmod"] = f(np.stack([inp["l0_w_mod"], inp["l1_w_mod"]]))
    bm = np.stack([inp["l0_b_mod"], inp["l1_b_mod"]])
    bmT = bm.reshape(2, 48, 128).transpose(2, 0, 1)
    W["bmodT"] = f(np.repeat(bmT[:, :, :, None], 2, axis=3).reshape(128, 2, 96))
    g = np.stack([np.stack([inp["l0_g_pre_mix"], inp["l0_g_post_mix"], inp["l0_g_pre_ffn"], inp["l0_g_post_ffn"]]),
                  np.stack([inp["l1_g_pre_mix"], inp["l1_g_post_mix"], inp["l1_g_pre_ffn"], inp["l1_g_post_ffn"]])])
    gT = g.reshape(2, 4, 8, 128).transpose(3, 0, 1, 2)
    W["gvecT"] = f(np.repeat(gT[..., None], 2, axis=4).reshape(128, 2, 4, 16))
    w = np.asarray(inp["l0_w_in"], np.float32)
    qa, ka, va, ga, lra = w[:, 0:256], w[:, 256:512], w[:, 512:1024], w[:, 1024:1536], w[:, 1536:1568]
    qb, kb, vb, ob, gts = w[:, 1568:1824], w[:, 1824:2080], w[:, 2080:2592], w[:, 2592:3104], w[:, 3104:3120]
    w0 = np.zeros((D, 4096), np.float32)
    for i in range(4):
        s = i * 512
        w0[:, s:s + 64] = qa[:, i * 64:(i + 1) * 64]
        w0[:, s + 64:s + 128] = ka[:, i * 64:(i + 1) * 64]
        w0[:, s + 128:s + 192] = ka[:, i * 64:(i + 1) * 64]
        w0[:, s + 192:s + 320] = va[:, i * 128:(i + 1) * 128]
        w0[:, s + 320:s + 448] = ga[:, i * 128:(i + 1) * 128]
        s = (4 + i) * 512
        w0[:, s:s + 64] = qb[:, i * 64:(i + 1) * 64]
        w0[:, s + 64:s + 128] = kb[:, i * 64:(i + 1) * 64]
        w0[:, s + 128:s + 192] = kb[:, i * 64:(i + 1) * 64]
        w0[:, s + 192:s + 320] = vb[:, i * 128:(i + 1) * 128]
        w0[:, s + 320:s + 448] = ob[:, i * 128:(i + 1) * 128]
    w0[:, 448:480] = lra
    w0[:, 4 * 512 + 448:4 * 512 + 464] = gts
    W["w0cat"] = f(w0)
    wg = np.zeros((33, 2, 512), np.float32)
    for d, (wn, bn, r0) in enumerate((("l0_gla_w_gate_f", "l0_gla_b_gate_f", 0), ("l0_gla_w_gate_b", "l0_gla_b_gate_b", 16))):
        wgd, bgd = np.asarray(inp[wn], np.float32), np.asarray(inp[bn], np.float32)
        for h in range(4):
            for rep in range(2):
                wg[r0:r0 + 16, d, h * 128 + rep * 64:h * 128 + rep * 64 + 64] = wgd[:, h * 64:(h + 1) * 64]
                wg[32, d, h * 128 + rep * 64:h * 128 + rep * 64 + 64] = bgd[h * 64:(h + 1) * 64]
    W["wg"] = f(wg)
    W["gn"] = f(np.broadcast_to(np.stack([inp["l0_gla_g_norm"], inp["l0_mlstm_g_norm"]])[None], (128, 2, 128)))
    W["bgates"] = f(np.broadcast_to(np.asarray(inp["l0_mlstm_b_gates"])[None], (128, 16)))
    W["wout"] = f(np.stack([inp["l0_w_out"], inp["l1_w_out"]]))
    wups, cvs = [], []
    ffn_in = ((inp["l0_ffn_w_up"], inp["l0_ffn_conv_w"], inp["l0_ffn_conv_b"]),
              (inp["l1_ffn_w_up"], inp["l1_ffn_conv_w"], inp["l1_ffn_conv_b"]))
    for l in range(2):
        wu = np.asarray(ffn_in[l][0], np.float32)
        cols = []
        for s in range(11):
            cols.append(wu[:, s * 256:(s + 1) * 256])
            cols.append(wu[:, 2816 + s * 256:2816 + (s + 1) * 256])
        wups.append(np.concatenate(cols, axis=1))
        cw = np.asarray(ffn_in[l][1], np.float32)
        cb = np.asarray(ffn_in[l][2], np.float32)
        cc = np.concatenate([cw, cb[None]], axis=0)
        cvs.append(cc.reshape(4, 44, 128).transpose(2, 1, 0))
    W["wup"] = f(np.stack(wups))
    W["convT"] = f(np.stack(cvs, axis=1))
    W["wdown"] = f(np.stack([inp["l0_ffn_w_down"], inp["l1_ffn_w_down"]]))
    w1 = np.asarray(inp["l1_w_in"], np.float32)
    W["w1in"] = f(np.concatenate([w1[:, 0:704], w1[:, 640:704][:, _SW]], axis=1))
    wq = np.asarray(inp["l1_w_qb"], np.float32)
    qcols = []
    for h in range(8):
        qcols += [wq[:, h * 192:h * 192 + 128], wq[:, h * 192 + 128:h * 192 + 192], wq[:, h * 192 + 128:h * 192 + 192][:, _SW]]
    W["wqb"] = f(np.concatenate(qcols, axis=1))
    W["wkvb"] = f(inp["l1_w_kvb"])
    gq = np.asarray(inp["l1_g_q_norm"], np.float32).reshape(3, 128).T
    gkv = np.asarray(inp["l1_g_kv_norm"], np.float32).reshape(2, 128).T
    W["gqkv"] = f(np.concatenate([gq, gkv], axis=1))
    W["cst"] = make_consts()
    return W


def prep_core(inp, c):
    f = lambda a: np.ascontiguousarray(np.asarray(a, dtype=np.float32))
    mode, grp, sa = _core_units(c)
    xp, xs = np.asarray(inp["x_prompt"]), np.asarray(inp["x_sample"])
    m = {}
    if mode == "prompt":
        xg = np.concatenate([xp[s] for s in grp], axis=0)
        cond0 = np.asarray(inp["c_ctx"])
    else:
        xg = xs[grp]
        cond0 = np.asarray(inp["c"])[grp]
    m["xin"] = f(np.concatenate([xg, xp[sa]], axis=0))
    cond2 = np.stack([cond0, np.asarray(inp["c_ctx"])])
    m["cond2T"] = f(cond2.reshape(2, 8, 128).transpose(2, 1, 0))
    chain = np.zeros((128, 2, NU), np.float32)
    ginit = np.zeros((2, NU, 4, 64, 128), np.float32)
    minit = np.zeros((2, NU, 4, 64, 129), np.float32)
    mminit = np.zeros((128, 2, NU, 4), np.float32)
    ktab = np.zeros((6, NKEY), np.float32)
    qtab = np.zeros((5, T), np.float32)
    ckvc = np.zeros((512, 256), np.float32)
    krc = np.zeros((512, 64), np.float32)
    ktab[0, :] = 1.0
    for j in range(4):
        ktab[1 + j, 512 + 256 * j:512 + 256 * (j + 1)] = 1.0
    ktab[5, 0:512] = 1.0
    if mode == "sample":
        b = grp
        chain[:, 0, 1:4] = 1.0
        chain[:, 1, 0:3] = 1.0
        ginit[0, 0] = inp["state_l0_gla_fwd"][b]
        ginit[1, 3] = inp["state_l0_gla_bwd"][b]
        minit[0, 0, :, :, 0:128] = inp["state_l0_mlstm_c_fwd"][b]
        minit[0, 0, :, :, 128] = inp["state_l0_mlstm_n_fwd"][b]
        minit[1, 3, :, :, 0:128] = inp["state_l0_mlstm_c_bwd"][b]
        minit[1, 3, :, :, 128] = inp["state_l0_mlstm_n_bwd"][b]
        mminit[:, 0, 0, :] = np.asarray(inp["state_l0_mlstm_m_fwd"])[b][None, :]
        mminit[:, 1, 3, :] = np.asarray(inp["state_l0_mlstm_m_bwd"])[b][None, :]
        ckvc = inp["cache_l1_ckv"][b]
        krc = inp["cache_l1_krope"][b]
    else:
        for u in range(4):
            for j in range(4):
                if j != u:
                    qtab[j, 256 * u:256 * (u + 1)] = NEG
        qtab[4, 0:1024] = NEG
    m["chain"], m["ginit"], m["minit"], m["mminit"] = chain, ginit, minit, mminit
    m["ktab"], m["qtab"], m["ckvc"], m["krc"] = ktab, qtab, f(ckvc), f(krc)
    m["ropeT"] = _rope_table(mode)
    return m


def assemble(results):
    yp = np.zeros((32, 256, D), np.float32)
    ysm = np.zeros((2, 1024, D), np.float32)
    gla = np.zeros((2, 32, 4, 64, 128), np.float32)
    mC = np.zeros((2, 32, 4, 64, 128), np.float32)
    mn = np.zeros((2, 32, 4, 64), np.float32)
    mm = np.zeros((2, 32, 4), np.float32)
    ckv = np.zeros((32, 256, 256), np.float32)
    kr = np.zeros((32, 256, 64), np.float32)
    for c, r in enumerate(results):
        mode, grp, sa = _core_units(c)
        units = [(4, sa)]
        if mode == "prompt":
            units += [(j, grp[j]) for j in range(4)]
        else:
            ysm[grp] = r["y"][0:1024]
        mmo = r["mm_o"].reshape(2, NU, 4)
        for u, s in units:
            yp[s] = r["y"][u * 256:(u + 1) * 256]
            ckv[s] = r["ckv_o"][u * 256:(u + 1) * 256]
            kr[s] = r["kr_o"][u * 256:(u + 1) * 256]
            for d in range(2):
                gla[d, s] = r["gla_o"][d, u]
                mC[d, s] = r["mC_o"][d, u, :, :, 0:128]
                mn[d, s] = r["mC_o"][d, u, :, :, 128]
                mm[d, s] = mmo[d, u]
    return (yp, ysm, gla[0], gla[1], mC[0], mn[0], mm[0], mC[1], mn[1], mm[1], ckv, kr)


_PROG = {}


def kernel(**inputs):
    if "nc" not in _PROG:
        _PROG["nc"] = build_program()[0]
    nc = _PROG["nc"]
    W = prep_weights(inputs)
    in_maps = []
    for c in range(8):
        m = dict(W)
        m.update(prep_core(inputs, c))
        in_maps.append(m)
    res = run_bass_kernel_spmd(nc, in_maps, core_ids=list(range(8)))
    return assemble(res.results)
```

```python
import contextlib
import numpy as np
import concourse.bass as bass
import concourse.mybir as mybir
from concourse.bass_utils import run_bass_kernel_spmd
from concourse.alu_op_type import AluOpType as ALU

F32 = mybir.dt.float32
F32R = mybir.dt.float32r
AF = mybir.ActivationFunctionType
AX = mybir.AxisListType

SAME_ENGINE_SYNC = True
class Tile:
    def __init__(self, sc, name, shape, dtype=F32, space="sb"):
        self.name, self.shape, self.dtype, self.space = name, list(shape), dtype, space
        alloc = sc.nc.sbuf_tensor if space == "sb" else sc.nc.psum_tensor
        self.h = sc.es.enter_context(alloc(name, list(shape), dtype))
        self.wr = {}
        self.rd = {}

    def __getitem__(self, idx):
        return Ref(self, idx)

    def v(self, dtype):
        return _View(self, dtype)


class _View:
    def __init__(self, tile, dtype):
        self.tile, self.dtype = tile, dtype

    def __getitem__(self, idx):
        return Ref(self.tile, idx, self.dtype)


class Ref:
    def __init__(self, tile, idx, dtype=None):
        if not isinstance(idx, tuple):
            idx = (idx,)
        idx = idx + (slice(None),) * (len(tile.shape) - len(idx))
        self.tile = tile
        ap = tile.h[idx]
        if dtype is not None and dtype != tile.dtype:
            ap = ap.bitcast(dtype)
        self.ap = ap
        box = []
        for i, n in zip(idx, tile.shape):
            if isinstance(i, int):
                box.append((i, i + 1))
            else:
                a, b, st = i.indices(n)
                box.append((a, b))
        self.box = tuple(box)


def _overlap(a, b):
    return all(x[0] < y[1] and y[0] < x[1] for x, y in zip(a, b))


def _contains(a, b):
    return all(x[0] <= y[0] and x[1] >= y[1] for x, y in zip(a, b))


class Op:
    __slots__ = ("eng", "fn", "waits", "is_dma", "sem", "target", "signal", "rank", "seq", "clock", "selfwait")


class Sched:
    def __init__(self, nc, es, n_dma_sems=20):
        self.nc = nc
        self.stacks = [es]
        self.E = {"pe": nc.tensor, "act": nc.scalar, "dve": nc.vector, "pool": nc.gpsimd, "sp": nc.sync}
        self.ops = []
        self.seq = {e: 0 for e in self.E}
        self.clock = {e: {} for e in self.E}
        self.esem = {e: es.enter_context(nc.semaphore("es_" + e)) for e in ("pe", "act", "dve", "pool")}
        self.dsem = {}
        for q in ("sp", "pool"):
            self.dsem[q] = [[es.enter_context(nc.semaphore("ds_%s_%d" % (q, i))), 0] for i in range(n_dma_sems)]
        self.dnext = {q: 0 for q in self.dsem}
        self.ntile = 0
        self.emitted = 0
        self.dma_barrier = 0
        self.cnt = {e: 0 for e in self.E}
        self.waited = {}
        self.stats = dict(nops=0, nwait=0)

    @property
    def es(self):
        return self.stacks[-1]

    def tile(self, name, shape, dtype=F32, space="sb"):
        self.ntile += 1
        return Tile(self, "%s_%d" % (name, self.ntile), shape, dtype, space)

    @contextlib.contextmanager
    def scope(self):
        sub = contextlib.ExitStack()
        self.stacks.append(sub)
        try:
            yield sub
        finally:
            self.flush()
            self.stacks.pop()
            sub.close()

    def carve(self, parent, name, off, shape, dtype=F32):
        t = Tile.__new__(Tile)
        t.name, t.shape, t.dtype, t.space = name, list(shape), dtype, parent.space
        n = 1
        for d_ in shape[1:]:
            n *= d_
        names = "abcdefg"[:len(parent.shape) - 1]
        flat = parent.h[:].rearrange("p %s -> p (%s)" % (" ".join(names), " ".join(names)))
        ap = flat[0:shape[0], off:off + n]
        if dtype != parent.dtype:
            ap = ap.bitcast(dtype)
        if len(shape) > 2:
            nm = "abcdefg"[:len(shape) - 1]
            kw = {nm[i]: shape[1 + i] for i in range(len(shape) - 2)}
            ap = ap.rearrange("p (%s) -> p %s" % (" ".join(nm), " ".join(nm)), **kw)
        t.h = ap
        t.wr, t.rd = {}, {}
        return t

    def _record(self, eng, fn, reads, writes, is_dma=False):
        op = Op()
        op.eng, op.fn, op.is_dma = eng, fn, is_dma
        op.signal, op.rank, op.sem, op.target, op.selfwait = False, 0, None, 0, None
        oid = len(self.ops)
        deps = set()
        ps_seen = {}
        for r, isw in [(r, False) for r in reads] + [(w, True) for w in writes]:
            if isinstance(r, Ref) and r.tile.space == "ps":
                ps_seen[id(r.tile)] = (r.tile, ps_seen.get(id(r.tile), (None, False))[1] or isw)
        for t, isw in ps_seen.values():
            acc = t.__dict__.get("acc", {})
            for f, (o, w) in acc.items():
                if f != eng or w or isw:
                    deps.add(o)
            t.acc = {eng: (oid, isw)}
        reads = [r for r in reads if isinstance(r, Ref) and r.tile.space != "ps"]
        writes = [w for w in writes if isinstance(w, Ref) and w.tile.space != "ps"]
        for r in reads:
            if not isinstance(r, Ref):
                continue
            for box, w in r.tile.wr.items():
                if _overlap(box, r.box):
                    deps.add(w)
        for w in writes:
            if not isinstance(w, Ref):
                continue
            for box, o in w.tile.wr.items():
                if _overlap(box, w.box):
                    deps.add(o)
            for (box, _e), o in w.tile.rd.items():
                if _overlap(box, w.box):
                    deps.add(o)
        for w in writes:
            if not isinstance(w, Ref):
                continue
            t = w.tile
            t.wr = {b: o for b, o in t.wr.items() if not _contains(w.box, b)}
            t.rd = {k: o for k, o in t.rd.items() if not _contains(w.box, k[0])}
            t.wr[w.box] = oid
        for r in reads:
            if not isinstance(r, Ref):
                continue
            r.tile.rd[(r.box, eng if not is_dma else ("dma", oid))] = oid
        clk = self.clock[eng]
        waits = []
        for d in sorted(deps):
            p = self.ops[d]
            if p.is_dma:
                if d < self.dma_barrier:
                    continue
                key = ("dma", d)
                if clk.get(key, -1) >= 0:
                    continue
                waits.append(d)
                clk[key] = 0
            else:
                if p.eng == eng and (eng == "pe" or not SAME_ENGINE_SYNC):
                    continue
                if clk.get(p.eng, -1) >= p.seq:
                    continue
                waits.append(d)
                if clk.get(p.eng, -1) < p.seq:
                    clk[p.eng] = p.seq
            for k, v in p.clock.items():
                if clk.get(k, -1) < v:
                    clk[k] = v
        op.waits = waits
        op.seq = self.seq[eng]
        self.seq[eng] += 1
        op.clock = dict(clk)
        if not is_dma:
            op.clock[eng] = op.seq
        self.ops.append(op)
        return op

    def op(self, eng, fn, reads=(), writes=()):
        return self._record(eng, fn, list(reads), list(writes))

    def dma(self, q, out, in_, **kw):
        oap = out.ap if isinstance(out, Ref) else out
        iap = in_.ap if isinstance(in_, Ref) else in_
        op = self._record(q, lambda e: e.dma_start(out=oap, in_=iap, **kw),
                          [in_] if isinstance(in_, Ref) else [], [out] if isinstance(out, Ref) else [], is_dma=True)
        i = self.dnext[q]
        self.dnext[q] = (i + 1) % len(self.dsem[q])
        ent = self.dsem[q][i]
        if ent[1] > 0:
            op.selfwait = (ent[0], ent[1])
        ent[1] += 16
        op.sem, op.target = ent[0], ent[1]
        return op

    def _wait(self, engname, key, val):
        wk = (engname, id(key))
        if self.waited.get(wk, -1) >= val:
            return
        self.waited[wk] = val
        self.E[engname].wait_ge(key, val)
        self.stats["nwait"] += 1

    def flush(self, final=False):
        pend = self.ops[self.emitted:]
        last = {}
        for op in pend:
            for d in op.waits:
                p = self.ops[d]
                if not p.is_dma:
                    assert d >= self.emitted, "dependency on pre-barrier op"
                    p.signal = True
            if not op.is_dma:
                last[op.eng] = op
        for op in last.values():
            op.signal = True
        for op in pend:
            if op.signal:
                self.cnt[op.eng] += 1
                op.rank = self.cnt[op.eng]
        for op in pend:
            eng = self.E[op.eng]
            need = {}
            for d in op.waits:
                p = self.ops[d]
                key, val = (p.sem, p.target) if p.is_dma else (self.esem[p.eng], p.rank)
                if need.get(id(key), (None, -1))[1] < val:
                    need[id(key)] = (key, val)
            if op.selfwait is not None:
                key, val = op.selfwait
                if need.get(id(key), (None, -1))[1] < val:
                    need[id(key)] = (key, val)
            for key, val in need.values():
                self._wait(op.eng, key, val)
            ins = op.fn(eng)
            if op.is_dma:
                ins.then_inc(op.sem, 16)
            elif op.signal:
                ins.then_inc(self.esem[op.eng], 1)
            op.fn = None
        self.stats["nops"] += len(pend)
        self.emitted = len(self.ops)
        engs = ["sp"] if final else ["pe", "act", "dve", "sp", "pool"]
        for e in engs:
            for f in ("pe", "act", "dve", "pool"):
                if self.cnt[f] > 0:
                    self._wait(e, self.esem[f], self.cnt[f])
            for q in self.dsem:
                for sem, tgt in self.dsem[q]:
                    if tgt > 0:
                        self._wait(e, sem, tgt)
        for e in self.E:
            for f in ("pe", "act", "dve", "pool"):
                self.clock[e][f] = self.seq[f] - 1
        self.dma_barrier = len(self.ops)

    def mm(self, out, lhsT, rhs, start=True, stop=True):
        return self.op("pe", lambda e: e.matmul(out.ap, lhsT.ap, rhs.ap, start=start, stop=stop), [lhsT, rhs], [out])

    def tr(self, out, in_, ident):
        return self.op("pe", lambda e: e.transpose(out.ap, in_.ap, ident.ap), [in_, ident], [out])

    def act(self, out, in_, func, bias=None, scale=None, accum=None, eng="act"):
        kw = {}
        rd = [in_]
        wr = [out]
        if bias is not None:
            kw["bias"] = bias.ap if isinstance(bias, Ref) else bias
            if isinstance(bias, Ref):
                rd.append(bias)
        if scale is not None:
            kw["scale"] = scale.ap if isinstance(scale, Ref) else scale
            if isinstance(scale, Ref):
                rd.append(scale)
        if accum is not None:
            kw["accum_out"] = accum.ap
            wr.append(accum)
        return self.op(eng, lambda e: e.activation(out=out.ap, in_=in_.ap, func=func, **kw), rd, wr)

    def tt(self, out, a, b, op, eng="dve"):
        return self.op(eng, lambda e: e.tensor_tensor(out=out.ap, in0=a.ap, in1=b.ap, op=op), [a, b], [out])

    def ts(self, out, a, s1, op0, s2=None, op1=None, eng="dve", accum=None):
        rd = [a]
        wr = [out]
        v1 = s1.ap if isinstance(s1, Ref) else s1
        v2 = s2.ap if isinstance(s2, Ref) else s2
        if isinstance(s1, Ref):
            rd.append(s1)
        if isinstance(s2, Ref):
            rd.append(s2)
        kw = {}
        if op1 is not None:
            kw["op1"] = op1
        if accum is not None:
            kw["accum_out"] = accum.ap
            wr.append(accum)
        return self.op(eng, lambda e: e.tensor_scalar(out=out.ap, in0=a.ap, scalar1=v1, scalar2=v2, op0=op0, **kw), rd, wr)

    def stt(self, out, a, scalar, b, op0, op1):
        rd = [a, b]
        v = scalar.ap if isinstance(scalar, Ref) else scalar
        if isinstance(scalar, Ref):
            rd.append(scalar)
        return self.op("dve", lambda e: e.scalar_tensor_tensor(out=out.ap, in0=a.ap, scalar=v, in1=b.ap, op0=op0, op1=op1), rd, [out])

    def copy(self, out, in_, eng="act"):
        if eng == "act":
            return self.op("act", lambda e: e.copy(out.ap, in_.ap), [in_], [out])
        return self.op(eng, lambda e: e.tensor_copy(out.ap, in_.ap), [in_], [out])

    def rmax(self, out, in_):
        return self.op("dve", lambda e: e.reduce_max(out.ap, in_.ap, axis=AX.X), [in_], [out])

    def rsum(self, out, in_):
        return self.op("dve", lambda e: e.reduce_sum(out.ap, in_.ap, axis=AX.X), [in_], [out])

    def memset(self, out, val, eng="dve"):
        return self.op(eng, lambda e: e.memset(out.ap, val), [], [out])


def _reref(ref, pattern, **kw):
    r = Ref.__new__(Ref)
    r.tile, r.box = ref.tile, ref.box
    r.ap = ref.ap.rearrange(pattern, **kw)
    return r


Ref.re = _reref

D = 1024
T = 1280
NU = 5
EPS = 1e-6
TT = [(0, 512), (512, 512), (1024, 256)]
NKEY = 1792
SLOTW = 4096
ATT_SCALE = 192.0 ** -0.5
NEG = -30000.0
WARM_ATT = 0
WARM_PROJ = 0
BOTH_SWAP = 0
FFN_POOL_ACC = 0
OUT_DELAY = 6
MOD_IDLE = 15
WARM_MIX = 0

C_ID, C_ONE, C_TGF, C_TGB, C_TMF, C_TMB, C_MPF, C_MPB, C_MAF, C_MAB, C_SLF, C_SLB = [i * 128 for i in range(12)]
NCST = 12 * 128


def make_consts():
    c = np.zeros((128, NCST), np.float32)
    s = np.arange(128)[:, None]
    t = np.arange(128)[None, :]
    le = (s <= t).astype(np.float32)
    ge = (s >= t).astype(np.float32)
    c[:, C_ID:C_ID + 128] = np.eye(128, dtype=np.float32)
    c[:, C_ONE:C_ONE + 128] = 1.0
    c[:, C_TGF:C_TGF + 128] = -le / 16.0
    c[:, C_TGB:C_TGB + 128] = -ge / 16.0
    c[:, C_TMF:C_TMF + 128] = -le
    c[:, C_TMB:C_TMB + 128] = -ge
    c[:, C_MPF:C_MPF + 128] = le
    c[:, C_MPB:C_MPB + 128] = ge
    c[:, C_MAF:C_MAF + 128] = np.where(s >= t, 0.0, -1e30)
    c[:, C_MAB:C_MAB + 128] = np.where(s <= t, 0.0, -1e30)
    c[127, C_SLF:C_SLF + 128] = 1.0
    c[0, C_SLB:C_SLB + 128] = 1.0
    return c


class Ring:
    def __init__(self, sc, nslot):
        self.sc = sc
        self.slots = [sc.tile("ring", [128, SLOTW], F32R) for _ in range(nslot)]
        self.i = 0
        self.pre = {}

    def load(self, dram3d, nk, W, q="pool", idx=None, key=None):
        if key is not None and key in self.pre:
            return self.pre.pop(key)
        reserved = {id(s_.t) for s_ in self.pre.values()}
        if idx is None:
            for _ in range(len(self.slots)):
                idx = self.i
                self.i = (self.i + 1) % len(self.slots)
                if id(self.slots[idx]) not in reserved:
                    break
        assert id(self.slots[idx]) not in reserved, "ring slot holds preloaded weights that were not consumed yet"
        t = self.slots[idx]
        dst = t[:, 0:nk * W].re("p (k n) -> p k n", k=nk)
        self.sc.dma(q, dst, dram3d)
        s_ = _Slot(t, nk, W)
        s_.idx = idx
        return s_

    def preload(self, key, dram3d, nk, W, idx=None):
        self.pre[key] = self.load(dram3d, nk, W, idx=idx)


class _Slot:
    def __init__(self, t, nk, W):
        self.t, self.nk, self.W = t, nk, W

    def w(self, kc, a, b, p0=0, p1=128):
        return self.t[p0:p1, kc * self.W + a: kc * self.W + b]


class PsPool:
    def __init__(self, banks, width):
        self.banks, self.width = banks, width
        self.slots = [(b, o) for b in banks for o in range(0, 512, width)]
        self.i = 0

    def get(self):
        b, o = self.slots[self.i]
        self.i = (self.i + 1) % len(self.slots)
        return _PsView(b, o)


class _PsView:
    def __init__(self, bank, off):
        self.bank, self.off = bank, off

    def __getitem__(self, idx):
        p, c = idx
        a, b, _ = c.indices(512 - self.off)
        return self.bank[p, self.off + a:self.off + b]


def build_program(stop=None, dumps=(), nslot=2):
    nc = bass.Bass("TRN2", target_bir_lowering=False)

    def din(name, shape):
        return nc.dram_tensor(name, list(shape), F32, kind="ExternalInput").ap()

    def dout(name, shape):
        return nc.dram_tensor(name, list(shape), F32, kind="ExternalOutput").ap()

    xin = din("xin", [T, D])
    cond2T = din("cond2T", [128, 8, 2])
    cstd = din("cst", [128, NCST])
    chaind = din("chain", [128, 2, NU])
    ginit = din("ginit", [2, NU, 4, 64, 128])
    minit = din("minit", [2, NU, 4, 64, 129])
    mminit = din("mminit", [128, 2, NU, 4])
    ropeT = din("ropeT", [128, T])
    ktab = din("ktab", [6, NKEY])
    qtab = din("qtab", [5, T])
    ckvc = din("ckvc", [512, 256])
    krc = din("krc", [512, 64])
    wmod = din("wmod", [2, D, 6144])
    bmodT = din("bmodT", [128, 2, 96])
    gvecT = din("gvecT", [128, 2, 4, 16])
    w0cat = din("w0cat", [D, 4096])
    wgd = din("wg", [33, 2, 512])
    gnd = din("gn", [128, 2, 128])
    bgd = din("bgates", [128, 16])
    woutd = din("wout", [2, D, D])
    wupd = din("wup", [2, D, 5632])
    convd = din("convT", [128, 2, 44, 4])
    wdownd = din("wdown", [2, 2816, D])
    w1in = din("w1in", [D, 768])
    wqbd = din("wqb", [384, 2048])
    wkvbd = din("wkvb", [256, 2048])
    gqkvd = din("gqkv", [128, 5])
    yout = dout("y", [T, D])
    gla_o = dout("gla_o", [2, NU, 4, 64, 128])
    mC_o = dout("mC_o", [2, NU, 4, 64, 129])
    mm_o = dout("mm_o", [1, 40])
    ckv_o = dout("ckv_o", [T, 256])
    kr_o = dout("kr_o", [T, 64])
    dbg_out = {}

    class Stop(Exception):
        pass

    es = contextlib.ExitStack()
    with es:
        sc = Sched(nc, es)
        tile = sc.tile
        x = tile("x", [128, 8, T])
        A = tile("A", [128, 8, T], F32R)
        B = tile("B", [128, 8, T], F32R)
        Af, Bf = A.v(F32), B.v(F32)
        ring = Ring(sc, nslot)
        cst = tile("cst", [128, NCST])
        onesR = tile("onesR", [128, 128], F32R)
        chain = tile("chain", [128, 2, NU])
        modv = tile("modv", [128, 2, 96])
        bmod = tile("bmod", [128, 2, 96])
        gv = tile("gv", [128, 2, 4, 16])
        A1 = tile("A1", [128, 2, 2, 16])
        GG = tile("GG", [128, 2, 2, 16])
        banks = [tile("ps", [128, 512], F32, "ps") for _ in range(8)]

        ident = cst[:, C_ID:C_ID + 128]
        ones = cst[:, C_ONE:C_ONE + 128]

        pG = PsPool(banks[0:4], 512)
        pS = banks[4]
        pQ = PsPool(banks[5:8] + banks[0:4], 512)
        pO, pD = banks[5], banks[6]
        pX = PsPool([banks[7]] + banks[0:3], 512)
        pF = PsPool(banks[0:8], 512)

        def dump(name, ref, shape):
            if name in dumps:
                o = dout("dbg_" + name, shape)
                sc.dma("sp", o, ref)
                dbg_out[name] = shape

        def phase_end(name):
            if stop == name:
                raise Stop()

        def norm_tiles():
            return dict(sq=[tile("sq", [128, 512], F32R) for _ in range(2)], R=[tile("R", [128, 512]) for _ in range(2)],
                        lnt=tile("lnt", [128, 512]), tmp=[tile("tmp512", [128, 512]) for _ in range(2)], c=[0, 0, 0])

        def compute_R(nt, src_fn, nch, dim, n, bank=None):
            bank = pS if bank is None else bank
            for ch in range(nch):
                nt["c"][0] += 1
                sq = nt["sq"][nt["c"][0] % len(nt["sq"])]
                sc.act(sq[:, 0:n], src_fn(ch), AF.Square)
                sc.mm(bank[:, 0:n], onesR[:, :], sq[:, 0:n], start=(ch == 0), stop=(ch == nch - 1))
            sc.act(nt["lnt"][:, 0:n], bank[:, 0:n], AF.Ln, bias=EPS, scale=1.0 / dim)
            nt["c"][1] += 1
            R = nt["R"][nt["c"][1] % len(nt["R"])]
            sc.act(R[:, 0:n], nt["lnt"][:, 0:n], AF.Exp, scale=-0.5)
            return R

        def shiftv(l, which, ch, cond):
            c = (which * 3) * 16 + ch * 2 + cond
            return modv[:, l, c:c + 1]

        def norm_mod(l, which, dst):
            with sc.scope():
                nt = norm_tiles()
                for ti, (c0, n) in enumerate(TT):
                    cond = 0 if ti < 2 else 1
                    R = compute_R(nt, lambda ch: x[:, ch, c0:c0 + n], 8, D, n)
                    for ch in range(8):
                        nt["c"][2] += 1
                        tm = nt["tmp"][nt["c"][2] % 2]
                        sc.tt(tm[:, 0:n], x[:, ch, c0:c0 + n], R[:, 0:n], ALU.mult)
                        k = ch * 2 + cond
                        sc.act(dst[:, ch, c0:c0 + n], tm[:, 0:n], AF.Identity,
                               bias=shiftv(l, which, ch, cond), scale=A1[:, l, which, k:k + 1])

        def post_pre(lp, wp, src, ln_, wn, dst, oproj=None):
            with sc.scope():
                sqs = [tile("sq", [128, 512], F32R) for _ in range(3)]
                tms = [tile("tmp512", [128, 512]) for _ in range(3)]
                Ra = [tile("Ra", [128, 512]) for _ in range(3)]
                Rb = [tile("Rb", [128, 512]) for _ in range(3)]
                lnts = [tile("lnt", [128, 512]) for _ in range(3)]

                if oproj is not None:
                    ol, osrc = oproj
                    s0_ = ring.load(wout_ap(ol, 0), 8, 512, key=("wout", ol, 0))
                    slots_ = [s0_, ring.load(wout_ap(ol, 1), 8, 512, idx=1 - s0_.idx)]
                    pend = []

                    def stats_mm(item):
                        nb_, ti_, n_ = item
                        sc.mm(banks[5 + ti_][:, 0:n_], onesR[:, :], sqs[ti_][:, 0:n_], start=(nb_ == 0), stop=(nb_ == 7))

                    for half in range(2):
                        slot = slots_[half]
                        for nb4 in range(4):
                            nb = half * 4 + nb4
                            for ti, (c0, n) in enumerate(TT):
                                ps = pG.get()
                                for kc in range(8):
                                    sc.mm(ps[:, 0:n], slot.w(kc, nb4 * 128, nb4 * 128 + 128), osrc[:, kc, c0:c0 + n],
                                          start=(kc == 0), stop=(kc == 7))
                                while len(pend) > 2:
                                    stats_mm(pend.pop(0))
                                sc.copy(A[:, nb, c0:c0 + n], ps[:, 0:n], "act")
                                sc.act(sqs[ti][:, 0:n], ps[:, 0:n], AF.Square)
                                pend.append((nb, ti, n))
                        if half == 0:
                            ring.preload(("wup", ol, 0), wup_ap(ol, 0), 8, 512)
                    while pend:
                        stats_mm(pend.pop(0))

                def tile_gen(ti):
                    c0, n = TT[ti]
                    cond = 0 if ti < 2 else 1
                    bA, bB = (banks[5 + ti], banks[ti]) if oproj is not None else (banks[ti], banks[3 + ti])
                    sq, tm, R, R2, lnt_ = sqs[ti], tms[ti], Ra[ti], Rb[ti], lnts[ti]
                    if oproj is None:
                        for ch in range(8):
                            sc.act(sq[:, 0:n], src[:, ch, c0:c0 + n], AF.Square)
                            sc.mm(bA[:, 0:n], onesR[:, :], sq[:, 0:n], start=(ch == 0), stop=(ch == 7))
                            yield
                    sc.act(lnt_[:, 0:n], bA[:, 0:n], AF.Ln, bias=EPS, scale=1.0 / D)
                    sc.act(R[:, 0:n], lnt_[:, 0:n], AF.Exp, scale=-0.5)
                    yield
                    for ch in range(8):
                        sc.tt(tm[:, 0:n], src[:, ch, c0:c0 + n], R[:, 0:n], ALU.mult)
                        k = ch * 2 + cond
                        sc.stt(x[:, ch, c0:c0 + n], tm[:, 0:n], GG[:, lp, wp, k:k + 1], x[:, ch, c0:c0 + n],
                               ALU.mult, ALU.add)
                        sc.act(sq[:, 0:n], x[:, ch, c0:c0 + n], AF.Square)
                        sc.mm(bB[:, 0:n], onesR[:, :], sq[:, 0:n], start=(ch == 0), stop=(ch == 7))
                        yield
                    sc.act(lnt_[:, 0:n], bB[:, 0:n], AF.Ln, bias=EPS, scale=1.0 / D)
                    sc.act(R2[:, 0:n], lnt_[:, 0:n], AF.Exp, scale=-0.5)
                    yield
                    for ch in range(8):
                        sc.tt(tm[:, 0:n], x[:, ch, c0:c0 + n], R2[:, 0:n], ALU.mult)
                        k = ch * 2 + cond
                        sc.act(dst[:, ch, c0:c0 + n], tm[:, 0:n], AF.Identity,
                               bias=shiftv(ln_, wn, ch, cond), scale=A1[:, ln_, wn, k:k + 1])
                        yield

                gens = [tile_gen(ti) for ti in range(3)]
                while gens:
                    for g in list(gens):
                        try:
                            next(g)
                        except StopIteration:
                            gens.remove(g)

        def post_norm(l, which, src, final=False):
            with sc.scope():
                sqs = [tile("sq", [128, 512], F32R) for _ in range(3)]
                tms = [tile("tmp512", [128, 512]) for _ in range(3)]
                Ra = [tile("Ra", [128, 512]) for _ in range(3)]
                lnts = [tile("lnt", [128, 512]) for _ in range(3)]
                ys = [tile("ys", [128, D]) for _ in range(3)] if final else None
                pT_ = PsPool(banks[3:8], 512)

                def tile_gen(ti):
                    c0, n = TT[ti]
                    cond = 0 if ti < 2 else 1
                    bA = banks[ti]
                    sq, tm, R, lnt_ = sqs[ti], tms[ti], Ra[ti], lnts[ti]
                    for ch in range(8):
                        sc.act(sq[:, 0:n], src[:, ch, c0:c0 + n], AF.Square)
                        sc.mm(bA[:, 0:n], onesR[:, :], sq[:, 0:n], start=(ch == 0), stop=(ch == 7))
                        yield
                    sc.act(lnt_[:, 0:n], bA[:, 0:n], AF.Ln, bias=EPS, scale=1.0 / D)
                    sc.act(R[:, 0:n], lnt_[:, 0:n], AF.Exp, scale=-0.5)
                    yield
                    for ch in range(8):
                        sc.tt(tm[:, 0:n], src[:, ch, c0:c0 + n], R[:, 0:n], ALU.mult)
                        k = ch * 2 + cond
                        sc.stt(x[:, ch, c0:c0 + n], tm[:, 0:n], GG[:, l, which, k:k + 1], x[:, ch, c0:c0 + n],
                               ALU.mult, ALU.add)
                        yield
                    if final:
                        for tt in range(c0 // 128, (c0 + n) // 128):
                            yt_ = ys[ti]
                            for half in range(2):
                                ps = pT_.get()
                                for j in range(4):
                                    ch = half * 4 + j
                                    sc.tr(ps[:, j * 128:(j + 1) * 128], x[:, ch, tt * 128:(tt + 1) * 128], ident)
                                sc.copy(yt_[:, half * 512:(half + 1) * 512], ps[:, 0:512], "act" if half == 0 else "dve")
                                yield
                            sc.dma("sp", yout[tt * 128:(tt + 1) * 128, :], yt_[:])

                gens = [tile_gen(ti) for ti in range(3)]
                while gens:
                    for g in list(gens):
                        try:
                            next(g)
                        except StopIteration:
                            gens.remove(g)

        def wout_ap(l, half):
            return woutd[l, :, half * 512:(half + 1) * 512].rearrange("(k p) n -> p k n", p=128)

        def wup_ap(l, s):
            return wupd[l, :, s * 512:(s + 1) * 512].rearrange("(k p) n -> p k n", p=128)

        def out_proj(l, src, dst):
            s0_ = ring.load(wout_ap(l, 0), 8, 512, key=("wout", l, 0))
            slots_ = [s0_, ring.load(wout_ap(l, 1), 8, 512, idx=1 - s0_.idx)]
            for half in range(2):
                slot = slots_[half]
                for nb4 in range(4):
                    nb = half * 4 + nb4
                    for (c0, n) in TT:
                        ps = pG.get()
                        for kc in range(8):
                            sc.mm(ps[:, 0:n], slot.w(kc, nb4 * 128, nb4 * 128 + 128), src[:, kc, c0:c0 + n],
                                  start=(kc == 0), stop=(kc == 7))
                        sc.copy(dst[:, nb, c0:c0 + n], ps[:, 0:n], "act")
                if half == 0:
                    ring.preload(("wup", l, 0), wup_ap(l, 0), 8, 512)

        c2 = tile("c2", [128, 8, 2])
        scond = tile("scond", [128, 8, 2], F32R)
        mrow_ref = [None]

        def mod_gen(slabs, idle=0):
            for (l, sl) in slabs:
                slot = ring.load(wmod[l, :, sl * 512:(sl + 1) * 512].rearrange("(k p) n -> p k n", p=128), 8, 512, idx=mod_slot[0])
                mod_inflight[0] = True
                for _ in range(idle):
                    yield
                for kc in range(8):
                    sc.mm(pS[0:2, 0:512], scond[:, kc, :], slot.w(kc, 0, 512), start=(kc == 0), stop=(kc == 7))
                yield
                sc.copy(mrow_ref[0][:], pS[0:2, 0:512], "act")
                yield
                for j4 in range(4):
                    sc.tr(pS[:, j4 * 2:j4 * 2 + 2], mrow_ref[0][0:2, j4 * 128:(j4 + 1) * 128], cst[0:2, C_ID:C_ID + 2])
                c0_ = sl * 8
                sc.tt(modv[:, l, c0_:c0_ + 8], pS[:, 0:8], bmod[:, l, c0_:c0_ + 8], ALU.add)
                j = sl // 2
                if sl % 2 == 1 and j in (1, 4):
                    which = 0 if j == 1 else 1
                    sc.stt(A1[:, l, which, :], modv[:, l, j * 16:(j + 1) * 16], 1.0, gv[:, l, which * 2, :], ALU.add, ALU.mult)
                if sl % 2 == 1 and j in (2, 5):
                    which = 0 if j == 2 else 1
                    sc.tt(GG[:, l, which, :], modv[:, l, j * 16:(j + 1) * 16], gv[:, l, which * 2 + 1, :], ALU.mult)
                mod_inflight[0] = False
                yield

        def take(g, n):
            for _ in range(n):
                try:
                    next(g)
                except StopIteration:
                    return
                yield

        modbg = [None]
        mod_slot = [None]
        mod_inflight = [False]

        def mod_settle():
            while mod_inflight[0] and modbg[0] is not None:
                try:
                    next(modbg[0])
                except StopIteration:
                    modbg[0] = None

        def ffn(l):
            with sc.scope():
                U = [tile("U", [128, NU, 258]) for _ in range(2)]
                ft = [tile("ft", [128, NU, 256]) for _ in range(2)]
                actg = tile("actg", [128, 4, T], F32R)
                cv = tile("cv", [128, 44, 4])
                tmpw = [tile("tmpw", [128, 512]) for _ in range(2)] if FFN_POOL_ACC else None
                wi = [0]
                sc.dma("sp", cv[:], convd[:, l, :, :])
                for u_ in U:
                    sc.memset(u_[:], 0.0)
                for grp in range(6):
                    npair = 4 if grp < 5 else 2
                    for half in range(npair // 2):
                        s = grp * 2 + half
                        slot = ring.load(wup_ap(l, s), 8, 512, key=("wup", l, s))
                        for pp in range(2):
                            j = s * 2 + pp
                            jj = j - grp * 4
                            for bi, coff in ((1, 256 + pp * 128), (0, pp * 128)):
                                Ub = U[bi]
                                for ti, (c0, n) in enumerate(TT):
                                    ps = pF.get()
                                    for kc in range(8):
                                        sc.mm(ps[:, 0:n], slot.w(kc, coff, coff + 128), B[:, kc, c0:c0 + n],
                                              start=(kc == 0), stop=(kc == 7))
                                    u0, nu = c0 // 256, n // 256
                                    sc.copy(Ub[:, u0:u0 + nu, 1:257], ps[:, 0:n].re("p (a b) -> p a b", a=nu), "act")
                                sc.tt(Ub[:, 1:5, 0:1], Ub[:, 0:4, 256:257], chain[:, 0, 1:5].re("p (a b) -> p a b", b=1), ALU.mult)
                                sc.tt(Ub[:, 0:4, 257:258], Ub[:, 1:5, 1:2], chain[:, 1, 0:4].re("p (a b) -> p a b", b=1), ALU.mult)
                                blk = j if bi == 0 else 22 + j
                                t1 = ft[bi]
                                sc.act(t1[:], Ub[:, :, 1:257], AF.Identity, bias=cv[:, blk, 3:4], scale=cv[:, blk, 1:2])
                                sc.stt(t1[:], Ub[:, :, 0:256], cv[:, blk, 0:1], t1[:], ALU.mult, ALU.add)
                                sc.stt(t1[:], Ub[:, :, 2:258], cv[:, blk, 2:3], t1[:], ALU.mult, ALU.add)
                            sc.act(ft[1][:], ft[1][:], AF.Silu)
                            sc.tt(actg[:, jj, :].re("p (a b) -> p a b", a=NU), ft[1][:], ft[0][:], ALU.mult)
                    slotd = ring.load(wdownd[l, grp * 512:grp * 512 + npair * 128, :].rearrange("(k p) n -> p k n", p=128),
                                      npair, 1024)
                    if l == 0 and grp == 5:
                        ring.preload(("w1in", 0), w1in[:, 0:512].rearrange("(k p) n -> p k n", p=128), 8, 512)
                    for nb in range(8):
                        for (c0, n) in TT:
                            ps = pF.get()
                            for kk in range(npair):
                                sc.mm(ps[:, 0:n], slotd.w(kk, nb * 128, nb * 128 + 128), actg[:, kk, c0:c0 + n],
                                      start=(kk == 0), stop=(kk == npair - 1))
                            if grp == 0:
                                sc.copy(A[:, nb, c0:c0 + n], ps[:, 0:n], "act")
                            elif FFN_POOL_ACC:
                                wi[0] += 1
                                tw = tmpw[wi[0] % 2]
                                sc.copy(tw[:, 0:n], ps[:, 0:n], "act")
                                sc.tt(A[:, nb, c0:c0 + n], tw[:, 0:n], Af[:, nb, c0:c0 + n], ALU.add, eng="pool")
                            else:
                                sc.tt(A[:, nb, c0:c0 + n], ps[:, 0:n], Af[:, nb, c0:c0 + n], ALU.add)
                if l == 0:
                    ring.preload(("w1in", 1), w1in[:, 512:768].rearrange("(k p) n -> p k n", p=128), 8, 256)
            if l == 0:
                post_pre(0, 1, Af, 1, 0, A)
            else:
                post_norm(l, 1, Af, final=True)

        def mixer0():
            qk = tile("qk", [128, T], F32R)
            qkf = qk.v(F32)
            ktok = tile("ktok", [128, 10, 64])
            vtok = tile("vtok", [128, 10, 130], F32R)
            gate = tile("gate", [128, 10, 128])
            mixer_lra = [None]
            mixer_cr = [None]
            Sinit = [[tile("Sinit", [64, 129]) for _ in range(2)] for _ in range(2)]
            Sst = [[tile("Sst", [64, 130], F32R) for _ in range(2)] for _ in range(2)]
            mrep = [[tile("mrep", [128, 1]) for _ in range(2)] for _ in range(2)]
            gn = tile("gn", [128, 2, 128])
            bgt = tile("bgt", [128, 16])
            mmi = tile("mmi", [128, 2, NU, 4])
            gsb = tile("gsb", [128, 10, 16])
            lnf = tile("lnf", [128, 10, 8])
            bsb = tile("bsb", [128, 10, 8])
            csb = tile("csb", [128, 10, 8])
            mcol = tile("mcol", [128, 40])
            sm = {}
            si = [0]

            def ofw(chn, c, wr=False):
                return (B if wr else Bf)[:, chn, c * 128:(c + 1) * 128]

            sc.dma("sp", gn[:], gnd)
            sc.dma("sp", bgt[:], bgd)
            sc.dma("sp", mmi[:], mminit)
            def fill(out, src, val):
                sc.act(out, src, AF.Identity, bias=float(val), scale=0.0)

            fill(vtok[:, :, 128:129], x[:, 0, 0:10].re("p (a b) -> p a b", b=1), 1.0)
            fill(vtok[:, :, 129:130], x[:, 0, 0:10].re("p (a b) -> p a b", b=1), 0.0)
            for a_ in Sst:
                for b_ in a_:
                    fill(b_[:], x[0:64, 0, 0:130], 0.0)
            for a_ in mrep:
                for b_ in a_:
                    sc.memset(b_[:], 0.0)

            if stop == "mixsetup":
                raise Stop()

            def w0_used(slot_idx):
                return 480 if slot_idx == 0 else (464 if slot_idx == 4 else 448)

            def w0_ap(slot_idx):
                return w0cat[:, slot_idx * 512:slot_idx * 512 + w0_used(slot_idx)].rearrange("(k p) n -> p k n", p=128)

            def project(slot_idx, is_gla, head):
                mod_settle()
                used = w0_used(slot_idx)
                slot = ring.load(w0_ap(slot_idx), 8, used, idx=slot_idx % 2, key=("w0", slot_idx))
                mod_slot[0] = slot_idx % 2
                qs, ks = (0.125, 1.0) if is_gla else (1.0, 0.125)
                for (c0, n) in TT:
                    ps = pG.get()
                    for kc in range(8):
                        sc.mm(ps[:, 0:n], slot.w(kc, 0, 128), A[:, kc, c0:c0 + n], start=(kc == 0), stop=(kc == 7))
                    sc.act(qk[0:64, c0:c0 + n], ps[0:64, 0:n], AF.Copy, scale=qs)
                    sc.act(qk[64:128, c0:c0 + n], ps[64:128, 0:n], AF.Copy, scale=ks)
                    if slot_idx == 0:
                        ps2 = pG.get()
                        for kc in range(8):
                            sc.mm(ps2[0:32, 0:n], slot.w(kc, 448, 480), A[:, kc, c0:c0 + n], start=(kc == 0), stop=(kc == 7))
                        sc.copy(mixer_lra[0][0:32, c0:c0 + n], ps2[0:32, 0:n], "dve")
                if stop == "projF":
                    raise Stop()
                NT = 336 if (not is_gla and head == 0) else 320
                for tt in range(10):
                    ps = pG.get()
                    for kc in range(8):
                        sc.mm(ps[:, 0:NT], A[:, kc, tt * 128:(tt + 1) * 128], slot.w(kc, 128, 128 + NT),
                              start=(kc == 0), stop=(kc == 7))
                    sc.act(ktok[:, tt, :], ps[:, 0:64], AF.Copy, scale=ks)
                    sc.act(gate[:, tt, :], ps[:, 192:320], AF.Silu if is_gla else AF.Sigmoid)
                    sc.copy(vtok[:, tt, 0:128], ps[:, 64:192], "dve")
                    if NT == 336:
                        sc.tt(gsb[:, tt, :], ps[:, 320:336], bgt[:, :], ALU.add)

            tasks = []

            def interleave(gens):
                tasks[:] = list(gens)
                while tasks:
                    for g in list(tasks):
                        try:
                            next(g)
                        except StopIteration:
                            tasks.remove(g)

            def both(ga, gb, res):
                act = [gb, ga] if BOTH_SWAP else [ga, gb]
                while act:
                    for g in list(act):
                        if g is None:
                            act.remove(g)
                            continue
                        try:
                            next(g)
                        except StopIteration as e_:
                            if g is ga:
                                res["v"] = e_.value
                            act.remove(g)
                    yield

            def warm_gen():
                while True:
                    for _w in range(WARM_MIX):
                        sc.mm(banks[3][:, 0:512], onesR[:, :], A[:, 0, 0:512])
                    yield

            def run_head(chains):
                tasks[:] = list(chains)
                while tasks:
                    for g in list(tasks):
                        try:
                            next(g)
                        except StopIteration:
                            tasks.remove(g)
                    if modbg[0] is not None:
                        try:
                            next(modbg[0])
                        except StopIteration:
                            modbg[0] = None

            def delayed(n, g):
                for _ in range(n):
                    yield
                yield from g

            def out_stage(smt, osum, c, is_gla, head):
                y1 = smt("y1", [128, 128])
                ss = smt("ss", [128, 1])
                sc.act(y1[:], osum[:, 0:128], AF.Square, accum=ss[:])
                lr_ = smt("lr_", [128, 1])
                sc.act(lr_[:], ss[:], AF.Ln, bias=EPS, scale=1.0 / 128)
                rs = smt("rs", [128, 1])
                sc.act(rs[:], lr_[:], AF.Exp, scale=-0.5)
                yield
                sc.stt(y1[:], osum[:, 0:128], rs[:], gn[:, 0 if is_gla else 1, :], ALU.mult, ALU.mult)
                sc.tt(y1[:], y1[:], gate[:, c, :], ALU.mult)
                yield
                pt = pQ.get()
                sc.tr(pt[:, 0:128], y1[:], ident)
                chn = head if is_gla else 4 + head
                sc.copy(B[:, chn, c * 128:(c + 1) * 128], pt[:, 0:128], "act")
                yield

            def chunk_order(d):
                return list(range(10)) if d == 0 else list(range(9, -1, -1))

            def cslice(d, a, b):
                o = a if d == 0 else b
                return cst[:, o:o + 128]

            def mk_smt(prefix):
                def f(name, shape, nbuf=1, par=0, dtype=F32):
                    key = prefix + name
                    if key not in sm:
                        sm[key] = [tile(key, shape, dtype) for _ in range(nbuf)]
                    return sm[key][par % nbuf]
                return f

            def gla_chain(i, d, wg):
                smt = mk_smt("g%d_" % d)
                triG = mixer_cr[0][:, d * 128:(d + 1) * 128]
                maskP = cslice(d, C_MPF, C_MPB)
                order = chunk_order(d)
                st = {"cur": 0}

                def PRE(k):
                    c = order[k]
                    cs = slice(c * 128, (c + 1) * 128)
                    if ((c % 2 == 0) if d == 0 else (c % 2 == 1)):
                        sc.dma("sp", Sinit[d][(c // 2) % 2][:, 0:128], ginit[d, c // 2, i, :, :])
                    pz = pQ.get()
                    sc.mm(pz[:, 0:128], mixer_lra[0][0:33, cs], wg[0:33, d, :])
                    lnv = smt("lnv", [128, 128], dtype=F32R)
                    e1 = lnv.v(F32)
                    sc.act(lnv[:], pz[:, 0:128], AF.Exp, scale=-1.0)
                    yield
                    sc.act(lnv[:], e1[:], AF.Ln, bias=1.0)
                    yield
                    pbT = pQ.get()
                    sc.mm(pbT[:, 0:128], lnv[:, :], triG)
                    eqk = smt("eqk", [128, 128])
                    sc.act(eqk[0:64, :], pbT[0:64, 0:128], AF.Exp)
                    sc.act(eqk[64:128, :], pbT[64:128, 0:128], AF.Exp, scale=-1.0)
                    yield
                    pbt = pQ.get()
                    sc.mm(pbt[:, 0:64], triG, lnv[:, 0:64])
                    ekt = smt("ekt", [128, 64])
                    sc.act(ekt[:], pbt[:, 0:64], AF.Exp, scale=-1.0)
                    yield
                    qt = smt("qt", [64, 128], 2, k, dtype=F32R)
                    sc.tt(qt[:], qkf[0:64, cs], eqk[0:64, :], ALU.mult)
                    kt = smt("kt", [64, 128], dtype=F32R)
                    sc.tt(kt[:], qkf[64:128, cs], eqk[64:128, :], ALU.mult)
                    ktk = smt("ktk", [128, 64], 2, k, dtype=F32R)
                    sc.tt(ktk[:], ktok[:, c, :], ekt[:], ALU.mult)
                    yield
                    psT = pQ.get()
                    sc.mm(psT[:, 0:128], kt[:, :], qt[:, :])
                    P = smt("P", [128, 128], 2, k, dtype=F32R)
                    sc.tt(P[:], psT[:, 0:128], maskP, ALU.mult)
                    yield
                    ecl = smt("ecl", [64, 1], 2, k)
                    sc.copy(ecl[:], eqk[0:64, 127:128] if d == 0 else eqk[0:64, 0:1], "dve")
                    return dict(qt=qt, P=P, ktk=ktk, ecl=ecl)

                def POST(k, pre):
                    c = order[k]
                    u = c // 2
                    first = (c % 2 == 0) if d == 0 else (c % 2 == 1)
                    last = not first
                    cur = st["cur"]
                    if first:
                        Si = Sinit[d][u % 2]
                        sc.stt(Sst[d][1 - cur][:, 0:128], Sst[d][cur].v(F32)[:, 0:128], chain[0:64, d, u:u + 1],
                               Si[:, 0:128], ALU.mult, ALU.add)
                        cur = 1 - cur
                    S = Sst[d][cur]
                    qt, P, ktk, ecl = pre["qt"], pre["P"], pre["ktk"], pre["ecl"]
                    po = pQ.get()
                    sc.mm(po[:, 0:128], qt[:, :], S[:, 0:128], start=True, stop=False)
                    sc.mm(po[:, 0:128], P[:, :], vtok[:, c, 0:128], start=False, stop=True)
                    pdS = pQ.get()
                    sc.mm(pdS[0:64, 0:128], ktk[:, :], vtok[:, c, 0:128])
                    yield
                    Sn = Sst[d][1 - cur]
                    sc.tt(Sn[:, 0:128], pdS[0:64, 0:128], S.v(F32)[:, 0:128], ALU.add)
                    sc.ts(Sn[:, 0:128], Sn.v(F32)[:, 0:128], ecl[:], ALU.mult)
                    st["cur"] = 1 - cur
                    if last:
                        sc.dma("sp", gla_o[d, u, i, :, :], Sn.v(F32)[:, 0:128])
                    yield
                    if k < 5:
                        sc.copy(ofw(i, c, True), po[:, 0:128], "act")
                        yield
                    else:
                        osum = smt("osum", [128, 128], 2, k)
                        sc.tt(osum[:], po[:, 0:128], ofw(i, c), ALU.add)
                        yield
                        tasks.append(delayed(OUT_DELAY, out_stage(smt, osum, c, True, i)))

                cur_pre = yield from PRE(0)
                for k in range(10):
                    res = {}
                    yield from both(PRE(k + 1) if k + 1 < 10 else None, POST(k, cur_pre), res)
                    cur_pre = res.get("v")

            def gla_head(i, wg):
                project(i, True, i)
                ring.preload(("w0", i + 1), w0_ap(i + 1), 8, w0_used(i + 1), idx=(i + 1) % 2)
                sc.dma("pool", wg[:], wgd[:, :, i * 128:(i + 1) * 128])
                run_head([gla_chain(i, 0, wg), gla_chain(i, 1, wg)])

            def mlstm_gates():
                e8 = tile("e8", [128, 10, 8])
                sc.act(e8[:], gsb[:, :, 8:16], AF.Exp, scale=-1.0)
                sc.act(lnf[:], e8[:], AF.Ln, bias=1.0)
                for d in range(2):
                    triM = cslice(d, C_TMF, C_TMB)
                    for tt in range(10):
                        pb = pQ.get()
                        sc.mm(pb[:, 0:4], triM, lnf[:, tt, d * 4:(d + 1) * 4])
                        sc.copy(bsb[:, tt, d * 4:(d + 1) * 4], pb[:, 0:4], "act")
                sc.tt(csb[:], gsb[:, :, 0:8], bsb[:], ALU.subtract)

            def mlstm_chain(i, d):
                smt = mk_smt("m%d_" % d)
                maskadd = cslice(d, C_MAF, C_MAB)
                sel = cslice(d, C_SLF, C_SLB)
                kk = d * 4 + i
                order = chunk_order(d)
                st = {"cur": 0}

                def PRE(k):
                    c = order[k]
                    cs = slice(c * 128, (c + 1) * 128)
                    if ((c % 2 == 0) if d == 0 else (c % 2 == 1)):
                        sc.dma("sp", Sinit[d][(c // 2) % 2][:], minit[d, c // 2, i, :, :])
                    ccol = csb[:, c, kk:kk + 1]
                    bcol = bsb[:, c, kk:kk + 1]
                    kTc = smt("kTc", [64, 128], 2, k, dtype=F32R)
                    sc.copy(kTc[:], qkf[64:128, cs], "act")
                    diagc = smt("diagc", [128, 128])
                    sc.ts(diagc[:], ident, ccol, ALU.mult)
                    yield
                    pcb = pQ.get()
                    sc.mm(pcb[:, 0:128], ones, diagc[:, :])
                    Dm = smt("Dm", [128, 128], 2, k)
                    sc.stt(Dm[:], pcb[:, 0:128], bcol, maskadd, ALU.add, ALU.add)
                    yield
                    rmx = smt("rmx", [128, 1], 2, k)
                    sc.rmax(rmx[:], Dm[:])
                    yield
                    return dict(kTc=kTc, Dm=Dm, rmx=rmx)

                def POST(k, pre):
                    c = order[k]
                    cs = slice(c * 128, (c + 1) * 128)
                    u = c // 2
                    first = (c % 2 == 0) if d == 0 else (c % 2 == 1)
                    last = not first
                    cur = st["cur"]
                    ccol = csb[:, c, kk:kk + 1]
                    bcol = bsb[:, c, kk:kk + 1]
                    kTc, Dm, rmx = pre["kTc"], pre["Dm"], pre["rmx"]
                    if first:
                        Si = Sinit[d][u % 2]
                        sc.stt(Sst[d][1 - cur][:, 0:129], Sst[d][cur].v(F32)[:, 0:129], chain[0:64, d, u:u + 1], Si[:], ALU.mult, ALU.add)
                        sc.stt(mrep[d][1 - cur][:], mrep[d][cur][:], chain[:, d, u:u + 1], mmi[:, d, u, i:i + 1],
                               ALU.mult, ALU.add)
                        cur = 1 - cur
                    Cg = Sst[d][cur]
                    mr = mrep[d][cur]
                    bm = smt("bm", [128, 1])
                    sc.tt(bm[:], bcol, mr[:], ALU.add)
                    mb = smt("mb", [128, 2])
                    mbf = mb
                    sc.tt(mb[:, 0:1], bm[:], rmx[:], ALU.max)
                    sc.copy(mb[:, 1:2], bcol, "dve")
                    yield
                    psl = pQ.get()
                    sc.mm(psl[:, 0:2], sel, mb[:, :])
                    mbn = smt("mbn", [128, 2])
                    sc.copy(mbn[:], psl[:, 0:2], "act")
                    yield
                    dlt = smt("dlt", [128, 1])
                    sc.tt(dlt[:], mbn[:, 1:2], mbn[:, 0:1], ALU.subtract)
                    t2 = smt("t2", [128, 1])
                    sc.tt(t2[:], dlt[:], mr[:], ALU.add)
                    sc.copy(mrep[d][1 - cur][:], mbn[:, 0:1], "dve")
                    nm = smt("nm", [128, 1])
                    sc.ts(nm[:], mbf[:, 0:1], -1.0, ALU.mult)
                    yield
                    srcw = smt("srcw", [128, 1])
                    sc.act(srcw[:], ccol, AF.Exp, bias=dlt[:])
                    cw = smt("cw", [128, 1])
                    sc.act(cw[:], t2[:], AF.Exp)
                    Wi = smt("Wi", [128, 128])
                    sc.act(Wi[:], Dm[:], AF.Exp, bias=nm[:])
                    winter = smt("winter", [128, 1])
                    sc.act(winter[:], bm[:], AF.Exp, bias=nm[:])
                    enm = smt("enm", [128, 1])
                    sc.act(enm[:], nm[:], AF.Exp)
                    yield
                    ksw = smt("ksw", [128, 64], dtype=F32R)
                    sc.ts(ksw[:], ktok[:, c, :], srcw[:], ALU.mult)
                    pit = pG.get()
                    sc.mm(pit[:, 0:130], qk[0:64, cs], Cg[:, 0:130])
                    its = smt("its", [128, 129])
                    sc.act(its[:], pit[:, 0:129], AF.Copy, scale=winter[:])
                    yield
                    pdC = pG.get()
                    sc.mm(pdC[0:64, 0:130], ksw[:, :], vtok[:, c, 0:130])
                    Cn = Sst[d][1 - cur]
                    sc.stt(Cn[:], Cg.v(F32)[:], cw[0:64, :], pdC[0:64, 0:130], ALU.mult, ALU.add)
                    st["cur"] = 1 - cur
                    if last:
                        sc.dma("sp", mC_o[d, u, i, :, :], Cn.v(F32)[:, 0:129])
                        col = (d * NU + u) * 4 + i
                        sc.copy(mcol[:, col:col + 1], mbn[:, 0:1], "dve")
                    yield
                    pqk = pQ.get()
                    sc.mm(pqk[:, 0:128], qk[0:64, cs], kTc[:, :])
                    qkw = smt("qkw", [128, 128])
                    sc.tt(qkw[:], pqk[:, 0:128], Wi[:], ALU.mult)
                    yield
                    pT = pQ.get()
                    sc.tr(pT[:, 0:128], qkw[:], ident)
                    qkT = smt("qkT", [128, 128], dtype=F32R)
                    sc.copy(qkT[:], pT[:, 0:128], "act")
                    yield
                    pin = pG.get()
                    sc.mm(pin[:, 0:130], qkT[:, :], vtok[:, c, 0:130])
                    nd = smt("nd", [128, 129])
                    sc.tt(nd[:], pin[:, 0:129], its[:], ALU.add)
                    yield
                    nden = smt("nden", [128, 1])
                    sc.ts(nden[:], nd[:, 128:129], -1.0, ALU.mult)
                    aden = smt("aden", [128, 1])
                    sc.tt(aden[:], nden[:], nd[:, 128:129], ALU.max)
                    dd = smt("dd", [128, 1])
                    sc.tt(dd[:], aden[:], enm[:], ALU.max)
                    rden = smt("rden", [128, 1])
                    sc.op("dve", lambda e, o=rden[:], i_=dd[:]: e.reciprocal(o.ap, i_.ap), [dd[:]], [rden[:]])
                    yield
                    if k < 5:
                        sc.ts(ofw(4 + i, c, True), nd[:, 0:128], rden[:], ALU.mult)
                        yield
                    else:
                        osum = smt("osum", [128, 128], 2, k)
                        sc.stt(osum[:], nd[:, 0:128], rden[:], ofw(4 + i, c), ALU.mult, ALU.add)
                        yield
                        tasks.append(delayed(OUT_DELAY, out_stage(smt, osum, c, False, i)))

                cur_pre = yield from PRE(0)
                for k in range(10):
                    res = {}
                    yield from both(PRE(k + 1) if k + 1 < 10 else None, POST(k, cur_pre), res)
                    cur_pre = res.get("v")

            def mlstm_head(i):
                project(4 + i, False, i)
                if i < 3:
                    ring.preload(("w0", 5 + i), w0_ap(5 + i), 8, w0_used(5 + i), idx=(5 + i) % 2)
                else:
                    ring.preload(("wout", 0, 0), wout_ap(0, 0), 8, 512, idx=0)
                if i == 0:
                    mlstm_gates()
                run_head([mlstm_chain(i, 0), mlstm_chain(i, 1)])

            with sc.scope():
                lraT = tile("lraT", [33, T], F32R)
                wgt = [tile("wgt", [33, 2, 128], F32R) for _ in range(1)]
                cstR = tile("cstR", [128, 256], F32R)
                sc.copy(cstR[:, 0:128], cst[:, C_TGF:C_TGF + 128], "act")
                sc.copy(cstR[:, 128:256], cst[:, C_TGB:C_TGB + 128], "act")
                mixer_cr[0] = cstR
                fill(lraT[32:33, :], x[32:33, 0, :], 1.0)
                mixer_lra[0] = lraT
                for i in range(4):
                    gla_head(i, wgt[0])
                    if stop == "gla%d" % i:
                        raise Stop()
            with sc.scope():
                for i in range(4):
                    mlstm_head(i)
                    if stop == "mlstm%d" % i:
                        raise Stop()
                sc.dma("sp", mm_o, mcol[0:1, :])

        def mixer1():
            def warm(n_=None):
                for _w in range(WARM_ATT if n_ is None else n_):
                    sc.mm(banks[3][:, 0:512], onesR[:, :], ckvC[:, 0, 0:512])

            gqkv = tile("gqkv", [128, 5])
            rope = tile("rope", [128, T])
            kmx = tile("kmx", [128, 4])
            kmax2 = tile("kmax2", [128, 1])
            krmax = tile("krmax", [128, 1])
            ckvC = tile("ckvC", [128, 2, 512], F32R)
            KRc = tile("KRc", [70, 512], F32R)
            QR = tile("QR", [70, T], F32R)
            sc.dma("sp", gqkv[:], gqkvd)
            sc.dma("sp", rope[:], ropeT)
            sc.dma("pool", KRc[64:70, :], ktab[:, 0:512])
            sc.dma("pool", QR[65:70, :], qtab)
            with sc.scope():
                cstage = tile("cstage", [128, 4, 256])
                kstage = tile("kstage", [128, 4, 64])
                sc.dma("sp", cstage[:], ckvc.rearrange("(a p) n -> p a n", p=128))
                sc.dma("sp", kstage[:], krc.rearrange("(a p) n -> p a n", p=128))
                for a_ in range(4):
                    for rc in range(2):
                        pt = pX.get()
                        sc.tr(pt[:, 0:128], cstage[:, a_, rc * 128:(rc + 1) * 128], ident)
                        sc.copy(ckvC[:, rc, a_ * 128:(a_ + 1) * 128], pt[:, 0:128], "act")
                    pt = pX.get()
                    sc.tr(pt[0:64, 0:128], kstage[:, a_, :], ident)
                    sc.copy(KRc[0:64, a_ * 128:(a_ + 1) * 128], pt[0:64, 0:128], "act")
            with sc.scope():
                qa_t = tile("qa_t", [128, 3, 512])
                kv_t = tile("kv_t", [128, 2, 512])
                kr_t = tile("kr_t", [128, 512])
                kr_s = tile("kr_s", [64, 512])
                kr_u = tile("kr_u", [64, 512])
                stg = tile("stg", [128, 4, 256])
                stg2 = tile("stg2", [128, 4, 64])
                nt = dict(sq=[tile("sq", [128, 512], F32R)], R=[tile("R", [128, 512])], lnt=tile("lnt", [128, 512]), c=[0, 0, 0])
                slot0 = ring.load(w1in[:, 0:512].rearrange("(k p) n -> p k n", p=128), 8, 512, key=("w1in", 0))
                slot1 = ring.load(w1in[:, 512:768].rearrange("(k p) n -> p k n", p=128), 8, 256, key=("w1in", 1))

                def fproj(slot, coff, c0, n):
                    ps = pX.get()
                    for kc in range(8):
                        sc.mm(ps[:, 0:n], slot.w(kc, coff, coff + 128), A[:, kc, c0:c0 + n], start=(kc == 0), stop=(kc == 7))
                    warm(WARM_PROJ)
                    return ps

                for ti, (c0, n) in enumerate(TT):
                    for b_ in range(3):
                        ps = fproj(slot0, b_ * 128, c0, n)
                        sc.copy(qa_t[:, b_, 0:n], ps[:, 0:n], "act")
                    ps = fproj(slot0, 384, c0, n)
                    sc.copy(kv_t[:, 0, 0:n], ps[:, 0:n], "act")
                    ps = fproj(slot1, 0, c0, n)
                    sc.copy(kv_t[:, 1, 0:n], ps[:, 0:n], "act")
                    ps = fproj(slot1, 128, c0, n)
                    sc.copy(kr_u[:, 0:n], ps[0:64, 0:n], "dve")
                    sc.tt(kr_t[:, 0:n], ps[:, 0:n], rope[:, c0:c0 + n], ALU.mult)
                    sc.copy(kr_s[:, 0:n], kr_t[64:128, 0:n], "act")
                    sc.tt(A[0:64, 5, c0:c0 + n], kr_t[0:64, 0:n], kr_s[:, 0:n], ALU.add)
                    sc.dma("pool", A[64:70, 5, c0:c0 + n], ktab[:, 512 + c0:512 + c0 + n])
                    R = compute_R(nt, lambda ch: qa_t[:, ch, 0:n], 3, 384, n)
                    for ch in range(3):
                        sc.stt(A[:, ch, c0:c0 + n], qa_t[:, ch, 0:n], gqkv[:, ch:ch + 1], R[:, 0:n], ALU.mult, ALU.mult)
                    R = compute_R(nt, lambda ch: kv_t[:, ch, 0:n], 2, 256, n)
                    for ch in range(2):
                        sc.stt(kv_t[:, ch, 0:n], kv_t[:, ch, 0:n], gqkv[:, 3 + ch:4 + ch], R[:, 0:n], ALU.mult, ALU.mult)
                        sc.copy(A[:, 3 + ch, c0:c0 + n], kv_t[:, ch, 0:n], "act")
                    na = n // 128
                    for a_ in range(na):
                        pt = pX.get()
                        for ch in range(2):
                            sc.tr(pt[:, ch * 128:(ch + 1) * 128], kv_t[:, ch, a_ * 128:(a_ + 1) * 128], ident)
                        sc.copy(stg[:, a_, :], pt[:, 0:256], "dve")
                        pt2 = pX.get()
                        sc.tr(pt2[:, 0:64], kr_u[:, a_ * 128:(a_ + 1) * 128], cst[0:64, C_ID:C_ID + 64])
                        sc.copy(stg2[:, a_, :], pt2[:, 0:64], "dve")
                    sc.dma("sp", ckv_o[c0:c0 + n, :].rearrange("(a p) n -> p a n", p=128), stg[:, 0:na, :])
                    sc.dma("sp", kr_o[c0:c0 + n, :].rearrange("(a p) n -> p a n", p=128), stg2[:, 0:na, :])
            phase_end("att_proj")
            with sc.scope():
                Kh = tile("Kh", [128, NKEY], F32R)
                Vh = tile("Vh", [128, 14, 128], F32R)
                sqa = tile("sqa", [128, 512], F32R)
                sqb = tile("sqb", [64, 512], F32R)
                sqq = tile("sqq", [128, 512], F32R)
                rdt = tile("rdt", [128, 512])
                lnd = tile("lnd", [128, 512])
                qrt = tile("qrt", [128, 512])
                qrs = tile("qrs", [64, 512])

                def Qn(c0, n):
                    return A[:, 6, c0:c0 + n]

                def PT(i, n):
                    return A[:, 7, i * 512:i * 512 + n]

                def ckv_src(rc, k0, n):
                    return ckvC[:, rc, k0:k0 + n] if k0 < 512 else A[:, 3 + rc, k0 - 512:k0 - 512 + n]

                def KR_src(k0, n, rows=70):
                    return KRc[0:rows, k0:k0 + n] if k0 < 512 else A[0:rows, 5, k0 - 512:k0 - 512 + n]

                KT4 = [(0, 512), (512, 512), (1024, 512), (1536, 256)]
                po_i = [0]
                norm_pend = []
                lnds = [lnd, tile("lnd2", [128, 512])]
                for ki, (k0, n) in enumerate(KT4):
                    sc.act(sqb[:, 0:n], (KRc.v(F32)[0:64, k0:k0 + n] if k0 < 512 else Af[0:64, 5, k0 - 512:k0 - 512 + n]), AF.Square)
                    sc.mm(pS[:, 0:n], onesR[0:64, :], sqb[:, 0:n])
                    sc.rmax(kmx[:, ki:ki + 1], pS[:, 0:n])
                sc.rmax(krmax[:], kmx[:, 0:4])
                slotkv = ring.load(wkvbd.rearrange("(k p) n -> p k n", p=128), 2, 2048, idx=0)
                slotq = None
                for h in range(8):
                    if h % 4 == 0:
                        slotq = ring.load(wqbd[:, (h // 4) * 1024:(h // 4 + 1) * 1024].rearrange("(k p) n -> p k n", p=128), 3, 1024, idx=1)
                    hh = h % 4
                    def k_chain():
                        for ki, (k0, n) in enumerate(KT4):
                            ps = pX.get()
                            for rc in range(2):
                                sc.mm(ps[:, 0:n], slotkv.w(rc, h * 256, h * 256 + 128), ckv_src(rc, k0, n), start=(rc == 0), stop=(rc == 1))
                            warm()
                            sc.copy(Kh[:, k0:k0 + n], ps[:, 0:n], "act")
                            sc.act(sqa[:, 0:n], ps[:, 0:n], AF.Square)
                            yield
                            sc.mm(pS[:, 0:n], onesR[:, :], sqa[:, 0:n])
                            sc.rmax(kmx[:, ki:ki + 1], pS[:, 0:n])
                            yield
                        sc.rmax(kmax2[:], kmx[:, 0:4])
                        sc.tt(kmax2[:], kmax2[:], krmax[:], ALU.add)
                        yield

                    def v_chain():
                        for kt in range(14):
                            ps = pX.get()
                            for rc in range(2):
                                sc.mm(ps[:, 0:128], ckv_src(rc, kt * 128, 128), slotkv.w(rc, h * 256 + 128, h * 256 + 256),
                                      start=(rc == 0), stop=(rc == 1))
                            sc.copy(Vh[:, kt, :], ps[:, 0:128], "dve")
                            yield

                    def q_chain():
                        for (c0, n) in TT:
                            ps = pX.get()
                            for kc in range(3):
                                sc.mm(ps[:, 0:n], slotq.w(kc, hh * 256, hh * 256 + 128), A[:, kc, c0:c0 + n], start=(kc == 0), stop=(kc == 2))
                            warm()
                            sc.copy(Qn(c0, n), ps[:, 0:n], "act")
                            sc.act(sqq[:, 0:n], ps[:, 0:n], AF.Square)
                            yield
                            ps2 = pX.get()
                            for kc in range(3):
                                sc.mm(ps2[:, 0:n], slotq.w(kc, hh * 256 + 128, hh * 256 + 256), A[:, kc, c0:c0 + n], start=(kc == 0), stop=(kc == 2))
                            warm()
                            sc.tt(qrt[:, 0:n], ps2[:, 0:n], rope[:, c0:c0 + n], ALU.mult)
                            yield
                            sc.copy(qrs[:, 0:n], qrt[64:128, 0:n], "act")
                            sc.tt(QR[0:64, c0:c0 + n], qrt[0:64, 0:n], qrs[:, 0:n], ALU.add)
                            sc.act(sqb[:, 0:n], QR.v(F32)[0:64, c0:c0 + n], AF.Square)
                            yield
                            sc.mm(pD[:, 0:n], onesR[:, :], sqq[:, 0:n], start=True, stop=False)
                            sc.mm(pD[:, 0:n], onesR[0:64, :], sqb[:, 0:n], start=False, stop=True)
                            sc.copy(QR[64:65, c0:c0 + n], pD[64:65, 0:n], "act")
                            yield

                    gens = [k_chain(), v_chain(), q_chain()]
                    while gens:
                        for g in list(gens):
                            try:
                                next(g)
                            except StopIteration:
                                gens.remove(g)
                    if h == 7:
                        ring.preload(("wout", 1, 0), wout_ap(1, 0), 8, 512, idx=1)
                    qrow = QR.v(F32)[64:65, :]
                    sc.act(QR[64:65, :], qrow, AF.Ln, scale=kmax2[64:65, :])
                    sc.act(QR[64:65, :], qrow, AF.Exp, scale=0.5)
                    sc.act(QR[64:65, :], qrow, AF.Copy, scale=-1.001)
                    for (c0, n, kbs) in ((0, 512, list(range(12))), (512, 512, list(range(12))), (1024, 256, [12, 13])):
                        po_i[0] += 1
                        pO = (banks[5], banks[3])[po_i[0] % 2]
                        lnd = lnds[po_i[0] % 2]
                        pend = None
                        for idx, kb in enumerate(kbs):
                            if idx == min(2, len(kbs) - 1) and norm_pend:
                                norm_pend.pop(0)()
                            ps = pX.get()
                            sc.mm(ps[:, 0:n], Kh[:, kb * 128:(kb + 1) * 128], Qn(c0, n), start=True, stop=False)
                            sc.mm(ps[:, 0:n], KR_src(kb * 128, 128), QR[0:70, c0:c0 + n], start=False, stop=True)
                            if pend is not None:
                                pidx, pkb, ppt = pend
                                sc.mm(pO[:, 0:n], Vh[:, pkb, :], ppt, start=(pidx == 0), stop=False)
                            pt = PT(idx % 2, n)
                            sc.act(pt, ps[:, 0:n], AF.Exp, scale=ATT_SCALE)
                            ptf = Af[:, 7, (idx % 2) * 512:(idx % 2) * 512 + n]
                            if idx == 0:
                                sc.copy(lnd[:, 0:n], ptf, "dve")
                            else:
                                sc.tt(lnd[:, 0:n], lnd[:, 0:n], ptf, ALU.add)
                            pend = (idx, kb, pt)
                        pidx, pkb, ppt = pend
                        sc.mm(pO[:, 0:n], Vh[:, pkb, :], ppt, start=(pidx == 0), stop=True)
                        sc.mm(pD[:, 0:n], ones, lnd[:, 0:n])

                        def finish(c0=c0, n=n, pO=pO, h=h):
                            sc.act(rdt[:, 0:n], pD[:, 0:n], AF.Ln)
                            sc.act(rdt[:, 0:n], rdt[:, 0:n], AF.Exp, scale=-1.0)
                            sc.tt(B[:, h, c0:c0 + n], pO[:, 0:n], rdt[:, 0:n], ALU.mult)

                        while norm_pend:
                            norm_pend.pop(0)()
                        norm_pend.append(finish)
                    while norm_pend:
                        norm_pend.pop(0)()
                    if stop == "att_h%d" % h:
                        raise Stop()

        try:
            sc.dma("sp", cst[:], cstd)
            sc.dma("sp", chain[:], chaind)
            sc.dma("sp", bmod[:], bmodT)
            sc.dma("sp", gv[:], gvecT)
            sc.copy(onesR[:], ones, "act")
            sc.dma("sp", c2[:], cond2T)
            sc.act(scond[:], c2[:], AF.Silu)
            with sc.scope():
                xt = [tile("xt", [128, D]) for _ in range(2)]
                mrow_ref[0] = tile("mrow", [2, 512])
                nt = norm_tiles()
                R3 = [tile("R3", [128, 512]) for _ in range(3)]
                g0 = mod_gen([(0, s_) for s_ in range(4)], idle=2)

                def adv(g, n_=1):
                    for _ in range(n_):
                        try:
                            next(g)
                        except StopIteration:
                            return

                for tt in range(10):
                    t = xt[tt % 2]
                    sc.dma("sp", t[:], xin[tt * 128:(tt + 1) * 128, :])
                    for half in range(2):
                        ps = pG.get()
                        for j in range(4):
                            ch = half * 4 + j
                            sc.tr(ps[:, j * 128:(j + 1) * 128], t[:, ch * 128:(ch + 1) * 128], ident)
                        sc.copy(x[:, half * 4:half * 4 + 4, tt * 128:(tt + 1) * 128],
                                ps[:, 0:512].re("p (a b) -> p a b", a=4), "act" if half == 0 else "dve")
                    adv(g0)
                    if tt in (3, 7, 9):
                        ti = {3: 0, 7: 1, 9: 2}[tt]
                        c0, n = TT[ti]
                        Rr = compute_R(nt, lambda ch, c0=c0, n=n: x[:, ch, c0:c0 + n], 8, D, n, bank=banks[5])
                        sc.copy(R3[ti][:, 0:n], Rr[:, 0:n], "dve")
                adv(g0, 1000)
                ring.preload(("w0", 0), w0cat[:, 0:480].rearrange("(k p) n -> p k n", p=128), 8, 480, idx=0)
                modbg[0] = mod_gen([(0, s_) for s_ in range(4, 12)] + [(1, s_) for s_ in range(12)], idle=MOD_IDLE)
                for ti, (c0, n) in enumerate(TT):
                    cond = 0 if ti < 2 else 1
                    for ch in range(8):
                        nt["c"][2] += 1
                        tm = nt["tmp"][nt["c"][2] % 2]
                        sc.tt(tm[:, 0:n], x[:, ch, c0:c0 + n], R3[ti][:, 0:n], ALU.mult)
                        k = ch * 2 + cond
                        sc.act(A[:, ch, c0:c0 + n], tm[:, 0:n], AF.Identity,
                               bias=shiftv(0, 0, ch, cond), scale=A1[:, 0, 0, k:k + 1])
            dump("x0", x[:], [128, 8, T])
            phase_end("norm0")
            dump("h0", Af[:], [128, 8, T])
            phase_end("norm0")
            with sc.scope():
                mrow_ref[0] = tile("mrow", [2, 512])
                mixer0()
                mod_slot[0] = None
                if modbg[0] is not None:
                    for _ in modbg[0]:
                        pass
            dump("modv", modv[:], [128, 2, 96])
            dump("mixed0", Bf[:], [128, 8, T])
            phase_end("mixer0")
            post_pre(0, 0, Af, 0, 1, B, oproj=(0, B))
            dump("x1", x[:], [128, 8, T])
            phase_end("mix0_done")
            ffn(0)
            dump("x2", x[:], [128, 8, T])
            phase_end("ffn0")
            with sc.scope():
                mixer1()
            dump("attn", Bf[:], [128, 8, T])
            phase_end("mixer1")
            post_pre(1, 0, Af, 1, 1, B, oproj=(1, B))
            dump("x3", x[:], [128, 8, T])
            phase_end("mix1_done")
            ffn(1)
            phase_end("ffn1")
        except Stop:
            pass
        sc.flush()
        dump("endx", x[:], [128, 8, T])
        dump("endA", Af[:], [128, 8, T])
        dump("endB", Bf[:], [128, 8, T])
        if stop is not None and stop != "ffn1":
            sc.flush()
            with sc.scope():
                ys = [tile("ys", [128, D]) for _ in range(2)]
                for tt in range(10):
                    yt_ = ys[tt % 2]
                    for half in range(2):
                        ps = pG.get()
                        for j in range(4):
                            ch = half * 4 + j
                            sc.tr(ps[:, j * 128:(j + 1) * 128], x[:, ch, tt * 128:(tt + 1) * 128], ident)
                        sc.copy(yt_[:, half * 512:(half + 1) * 512], ps[:, 0:512], "act" if half == 0 else "dve")
                    sc.dma("sp", yout[tt * 128:(tt + 1) * 128, :], yt_[:])
        sc.flush(final=True)
        stats = dict(sc.stats)
    return nc, dbg_out, stats


def _core_units(c):
    if c < 6:
        return "prompt", [5 * c + j for j in range(4)], 5 * c + 4
    return "sample", c - 6, 30 + (c - 6)


_SW = np.array([(d + 16) if (d % 32) < 16 else (d - 16) for d in range(64)])


def _rope_table(mode):
    tab = np.zeros((128, T), np.float32)
    tab[0:64, :] = 1.0
    if mode == "sample":
        pos = np.arange(1024)
        row = (pos // 64).astype(np.float32)
        col = (pos % 64).astype(np.float32)
        inv = (1.0 / (np.float32(10000.0) ** (np.arange(0, 32, 2, dtype=np.float32) / np.float32(32)))).astype(np.float32)
        for d in range(64):
            base = row if d < 32 else col
            ang = (base * inv[d % 16]).astype(np.float32)
            tab[d, 0:1024] = np.cos(ang)
            sgn = -1.0 if (d % 32) < 16 else 1.0
            tab[64 + d, 0:1024] = sgn * np.sin(ang)
    return tab


def prep_weights(inp):
    f = lambda a: np.ascontiguousarray(np.asarray(a, dtype=np.float32))
    W = {}
    W["wmod"] = f(np.stack([inp["l0_w_mod"], inp["l1_w_mod"]]))
    bm = np.stack([inp["l0_b_mod"], inp["l1_b_mod"]])
    bmT = bm.reshape(2, 48, 128).transpose(2, 0, 1)
    W["bmodT"] = f(np.repeat(bmT[:, :, :, None], 2, axis=3).reshape(128, 2, 96))
    g = np.stack([np.stack([inp["l0_g_pre_mix"], inp["l0_g_post_mix"], inp["l0_g_pre_ffn"], inp["l0_g_post_ffn"]]),
                  np.stack([inp["l1_g_pre_mix"], inp["l1_g_post_mix"], inp["l1_g_pre_ffn"], inp["l1_g_post_ffn"]])])
    gT = g.reshape(2, 4, 8, 128).transpose(3, 0, 1, 2)
    W["gvecT"] = f(np.repeat(gT[..., None], 2, axis=4).reshape(128, 2, 4, 16))
    w = np.asarray(inp["l0_w_in"], np.float32)
    qa, ka, va, ga, lra = w[:, 0:256], w[:, 256:512], w[:, 512:1024], w[:, 1024:1536], w[:, 1536:1568]
    qb, kb, vb, ob, gts = w[:, 1568:1824], w[:, 1824:2080], w[:, 2080:2592], w[:, 2592:3104], w[:, 3104:3120]
    w0 = np.zeros((D, 4096), np.float32)
    for i in range(4):
        s = i * 512
        w0[:, s:s + 64] = qa[:, i * 64:(i + 1) * 64]
        w0[:, s + 64:s + 128] = ka[:, i * 64:(i + 1) * 64]
        w0[:, s + 128:s + 192] = ka[:, i * 64:(i + 1) * 64]
        w0[:, s + 192:s + 320] = va[:, i * 128:(i + 1) * 128]
        w0[:, s + 320:s + 448] = ga[:, i * 128:(i + 1) * 128]
        s = (4 + i) * 512
        w0[:, s:s + 64] = qb[:, i * 64:(i + 1) * 64]
        w0[:, s + 64:s + 128] = kb[:, i * 64:(i + 1) * 64]
        w0[:, s + 128:s + 192] = kb[:, i * 64:(i + 1) * 64]
        w0[:, s + 192:s + 320] = vb[:, i * 128:(i + 1) * 128]
        w0[:, s + 320:s + 448] = ob[:, i * 128:(i + 1) * 128]
    w0[:, 448:480] = lra
    w0[:, 4 * 512 + 448:4 * 512 + 464] = gts
    W["w0cat"] = f(w0)
    wg = np.zeros((33, 2, 512), np.float32)
    for d, (wn, bn, r0) in enumerate((("l0_gla_w_gate_f", "l0_gla_b_gate_f", 0), ("l0_gla_w_gate_b", "l0_gla_b_gate_b", 16))):
        wgd, bgd = np.asarray(inp[wn], np.float32), np.asarray(inp[bn], np.float32)
        for h in range(4):
            for rep in range(2):
                wg[r0:r0 + 16, d, h * 128 + rep * 64:h * 128 + rep * 64 + 64] = wgd[:, h * 64:(h + 1) * 64]
                wg[32, d, h * 128 + rep * 64:h * 128 + rep * 64 + 64] = bgd[h * 64:(h + 1) * 64]
    W["wg"] = f(wg)
    W["gn"] = f(np.broadcast_to(np.stack([inp["l0_gla_g_norm"], inp["l0_mlstm_g_norm"]])[None], (128, 2, 128)))
    W["bgates"] = f(np.broadcast_to(np.asarray(inp["l0_mlstm_b_gates"])[None], (128, 16)))
    W["wout"] = f(np.stack([inp["l0_w_out"], inp["l1_w_out"]]))
    wups, cvs = [], []
    ffn_in = ((inp["l0_ffn_w_up"], inp["l0_ffn_conv_w"], inp["l0_ffn_conv_b"]),
              (inp["l1_ffn_w_up"], inp["l1_ffn_conv_w"], inp["l1_ffn_conv_b"]))
    for l in range(2):
        wu = np.asarray(ffn_in[l][0], np.float32)
        cols = []
        for s in range(11):
            cols.append(wu[:, s * 256:(s + 1) * 256])
            cols.append(wu[:, 2816 + s * 256:2816 + (s + 1) * 256])
        wups.append(np.concatenate(cols, axis=1))
        cw = np.asarray(ffn_in[l][1], np.float32)
        cb = np.asarray(ffn_in[l][2], np.float32)
        cc = np.concatenate([cw, cb[None]], axis=0)
        cvs.append(cc.reshape(4, 44, 128).transpose(2, 1, 0))
    W["wup"] = f(np.stack(wups))
    W["convT"] = f(np.stack(cvs, axis=1))
    W["wdown"] = f(np.stack([inp["l0_ffn_w_down"], inp["l1_ffn_w_down"]]))
    w1 = np.asarray(inp["l1_w_in"], np.float32)
    W["w1in"] = f(np.concatenate([w1[:, 0:704], w1[:, 640:704][:, _SW]], axis=1))
    wq = np.asarray(inp["l1_w_qb"], np.float32)
    qcols = []
    for h in range(8):
        qcols += [wq[:, h * 192:h * 192 + 128], wq[:, h * 192 + 128:h * 192 + 192], wq[:, h * 192 + 128:h * 192 + 192][:, _SW]]
    W["wqb"] = f(np.concatenate(qcols, axis=1))
    W["wkvb"] = f(inp["l1_w_kvb"])
    gq = np.asarray(inp["l1_g_q_norm"], np.float32).reshape(3, 128).T
    gkv = np.asarray(inp["l1_g_kv_norm"], np.float32).reshape(2, 128).T
    W["gqkv"] = f(np.concatenate([gq, gkv], axis=1))
    W["cst"] = make_consts()
    return W


def prep_core(inp, c):
    f = lambda a: np.ascontiguousarray(np.asarray(a, dtype=np.float32))
    mode, grp, sa = _core_units(c)
    xp, xs = np.asarray(inp["x_prompt"]), np.asarray(inp["x_sample"])
    m = {}
    if mode == "prompt":
        xg = np.concatenate([xp[s] for s in grp], axis=0)
        cond0 = np.asarray(inp["c_ctx"])
    else:
        xg = xs[grp]
        cond0 = np.asarray(inp["c"])[grp]
    m["xin"] = f(np.concatenate([xg, xp[sa]], axis=0))
    cond2 = np.stack([cond0, np.asarray(inp["c_ctx"])])
    m["cond2T"] = f(cond2.reshape(2, 8, 128).transpose(2, 1, 0))
    chain = np.zeros((128, 2, NU), np.float32)
    ginit = np.zeros((2, NU, 4, 64, 128), np.float32)
    minit = np.zeros((2, NU, 4, 64, 129), np.float32)
    mminit = np.zeros((128, 2, NU, 4), np.float32)
    ktab = np.zeros((6, NKEY), np.float32)
    qtab = np.zeros((5, T), np.float32)
    ckvc = np.zeros((512, 256), np.float32)
    krc = np.zeros((512, 64), np.float32)
    ktab[0, :] = 1.0
    for j in range(4):
        ktab[1 + j, 512 + 256 * j:512 + 256 * (j + 1)] = 1.0
    ktab[5, 0:512] = 1.0
    if mode == "sample":
        b = grp
        chain[:, 0, 1:4] = 1.0
        chain[:, 1, 0:3] = 1.0
        ginit[0, 0] = inp["state_l0_gla_fwd"][b]
        ginit[1, 3] = inp["state_l0_gla_bwd"][b]
        minit[0, 0, :, :, 0:128] = inp["state_l0_mlstm_c_fwd"][b]
        minit[0, 0, :, :, 128] = inp["state_l0_mlstm_n_fwd"][b]
        minit[1, 3, :, :, 0:128] = inp["state_l0_mlstm_c_bwd"][b]
        minit[1, 3, :, :, 128] = inp["state_l0_mlstm_n_bwd"][b]
        mminit[:, 0, 0, :] = np.asarray(inp["state_l0_mlstm_m_fwd"])[b][None, :]
        mminit[:, 1, 3, :] = np.asarray(inp["state_l0_mlstm_m_bwd"])[b][None, :]
        ckvc = inp["cache_l1_ckv"][b]
        krc = inp["cache_l1_krope"][b]
    else:
        for u in range(4):
            for j in range(4):
                if j != u:
                    qtab[j, 256 * u:256 * (u + 1)] = NEG
        qtab[4, 0:1024] = NEG
    m["chain"], m["ginit"], m["minit"], m["mminit"] = chain, ginit, minit, mminit
    m["ktab"], m["qtab"], m["ckvc"], m["krc"] = ktab, qtab, f(ckvc), f(krc)
    m["ropeT"] = _rope_table(mode)
    return m


def assemble(results):
    yp = np.zeros((32, 256, D), np.float32)
    ysm = np.zeros((2, 1024, D), np.float32)
    gla = np.zeros((2, 32, 4, 64, 128), np.float32)
    mC = np.zeros((2, 32, 4, 64, 128), np.float32)
    mn = np.zeros((2, 32, 4, 64), np.float32)
    mm = np.zeros((2, 32, 4), np.float32)
    ckv = np.zeros((32, 256, 256), np.float32)
    kr = np.zeros((32, 256, 64), np.float32)
    for c, r in enumerate(results):
        mode, grp, sa = _core_units(c)
        units = [(4, sa)]
        if mode == "prompt":
            units += [(j, grp[j]) for j in range(4)]
        else:
            ysm[grp] = r["y"][0:1024]
        mmo = r["mm_o"].reshape(2, NU, 4)
        for u, s in units:
            yp[s] = r["y"][u * 256:(u + 1) * 256]
            ckv[s] = r["ckv_o"][u * 256:(u + 1) * 256]
            kr[s] = r["kr_o"][u * 256:(u + 1) * 256]
            for d in range(2):
                gla[d, s] = r["gla_o"][d, u]
                mC[d, s] = r["mC_o"][d, u, :, :, 0:128]
                mn[d, s] = r["mC_o"][d, u, :, :, 128]
                mm[d, s] = mmo[d, u]
    return (yp, ysm, gla[0], gla[1], mC[0], mn[0], mm[0], mC[1], mn[1], mm[1], ckv, kr)


_PROG = {}


def kernel(**inputs):
    if "nc" not in _PROG:
        _PROG["nc"] = build_program()[0]
    nc = _PROG["nc"]
    W = prep_weights(inputs)
    in_maps = []
    for c in range(8):
        m = dict(W)
        m.update(prep_core(inputs, c))
        in_maps.append(m)
    res = run_bass_kernel_spmd(nc, in_maps, core_ids=list(range(8)))
    return assemble(res.results)
```
